# Optimizing a Trainium2 kernel written in Bass

```python
import math
import jax, jax.numpy as jnp
from jax import lax
import numpy as np

D_MODEL = 1024
BATCH = 4
SEQ = 4096
DEPTH = 2

GRID_W = 64
CTX_LEN = 256
N_MIXERS = 2
N_ATTN_LAYERS = (DEPTH + 1) // 2
N_SSM_LAYERS = DEPTH // 2
HEAD_DIM = 64
N_HEADS = D_MODEL // HEAD_DIM
N_KV_HEADS = 4
GQA_GROUP = N_HEADS // N_KV_HEADS
ATTN_WIDTH = N_HEADS * HEAD_DIM
KV_WIDTH = N_KV_HEADS * HEAD_DIM
ATTN_IN = 2 * ATTN_WIDTH + 2 * KV_WIDTH
WINDOW = 128
BLOCK = 128
ROPE_BASE = 10000.0
ROPE_FREQS = HEAD_DIM // 4
SSM_WIDTH = D_MODEL
SSM_GROUP = 16
SSM_GROUPS = SSM_WIDTH // SSM_GROUP
SSM_STATE = 64
DT_MIN = 0.001
DT_MAX = 0.1
NORM_EPS = 1e-6
NEG_INF = -1e30

kernel_name = "hybrid_swa_sink_s5_ctx_prefix"


def rmsnorm(x, w):
    xf = x.astype(jnp.float32)
    y = xf * lax.rsqrt(jnp.mean(xf * xf, axis=-1, keepdims=True) + NORM_EPS)
    return (y * w.astype(jnp.float32)).astype(x.dtype)


def rope_angles(rows):
    inv = ROPE_BASE ** (-jnp.arange(ROPE_FREQS, dtype=jnp.float32) / ROPE_FREQS)
    row = jnp.repeat(jnp.arange(rows, dtype=jnp.float32), GRID_W)
    col = jnp.tile(jnp.arange(GRID_W, dtype=jnp.float32), rows)
    return jnp.stack([row[:, None] * inv, col[:, None] * inv], axis=1)


def rope_2d(t, ang):
    ts = t.reshape(t.shape[:-1] + (2, 2, ROPE_FREQS))
    t1, t2 = ts[..., 0, :], ts[..., 1, :]
    cos = jnp.cos(ang)[None, :, None].astype(t.dtype)
    sin = jnp.sin(ang)[None, :, None].astype(t.dtype)
    out = jnp.stack([t1 * cos - t2 * sin, t2 * cos + t1 * sin], axis=-2)
    return out.reshape(t.shape)


def band_mask(n_blocks):
    qi = jnp.arange(BLOCK)[:, None]
    kj = jnp.arange(3 * BLOCK)[None, :]
    in_win = jnp.abs(kj - BLOCK - qi) <= WINDOW
    key_pos = jnp.arange(n_blocks)[:, None] * BLOCK + jnp.arange(3 * BLOCK)[None, :] - BLOCK
    valid = (key_pos >= 0) & (key_pos < n_blocks * BLOCK)
    return in_win[None] & valid[:, None, :]


def band(t):
    b, l = t.shape[0], t.shape[1]
    tp = jnp.pad(t, ((0, 0), (BLOCK, BLOCK), (0, 0), (0, 0)))
    tb = tp.reshape(b, l // BLOCK + 2, BLOCK, t.shape[2], t.shape[3])
    return jnp.concatenate([tb[:, :-2], tb[:, 1:-1], tb[:, 2:]], axis=2)


def attn_mixer(h, hc, w_in, sink, w_out, ang, mask, need_ctx_out):
    b, l, _ = h.shape
    nb = l // BLOCK
    f32 = jnp.float32
    scale = HEAD_DIM ** -0.5
    q, k, v, z = jnp.split(h @ w_in, [ATTN_WIDTH, ATTN_WIDTH + KV_WIDTH, ATTN_WIDTH + 2 * KV_WIDTH], axis=-1)
    q = rope_2d(q.reshape(b, l, N_HEADS, HEAD_DIM), ang)
    k = rope_2d(k.reshape(b, l, N_KV_HEADS, HEAD_DIM), ang)
    v = v.reshape(b, l, N_KV_HEADS, HEAD_DIM)
    kvc = hc @ w_in[:, ATTN_WIDTH:ATTN_WIDTH + 2 * KV_WIDTH]
    kc, vc = jnp.split(kvc, 2, axis=-1)
    n_ctx = hc.shape[1]
    kc = kc.reshape(b, n_ctx, N_KV_HEADS, HEAD_DIM).astype(f32)
    vc = vc.reshape(b, n_ctx, N_KV_HEADS, HEAD_DIM).astype(f32)
    sink_kg = sink.astype(f32).reshape(N_KV_HEADS, GQA_GROUP)

    qb = q.reshape(b, nb, BLOCK, N_KV_HEADS, GQA_GROUP, HEAD_DIM).astype(f32) * scale
    kw = band(k).astype(f32)
    vw = band(v).astype(f32)
    s_win = jnp.einsum('bnqkgd,bnjkd->bnkgqj', qb, kw)
    s_win = jnp.where(mask[None, :, None, None], s_win, NEG_INF)
    s_ctx = jnp.einsum('bnqkgd,bckd->bnkgqc', qb, kc)
    s_sink = jnp.broadcast_to(sink_kg[None, None, :, :, None, None], s_win.shape[:-1] + (1,))
    p = jax.nn.softmax(jnp.concatenate([s_win, s_ctx, s_sink], axis=-1), axis=-1)
    nw = 3 * BLOCK
    o = (jnp.einsum('bnkgqj,bnjkd->bnqkgd', p[..., :nw], vw)
         + jnp.einsum('bnkgqc,bckd->bnqkgd', p[..., nw:nw + n_ctx], vc))
    o = o.reshape(b, l, ATTN_WIDTH).astype(h.dtype)
    y = (o * jax.nn.silu(z)) @ w_out

    yc = None
    if need_ctx_out:
        qc = (hc @ w_in[:, :ATTN_WIDTH]).reshape(b, n_ctx, N_KV_HEADS, GQA_GROUP, HEAD_DIM).astype(f32) * scale
        zc = hc @ w_in[:, ATTN_WIDTH + 2 * KV_WIDTH:]
        sc = jnp.einsum('bqkgd,bckd->bkgqc', qc, kc)
        sc_sink = jnp.broadcast_to(sink_kg[None, :, :, None, None], sc.shape[:-1] + (1,))
        pc = jax.nn.softmax(jnp.concatenate([sc, sc_sink], axis=-1), axis=-1)
        oc = jnp.einsum('bkgqc,bckd->bqkgd', pc[..., :n_ctx], vc)
        oc = oc.reshape(b, n_ctx, ATTN_WIDTH).astype(hc.dtype)
        yc = (oc * jax.nn.silu(zc)) @ w_out
    return y, yc


def _scan_op(e1, e2):
    a1, b1 = e1
    a2, b2 = e2
    return a1 * a2, a2 * b1 + b2


def diag_scan(a_bar, bu):
    a = jnp.broadcast_to(a_bar, bu.shape)
    _, xs = lax.associative_scan(_scan_op, (a, bu), axis=1)
    return xs


def s5_output(y, z, w_glu, w_out, dtype):
    b, l = y.shape[0], y.shape[1]
    y = jax.nn.gelu(y.reshape(b, l, SSM_WIDTH), approximate=False).astype(dtype)
    ya, yg = jnp.split(y @ w_glu, 2, axis=-1)
    y = ya * jax.nn.sigmoid(yg)
    return (y * jax.nn.silu(z)) @ w_out


def s5_mixer(h, hc, w_in, lam_re, lam_im, log_dt, b_re, b_im, c_re, c_im, d_skip, w_glu, w_out, need_ctx_out):
    b, l, _ = h.shape
    n_ctx = hc.shape[1]
    f32 = jnp.float32
    u, z = jnp.split(h @ w_in, 2, axis=-1)
    uc = hc @ w_in[:, :SSM_WIDTH]
    ug = u.reshape(b, l, SSM_GROUPS, SSM_GROUP).astype(f32)
    ucg = uc.reshape(b, n_ctx, SSM_GROUPS, SSM_GROUP).astype(f32)
    d_g = d_skip.astype(f32).reshape(SSM_GROUPS, SSM_GROUP)
    y = d_g * ug
    yc = d_g * ucg if need_ctx_out else None
    for dirn in range(2):
        lam = lax.complex(lam_re[dirn].astype(f32), lam_im[dirn].astype(f32))
        dt = jnp.exp(log_dt[dirn].astype(f32))[:, None]
        a_bar = jnp.exp(lam * dt)
        b_mat = lax.complex(b_re[dirn].astype(f32), b_im[dirn].astype(f32))
        b_bar = ((a_bar - 1.0) / lam)[..., None] * b_mat
        c_mat = lax.complex(c_re[dirn].astype(f32), c_im[dirn].astype(f32))
        bu_c = jnp.einsum('gph,blgh->blgp', b_bar, ucg.astype(jnp.complex64))
        bu = jnp.einsum('gph,blgh->blgp', b_bar, ug.astype(jnp.complex64))
        if dirn == 1:
            bu_c = jnp.flip(bu_c, axis=1)
            bu = jnp.flip(bu, axis=1)
        xs_c = diag_scan(a_bar, bu_c)
        s0 = xs_c[:, -1]
        xs = diag_scan(a_bar, bu.at[:, 0].add(a_bar * s0))
        if dirn == 1:
            xs = jnp.flip(xs, axis=1)
            xs_c = jnp.flip(xs_c, axis=1)
        y = y + jnp.einsum('ghp,blgp->blgh', c_mat, xs).real
        if need_ctx_out:
            yc = yc + jnp.einsum('ghp,blgp->blgh', c_mat, xs_c).real
    out = s5_output(y, z, w_glu, w_out, h.dtype)
    out_c = None
    if need_ctx_out:
        zc = hc @ w_in[:, SSM_WIDTH:]
        out_c = s5_output(yc, zc, w_glu, w_out, hc.dtype)
    return out, out_c


def setup_inputs(seed: int = 0) -> dict:
    key = jax.random.key(seed)
    ks = jax.random.split(key, 24)
    f32 = jnp.float32

    def nrm(k, shape, scale):
        return jax.random.normal(k, shape, f32) * scale

    lam_shape = (N_SSM_LAYERS, 2, SSM_GROUPS, SSM_STATE)
    n_idx = jnp.arange(SSM_STATE, dtype=f32)
    return {
        "x": nrm(ks[0], (BATCH, SEQ, D_MODEL), 1.0),
        "c": nrm(ks[1], (BATCH, D_MODEL), 1.0),
        "ctx": nrm(ks[2], (BATCH, CTX_LEN, D_MODEL), 1.0),
        "c_ctx": nrm(ks[3], (D_MODEL,), 1.0),
        "norm_w": 1.0 + nrm(ks[4], (DEPTH, D_MODEL), 0.02),
        "w_ada": nrm(ks[5], (DEPTH, D_MODEL, 3 * D_MODEL), D_MODEL ** -0.5),
        "b_ada": nrm(ks[6], (DEPTH, 3 * D_MODEL), 0.02),
        "attn_w_in": nrm(ks[7], (N_ATTN_LAYERS, D_MODEL, ATTN_IN), D_MODEL ** -0.5),
        "attn_sink": nrm(ks[8], (N_ATTN_LAYERS, N_HEADS), 0.5),
        "attn_w_out": nrm(ks[9], (N_ATTN_LAYERS, ATTN_WIDTH, D_MODEL), ATTN_WIDTH ** -0.5),
        "ssm_w_in": nrm(ks[10], (N_SSM_LAYERS, D_MODEL, 2 * SSM_WIDTH), D_MODEL ** -0.5),
        "ssm_lam_re": -0.5 + nrm(ks[11], lam_shape, 0.01),
        "ssm_lam_im": math.pi * n_idx + nrm(ks[12], lam_shape, 0.01),
        "ssm_log_dt": jax.random.uniform(ks[13], (N_SSM_LAYERS, 2, SSM_GROUPS), f32,
                                         math.log(DT_MIN), math.log(DT_MAX)),
        "ssm_b_re": nrm(ks[14], (N_SSM_LAYERS, 2, SSM_GROUPS, SSM_STATE, SSM_GROUP), (2 * SSM_GROUP) ** -0.5),
        "ssm_b_im": nrm(ks[15], (N_SSM_LAYERS, 2, SSM_GROUPS, SSM_STATE, SSM_GROUP), (2 * SSM_GROUP) ** -0.5),
        "ssm_c_re": nrm(ks[16], (N_SSM_LAYERS, 2, SSM_GROUPS, SSM_GROUP, SSM_STATE), SSM_STATE ** -0.5),
        "ssm_c_im": nrm(ks[17], (N_SSM_LAYERS, 2, SSM_GROUPS, SSM_GROUP, SSM_STATE), SSM_STATE ** -0.5),
        "ssm_d": nrm(ks[18], (N_SSM_LAYERS, SSM_WIDTH), 1.0),
        "ssm_w_glu": nrm(ks[19], (N_SSM_LAYERS, SSM_WIDTH, 2 * SSM_WIDTH), SSM_WIDTH ** -0.5),
        "ssm_w_out": nrm(ks[20], (N_SSM_LAYERS, SSM_WIDTH, D_MODEL), SSM_WIDTH ** -0.5),
        "final_norm_w": 1.0 + nrm(ks[21], (D_MODEL,), 0.02),
    }


def reference(x, c, ctx, c_ctx, norm_w, w_ada, b_ada, attn_w_in, attn_sink, attn_w_out,
              ssm_w_in, ssm_lam_re, ssm_lam_im, ssm_log_dt, ssm_b_re, ssm_b_im, ssm_c_re, ssm_c_im,
              ssm_d, ssm_w_glu, ssm_w_out, final_norm_w):
    n_lat = x.shape[1]
    rows = n_lat // GRID_W
    ang = rope_angles(rows)
    mask = band_mask(n_lat // BLOCK)
    for i in range(DEPTH):
        need_ctx_out = i < DEPTH - 1
        mod = jax.nn.silu(c) @ w_ada[i] + b_ada[i]
        mod_c = jax.nn.silu(c_ctx) @ w_ada[i] + b_ada[i]
        shift, scale, gate = jnp.split(mod, 3, axis=-1)
        shift_c, scale_c, gate_c = jnp.split(mod_c, 3, axis=-1)
        h = rmsnorm(x, norm_w[i]) * (1.0 + scale[:, None]) + shift[:, None]
        hc = rmsnorm(ctx, norm_w[i]) * (1.0 + scale_c) + shift_c
        j = i // N_MIXERS
        if i % N_MIXERS == 0:
            y, yc = attn_mixer(h, hc, attn_w_in[j], attn_sink[j], attn_w_out[j], ang, mask, need_ctx_out)
        else:
            y, yc = s5_mixer(h, hc, ssm_w_in[j], ssm_lam_re[j], ssm_lam_im[j], ssm_log_dt[j],
                             ssm_b_re[j], ssm_b_im[j], ssm_c_re[j], ssm_c_im[j], ssm_d[j],
                             ssm_w_glu[j], ssm_w_out[j], need_ctx_out)
        x = x + gate[:, None] * y
        if need_ctx_out:
            ctx = ctx + gate_c * yc
    return rmsnorm(x, final_norm_w)
```

```python
import os
import numpy as np
import concourse.bass as bass
import concourse.mybir as mybir
from concourse.bass_utils import run_bass_kernel_spmd

F32 = mybir.dt.float32
BF16 = mybir.dt.bfloat16
AF = mybir.ActivationFunctionType
ALU = mybir.AluOpType
AX = mybir.AxisListType

EPOCH = 3000
NCORES = 8
D = 1024
TOK = 2048
NT = 16
EPS = 1e-6


class Tile:
    def __init__(self, name, t):
        self.name = name
        self.t = t
        self.wev = None
        self.revs = {}
        self.dsem = None
        self.dcnt = 0
        self.aliases = []

    def __getitem__(self, k):
        return self.t[k]


class Prog:
    ENGS = ("pe", "act", "dve", "pool", "sp")

    def __init__(self, nc, same_engine_sync=True):
        self.nc = nc
        self.same_sync = same_engine_sync
        self.stream = {e: [] for e in self.ENGS}
        self.sems = {e: [] for e in self.ENGS}
        self.cnt = {e: 0 for e in self.ENGS}
        self.waited = {e: {} for e in self.ENGS}
        self.own = {e: set() for e in self.ENGS}
        self.nsem = 0
        for e in self.ENGS:
            self._new_epoch(e)
        self.tiles = []
        self.out_evs = []

    def _sem(self, name):
        self.nsem += 1
        return self.nc.alloc_semaphore(name=f"{name}_{self.nsem}")

    def _new_epoch(self, e):
        s = self._sem(f"e_{e}")
        self.sems[e].append(s)
        self.own[e].add(id(s))
        self.cnt[e] = 0

    def sbuf(self, name, shape, dtype):
        t = self.nc.alloc_sbuf_tensor("s_" + name, list(shape), dtype)
        tl = Tile(name, t)
        self.tiles.append(tl)
        return tl

    def psum(self, name, shape, dtype=F32):
        t = self.nc.alloc_psum_tensor("p_" + name, list(shape), dtype)
        tl = Tile(name, t)
        self.tiles.append(tl)
        return tl

    def dram(self, name, shape, dtype, **kw):
        t = self.nc.dram_tensor(name, list(shape), dtype, **kw)
        tl = Tile(name, t)
        self.tiles.append(tl)
        return tl

    def _need(self, eng, waits, ev):
        if ev is None:
            return
        sem, val = ev
        k = id(sem)
        if k in self.own[eng] and (eng == "pe" or not self.same_sync):
            return
        if self.waited[eng].get(k, 0) >= val:
            return
        if waits.get(k, (None, 0))[1] < val:
            waits[k] = (sem, val)

    def _deps(self, eng, reads, writes):
        waits = {}
        for t in reads:
            self._need(eng, waits, t.wev)
        for t0 in writes:
            for t in [t0] + t0.aliases:
                self._need(eng, waits, t.wev)
                for ev in t.revs.values():
                    self._need(eng, waits, ev)
        for t0 in reads:
            for t in t0.aliases:
                pass
        wl = list(waits.values())
        for sem, val in wl:
            self.waited[eng][id(sem)] = val
        return wl

    def _mark(self, ev, reads, writes):
        sem, val = ev
        for t in reads:
            old = t.revs.get(id(sem))
            if old is None or old[1] < val:
                t.revs[id(sem)] = ev
        for t in writes:
            t.wev = ev
            t.revs = {}

    def op(self, eng, fn, reads=(), writes=(), flag=True):
        wl = self._deps(eng, reads, writes)
        sem = self.sems[eng][-1]
        ev = (sem, self.cnt[eng] + 1)
        if flag:
            self.cnt[eng] += 1
        self.stream[eng].append((wl, fn, ev if flag else None, 1))
        self._mark(ev, reads, writes)
        if flag and self.cnt[eng] >= EPOCH:
            self._new_epoch(eng)
        return ev

    def dma(self, q, fn, reads=(), writes=(), semtile=None, inc=16):
        wl = self._deps(q, reads, writes)
        st = semtile or (writes[0] if writes else reads[0])
        if st.dsem is None:
            st.dsem = self._sem("d_" + st.name)
        st.dcnt += inc
        ev = (st.dsem, st.dcnt)
        self.stream[q].append((wl, fn, ev, inc))
        self._mark(ev, reads, writes)
        return ev

    def wait_all(self, eng, evs):
        waits = {}
        for ev in evs:
            self._need(eng, waits, ev)
        wl = list(waits.values())
        for sem, val in wl:
            self.waited[eng][id(sem)] = val
        self.stream[eng].append((wl, None, None, 0))

    def emit(self):
        nc = self.nc
        with nc.Block() as block:
            def replay(name):
                def f(e):
                    for wl, fn, ev, inc in self.stream[name]:
                        for sem, val in wl:
                            e.wait_ge(sem, val)
                        if fn is None:
                            continue
                        ins = fn(e)
                        if ev is not None:
                            ins.then_inc(ev[0], inc)
                return f
            block.tensor(replay("pe"))
            block.scalar(replay("act"))
            block.vector(replay("dve"))
            block.gpsimd(replay("pool"))
            block.sync(replay("sp"))


class Arena:
    def __init__(self, P, name, words):
        self.P = P
        self.name = name
        self.words = words
        self.base = P.nc.alloc_sbuf_tensor("s_" + name, [128, words], F32)
        self.carves = []

    def carve(self, name, off, shape, dtype):
        n = int(np.prod(shape[1:]))
        w = n if dtype == F32 else (n + 1) // 2
        assert off + w <= self.words, (name, off, w, self.words)
        ap = self.base[:, off:off + w]
        if dtype != F32:
            ap = ap.bitcast(dtype)
        if len(shape) == 3:
            ap = ap.rearrange("p (a b) -> p a b", a=shape[1])
        elif len(shape) == 4:
            ap = ap.rearrange("p (a b c) -> p a b c", a=shape[1], b=shape[2])
        tl = Tile(name, ap)
        for lo, hi, old in self.carves:
            if lo < off + w and off < hi:
                tl.aliases.append(old)
                old.aliases.append(tl)
        self.carves.append((off, off + w, tl))
        self.P.tiles.append(tl)
        return tl


def _rope_tables(hf):
    inv = (10000.0 ** (-np.arange(16, dtype=np.float32) / 16)).astype(np.float32)
    j = np.arange(2304)
    g = (j - 128) if hf == 0 else (4095 - (j - 128))
    g = np.clip(g, 0, 4095)
    row = (g // 64).astype(np.float32)
    col = (g % 64).astype(np.float32)
    cosT = np.zeros((2304, 64), np.float32)
    sinT = np.zeros((2304, 64), np.float32)
    for d in range(64):
        axis, half, f = d // 32, (d % 32) // 16, d % 16
        ang = ((row if axis == 0 else col) * inv[f]).astype(np.float32)
        cosT[:, d] = np.cos(ang)
        sinT[:, d] = np.sin(ang) * (-1.0 if half == 0 else 1.0)
    cosT = np.ascontiguousarray(cosT.reshape(18, 128, 64).transpose(1, 0, 2).reshape(128, 18 * 64))
    sinT = np.ascontiguousarray(sinT.reshape(18, 128, 64).transpose(1, 0, 2).reshape(128, 18 * 64))
    return cosT, sinT


def _consts(hf):
    cosT, sinT = _rope_tables(hf)
    jj = np.arange(128)[:, None]
    ii = np.arange(128)[None, :]
    mP = (jj >= ii).astype(np.float32)
    mN = (jj <= ii).astype(np.float32)
    masks = np.stack([mP, mN, mP * 0.0, mN], 1)
    ident = np.eye(128, dtype=np.float32)
    psw = np.zeros((128, 128), np.float32)
    for m in range(128):
        d = m % 64
        sw = d + 16 if (d % 32) < 16 else d - 16
        psw[(m // 64) * 64 + sw, m] = 1.0
    return dict(cosT=cosT, sinT=sinT, masks=np.ascontiguousarray(masks.reshape(128, 4 * 128)),
                ident=ident, psw=psw)


def _colT(v):
    return np.ascontiguousarray(np.asarray(v, np.float32).reshape(-1, 128).T)


def build(mode="full"):
    do0 = mode in ("full", "l0")
    do1 = mode in ("full", "l1")
    nc = bass.Bass("TRN2", target_bir_lowering=False, dynamic_dma_scratch_size=4096)
    P = Prog(nc)

    def din(name, shape):
        return nc.dram_tensor(name, list(shape), F32, kind="ExternalInput").ap()

    def dout(name, shape):
        return nc.dram_tensor(name, list(shape), F32, kind="ExternalOutput").ap()

    x_d = din("x", [TOK, D])
    ctx_d = din("ctx", [256, D])
    cT_d = din("cT", [128, 16])
    nwT_d = din("nwT", [128, 16])
    badaT_d = din("badaT", [128, 48])
    wada_d = din("w_ada", [2, D, 3 * D])
    bada_d = din("b_ada", [2, 3 * D])
    ident_d = din("ident", [128, 128])
    if do0:
        xh_d = din("xh", [256, D])
        win0_d = din("attn_w_in", [D, 2560])
        wout0_d = din("attn_w_out", [D, D])
        sink_d = din("attn_sink", [1, 16])
        cos_d = din("cosT", [128, 1152])
        sin_d = din("sinT", [128, 1152])
        masks_d = din("masks", [128, 512])
        psw_d = din("psw", [128, 128])
    if mode == "l0":
        x1_d = dout("x1", [TOK, D])
        ctx1_d = dout("ctx1", [256, D])

    xs = [P.sbuf(f"xs{n}", [128, D], F32) for n in range(NT)]
    cx = [P.sbuf(f"cx{n}", [128, D], F32) for n in range(2)]
    ident = P.sbuf("ident", [128, 128], BF16)
    small = P.sbuf("small", [128, 256], F32)
    cTs = P.sbuf("cTs", [128, 16], F32)
    scbf = P.sbuf("scbf", [128, 8, 2], BF16)
    nwT = P.sbuf("nwT", [128, 16], F32)
    badaT = P.sbuf("badaT", [128, 48], F32)
    modT = P.sbuf("modT", [128, 16, 2], F32)
    wmT = P.sbuf("wmT", [128, 8, 2], F32)
    gate_lat = P.sbuf("gate_lat", [128, D], F32)
    banks = [P.psum(f"B{i}", [128, 512], F32) for i in range(8)]
    ss = [P.sbuf(f"ss{i}", [128, 4], F32) for i in range(2)]

    A = Arena(P, "arena", 28200 + 2 * 1280)
    stages = [A.carve(f"stg{i}", 28200 + i * 1280, [128, 1280], F32) for i in range(2)]
    cast_rr = {"n": 0}

    def load_cast(dst_tile, dst_ap, src_ap, ncols, engines=("act", "dve", "pool"), view=None):
        st = stages[cast_rr["n"] % 2]
        eng = engines[cast_rr["n"] % len(engines)]
        cast_rr["n"] += 1
        P.dma("sp", lambda e: e.dma_start(out=st[:, 0:ncols], in_=src_ap), writes=[st])
        src = st[:, 0:ncols] if view is None else view(st[:, 0:ncols])
        if eng == "act":
            P.op("act", lambda e: e.copy(out=dst_ap, in_=src), reads=[st], writes=[dst_tile])
        else:
            P.op(eng, lambda e: e.tensor_copy(out=dst_ap, in_=src), reads=[st], writes=[dst_tile])

    P.dma("pool", lambda e: e.dma_start(out=ident[:], in_=ident_d), writes=[ident])
    P.dma("sp", lambda e: e.dma_start(out=cTs[:], in_=cT_d), writes=[cTs])
    P.dma("sp", lambda e: e.dma_start(out=nwT[:], in_=nwT_d), writes=[nwT])
    P.dma("sp", lambda e: e.dma_start(out=badaT[:], in_=badaT_d), writes=[badaT])
    for n in range(NT):
        P.dma("sp", lambda e, n=n: e.dma_start(out=xs[n][:], in_=x_d[n * 128:(n + 1) * 128, :]),
              writes=[xs[n]])
    for n in range(2):
        P.dma("sp", lambda e, n=n: e.dma_start(out=cx[n][:], in_=ctx_d[n * 128:(n + 1) * 128, :]),
              writes=[cx[n]])
    P.op("act", lambda e: e.activation(out=scbf[:].rearrange("p t s -> p (t s)"), in_=cTs[:], func=AF.Silu),
         reads=[cTs], writes=[scbf])

    def make_screp(screp):
        for s in range(2):
            P.op("dve", lambda e, s=s: e.tensor_copy(
                out=screp[:, s, :, :], in_=scbf[:, :, s:s + 1].broadcast_to([128, 8, 128])),
                reads=[scbf], writes=[screp])

    cnt = {"n": 0}

    def rot(lst):
        cnt["n"] += 1
        return lst[cnt["n"] % len(lst)]

    def adaln(layer, wchunk, gate_tiles, brow, screp):
        accb = banks[7]
        for j6 in range(6):
            for kt in range(0, 8, 2):
                load_cast(wchunk, wchunk[:, kt:kt + 2, :],
                          wada_d[layer, kt * 128:(kt + 2) * 128, j6 * 512:(j6 + 1) * 512].rearrange("(a p) c -> p a c", p=128),
                          1024, view=lambda v: v.rearrange("p (a c) -> p a c", a=2))
            if j6 < 4:
                for jj in range(4):
                    j = j6 * 4 + jj
                    for kt in range(8):
                        P.op("pe", lambda e, kt=kt, jj=jj, j=j: e.matmul(
                            accb[:, 2 * j:2 * j + 2], lhsT=wchunk[:, kt, jj * 128:(jj + 1) * 128],
                            rhs=scbf[:, kt, :], start=(kt == 0), stop=(kt == 7)),
                            reads=[wchunk, scbf], writes=[accb], flag=(kt == 7))
            else:
                for (s, gt) in gate_tiles:
                    gb = banks[5 + s]
                    for kt in range(8):
                        P.op("pe", lambda e, kt=kt, s=s, gb=gb: e.matmul(
                            gb[:, :], lhsT=screp[:, s, kt, :], rhs=wchunk[:, kt, :],
                            start=(kt == 0), stop=(kt == 7)),
                            reads=[wchunk, screp], writes=[gb], flag=(kt == 7))
                    c0 = (j6 - 4) * 512
                    if s == gate_tiles[0][0]:
                        P.dma("sp", lambda e, c0=c0: e.dma_start(
                            out=brow[:, c0:c0 + 512],
                            in_=bada_d[layer:layer + 1, 2048 + c0:2048 + c0 + 512].partition_broadcast(128)),
                            writes=[brow])
                    P.op("dve", lambda e, gb=gb, gt=gt, c0=c0: e.tensor_tensor(
                        out=gt[:, c0:c0 + 512], in0=gb[:, :], in1=brow[:, c0:c0 + 512], op=ALU.add),
                        reads=[gb, brow], writes=[gt])
        P.op("dve", lambda e: e.tensor_tensor(
            out=modT[:], in0=accb[:, 0:32].rearrange("p (j s) -> p j s", s=2),
            in1=badaT[:, layer * 24:layer * 24 + 16].unsqueeze(2).broadcast_to([128, 16, 2]), op=ALU.add),
            reads=[accb, badaT], writes=[modT])
        P.op("dve", lambda e: e.tensor_scalar(out=wmT[:], in0=modT[:, 8:16, :], scalar1=1.0, scalar2=None,
                                              op0=ALU.add), reads=[modT], writes=[wmT])
        P.op("dve", lambda e: e.tensor_tensor(
            out=wmT[:], in0=wmT[:], in1=nwT[:, layer * 8:layer * 8 + 8].unsqueeze(2).broadcast_to([128, 8, 2]),
            op=ALU.mult), reads=[wmT, nwT], writes=[wmT])

    def norm_hT(xt, s, hT, xn, c0=0):
        sst = rot(ss)
        tb = rot(banks[6:8])
        P.op("act", lambda e: e.activation(out=xn[:], in_=xt[:], func=AF.Square, accum_out=sst[:, 0:1]),
             reads=[xt], writes=[xn, sst])
        P.op("act", lambda e: e.activation(out=sst[:, 1:2], in_=sst[:, 0:1], func=AF.Sqrt, scale=1.0 / D,
                                           bias=small[:, 0:1]), reads=[sst, small], writes=[sst])
        P.op("dve", lambda e: e.reciprocal(out=sst[:, 2:3], in_=sst[:, 1:2]), reads=[sst], writes=[sst])
        P.op("act", lambda e: e.activation(out=xn[:], in_=xt[:], func=AF.Identity, scale=sst[:, 2:3]),
             reads=[xt, sst], writes=[xn])
        tbb = tb[:, :].bitcast(BF16)
        for kt in range(8):
            P.op("pe", lambda e, kt=kt: e.transpose(out=tbb[:, kt * 128:(kt + 1) * 128],
                                                    in_=xn[:, kt * 128:(kt + 1) * 128], identity=ident[:]),
                 reads=[xn, ident], writes=[tb], flag=(kt == 7))
        for kt in range(8):
            if kt % 2 == 0:
                P.op("dve", lambda e, kt=kt: e.tensor_scalar(
                    out=hT[:, kt, c0:c0 + 128], in0=tbb[:, kt * 128:(kt + 1) * 128], scalar1=wmT[:, kt, s:s + 1],
                    scalar2=modT[:, kt, s:s + 1], op0=ALU.mult, op1=ALU.add),
                    reads=[tb, wmT, modT], writes=[hT])
            else:
                P.op("act", lambda e, kt=kt: e.activation(
                    out=hT[:, kt, c0:c0 + 128], in_=tbb[:, kt * 128:(kt + 1) * 128], func=AF.Identity,
                    scale=wmT[:, kt, s:s + 1], bias=modT[:, kt, s:s + 1]),
                    reads=[tb, wmT, modT], writes=[hT])

    P.op("dve", lambda e: e.memset(small[:, 0:1], EPS), writes=[small])

    if do0:
        o = 0
        def carve(name, shape, dtype):
            nonlocal o
            n = int(np.prod(shape[1:]))
            w = n if dtype == F32 else (n + 1) // 2
            t = A.carve(name, o, shape, dtype)
            o += w
            return t
        win0 = carve("win0", [128, 8, 2560], BF16)
        wout0 = carve("wout0", [128, 8, D], BF16)
        cosT = carve("cosT", [128, 18, 64], BF16)
        sinT = carve("sinT", [128, 18, 64], BF16)
        masks = carve("masks", [128, 4, 128], BF16)
        psw = carve("psw", [128, 128], BF16)
        esink = carve("esink", [128, 16], F32)
        gate_ctx = carve("gate_ctx", [128, D], F32)
        KTs = [carve(f"KT{i}", [128, 2, 128], BF16) for i in range(4)]
        Vs = [carve(f"V{i}", [128, 4, 65], BF16) for i in range(4)]
        KTc = [carve(f"KTc{i}", [128, 2, 128], BF16) for i in range(2)]
        Vc = [carve(f"Vc{i}", [128, 4, 65], BF16) for i in range(2)]
        o_scr = o
        wchunk = carve("wchunk", [128, 8, 512], BF16)
        brow = carve("brow", [128, D], F32)
        xh = [carve(f"xh{i}", [128, D], F32) for i in range(2)]
        screp = carve("screp", [128, 2, 8, 128], BF16)
        o_early_end = o
        o = o_scr
        W = {}
        W["QT"] = [carve(f"QT{i}", [128, 2, 4, 128], BF16) for i in range(2)]
        W["SZ"] = [carve(f"SZ{i}", [128, D], BF16) for i in range(2)]
        W["PT"] = [carve(f"PT{i}", [128, 512], BF16) for i in range(10)]
        o_t1 = o
        W["t1"] = carve("t1", [128, D], BF16)
        W["g"] = carve("g", [128, D], BF16)
        o_w = o
        o = o_t1
        W["ytmp"] = carve("ytmp", [128, D], F32)
        o = max(o, o_w)
        W["gT"] = carve("gT", [128, 8, 128], BF16)
        o = max(o, o_early_end)
        hTs = [carve(f"hT{i}", [128, 8, 128], BF16) for i in range(2)]
        xn0 = carve("xn0", [128, D], BF16)
        rA = carve("rA", [128, 512], F32)
        o_rB = o
        rB = carve("rB", [128, 512], F32)
        o_after_rB = o
        o = o_rB
        OTs = [carve(f"OT{i}", [128, 512], BF16) for i in range(2)]
        o = o_after_rB
        qrot = carve("qrot", [128, D], BF16)
        krot = carve("krot", [128, 256], BF16)
        lsum = carve("lsum", [128, 16], F32)
        rl = carve("rl", [128, 16], F32)
        KTr = carve("KTr", [128, 2, 128], BF16)
        Vr = carve("Vr", [128, 4, 65], BF16)
        print("layer0 arena words used:", o)

        make_screp(screp)

        adaln(0, wchunk, [(0, gate_lat), (1, gate_ctx)], brow, screp)
        for kt in range(8):
            rows = slice(kt * 128, (kt + 1) * 128)
            for pr in range(2):
                cast_rr["n"] += 0
            st_q = None
            st = stages[cast_rr["n"] % 2]
            cast_rr["n"] += 1
            P.dma("sp", lambda e, st=st, rows=rows: e.dma_start(out=st[:, 0:1280], in_=win0_d[rows, 0:1280]), writes=[st])
            for pr in range(2):
                eng = ("act", "dve")[pr]
                src = st[:, pr * 512:(pr + 1) * 512].rearrange("p (s g d) -> p g s d", s=2, g=4)
                dst = win0[:, kt, pr * 512:(pr + 1) * 512].rearrange("p (g s d) -> p g s d", g=4, s=2)
                if eng == "act":
                    P.op("act", lambda e, src=src, dst=dst: e.copy(out=dst, in_=src), reads=[st], writes=[win0])
                else:
                    P.op("dve", lambda e, src=src, dst=dst: e.tensor_copy(out=dst, in_=src), reads=[st], writes=[win0])
            P.op("pool", lambda e, st=st, kt=kt: e.tensor_copy(out=win0[:, kt, 1024:1280], in_=st[:, 1024:1280]),
                 reads=[st], writes=[win0])
            load_cast(win0, win0[:, kt, 1280:2560], win0_d[rows, 1280:2560], 1280)
        for kt in range(8):
            load_cast(wout0, wout0[:, kt, :], wout0_d[kt * 128:(kt + 1) * 128, :], 1024)
        P.dma("pool", lambda e: e.dma_start(out=cosT[:].rearrange("p a b -> p (a b)"), in_=cos_d), writes=[cosT])
        P.dma("pool", lambda e: e.dma_start(out=sinT[:].rearrange("p a b -> p (a b)"), in_=sin_d), writes=[sinT])
        P.dma("pool", lambda e: e.dma_start(out=masks[:].rearrange("p a b -> p (a b)"), in_=masks_d),
              writes=[masks])
        P.dma("pool", lambda e: e.dma_start(out=psw[:], in_=psw_d), writes=[psw])
        P.dma("sp", lambda e: e.dma_start(out=esink[:], in_=sink_d.partition_broadcast(128)), writes=[esink])
        P.op("act", lambda e: e.activation(out=esink[:], in_=esink[:], func=AF.Exp), reads=[esink], writes=[esink])
        for i in range(2):
            P.dma("sp", lambda e, i=i: e.dma_start(out=xh[i][:], in_=xh_d[i * 128:(i + 1) * 128, :]), writes=[xh[i]])
        for i in range(4):
            P.op("pool", lambda e, i=i: e.memset(Vs[i][:, :, 64:65], 1.0), writes=[Vs[i]])
        for i in range(2):
            P.op("pool", lambda e, i=i: e.memset(Vc[i][:, :, 64:65], 1.0), writes=[Vc[i]])


        def rope_tok(src, ncols, ti, dst_tile, dst_ap, rope):
            nh = ncols // 64
            bank = src[0]
            sap = src[1]
            if not rope:
                P.op("act", lambda e: e.copy(out=dst_ap, in_=sap), reads=[bank], writes=[dst_tile])
                return
            P.op("act", lambda e: e.copy(out=rA[:, 0:ncols], in_=sap), reads=[bank], writes=[rA])
            rA3 = rA[:, 0:ncols].rearrange("p (h d) -> p h d", d=64)
            rB3 = rB[:, 0:ncols].rearrange("p (h d) -> p h d", d=64)
            for a_ in range(2):
                for half in range(2):
                    oo = a_ * 32 + half * 16
                    io = a_ * 32 + (1 - half) * 16
                    P.op("dve", lambda e, oo=oo, io=io: e.tensor_tensor(
                        out=rB3[:, :, oo:oo + 16], in0=rA3[:, :, io:io + 16],
                        in1=sinT[:, ti:ti + 1, oo:oo + 16].broadcast_to([128, nh, 16]),
                        op=ALU.mult), reads=[rA, sinT], writes=[rB])
            P.op("dve", lambda e: e.tensor_tensor(out=rA3, in0=rA3, in1=cosT[:, ti:ti + 1, :].broadcast_to([128, nh, 64]),
                                                  op=ALU.mult), reads=[rA, cosT], writes=[rA])
            P.op("pool", lambda e: e.tensor_tensor(out=dst_ap, in0=rA[:, 0:ncols], in1=rB[:, 0:ncols], op=ALU.add),
                 reads=[rA, rB], writes=[dst_tile])

        def proj(xt, s, ti, KT, V, QT=None, SZ=None, rope=True):
            hT = rot(hTs)
            norm_hT(xt, s, hT, xn0)
            kvb = banks[2]
            for kt in range(8):
                P.op("pe", lambda e, kt=kt: e.matmul(kvb[:, :], lhsT=hT[:, kt, :], rhs=win0[:, kt, 1024:1536],
                                                     start=(kt == 0), stop=(kt == 7)),
                     reads=[win0, hT], writes=[kvb], flag=(kt == 7))
            if os.environ.get("V_ENG") == "dve":
                P.op("dve", lambda e: e.tensor_copy(out=V[:, :, 0:64], in_=kvb[:, 256:512].rearrange("p (k d) -> p k d", k=4)),
                     reads=[kvb], writes=[V])
            else:
                P.op("act", lambda e: e.copy(out=V[:, :, 0:64], in_=kvb[:, 256:512].rearrange("p (k d) -> p k d", k=4)),
                     reads=[kvb], writes=[V])
            rope_tok((kvb, kvb[:, 0:256]), 256, ti, krot, krot[:], rope)
            tb = rot(banks[6:8])
            tbb = tb[:, :].bitcast(BF16)
            for kt2 in range(2):
                P.op("pe", lambda e, kt2=kt2, tbb=tbb: e.transpose(out=tbb[:, kt2 * 128:(kt2 + 1) * 128],
                                                                   in_=krot[:, kt2 * 128:(kt2 + 1) * 128], identity=ident[:]),
                     reads=[krot, ident], writes=[tb], flag=(kt2 == 1))
            P.op("act", lambda e, tbb=tbb: e.copy(out=KT[:].rearrange("p a b -> p (a b)"), in_=tbb[:, 0:256]),
                 reads=[tb], writes=[KT])
            if QT is None:
                return
            for half in range(2):
                qb = banks[half]
                for kt in range(8):
                    P.op("pe", lambda e, kt=kt, half=half, qb=qb: e.matmul(
                        qb[:, :], lhsT=hT[:, kt, :], rhs=win0[:, kt, half * 512:(half + 1) * 512],
                        start=(kt == 0), stop=(kt == 7)), reads=[win0, hT], writes=[qb], flag=(kt == 7))
            for half in range(2):
                zb = banks[4 + half]
                for kt in range(8):
                    P.op("pe", lambda e, kt=kt, half=half, zb=zb: e.matmul(
                        zb[:, :], lhsT=hT[:, kt, :], rhs=win0[:, kt, 1536 + half * 512:1536 + (half + 1) * 512],
                        start=(kt == 0), stop=(kt == 7)), reads=[win0, hT], writes=[zb], flag=(kt == 7))
            for half in range(2):
                qb = banks[half]
                rope_tok((qb, qb[:, :]), 512, ti, qrot, qrot[:, half * 512:(half + 1) * 512], rope)
            for half in range(2):
                zb = banks[4 + half]
                P.op("act", lambda e, half=half, zb=zb: e.activation(
                    out=SZ[:, half * 512:(half + 1) * 512], in_=zb[:, :], func=AF.Silu), reads=[zb], writes=[SZ])
            tb = rot(banks[6:8])
            tbb = tb[:, :].bitcast(BF16)
            for qi in range(8):
                P.op("pe", lambda e, qi=qi, tbb=tbb: e.transpose(out=tbb[:, qi * 128:(qi + 1) * 128],
                                                                 in_=qrot[:, qi * 128:(qi + 1) * 128], identity=ident[:]),
                     reads=[qrot, ident], writes=[tb], flag=(qi == 7))
            P.op("act", lambda e, tbb=tbb: e.copy(out=QT[:].rearrange("p a b c -> p (a b c)"), in_=tbb),
                 reads=[tb], writes=[QT])

        def attn(W, QT, SZ, blocks, xt, gate, mask_of):
            PT = W["PT"]
            nb = len(blocks)
            Ob = banks[3:6]
            pts = {}

            def qk(k):
                rows = slice(0, 64) if k % 2 == 0 else slice(64, 128)
                lst = []
                for bi, (KT, V, mi) in enumerate(blocks):
                    sb = rot(banks[0:3])
                    pt = PT[(k % 2) * 5 + bi]
                    P.op("pe", lambda e, KT=KT, sb=sb, rows=rows, k=k: e.matmul(
                        sb[:, :], lhsT=KT[rows, k // 2, :], rhs=QT[rows, k // 2, :, :], start=True, stop=True),
                        reads=[KT, QT], writes=[sb])
                    P.op("act", lambda e, sb=sb, pt=pt: e.activation(out=pt[:], in_=sb[:, :], func=AF.Exp, scale=0.125),
                         reads=[sb], writes=[pt])
                    if mi is not None:
                        eng = "pool" if bi == 0 else "dve"
                        P.op(eng, lambda e, pt=pt, mi=mi: e.tensor_tensor(
                            out=pt[:].rearrange("p (g q) -> p g q", g=4), in0=pt[:].rearrange("p (g q) -> p g q", g=4),
                            in1=masks[:, mi:mi + 1, :].broadcast_to([128, 4, 128]), op=ALU.mult),
                            reads=[pt, masks], writes=[pt])
                    lst.append(pt)
                pts[k] = lst

            def pv(k):
                ob = banks[3 + k % 2]
                for bi, (KT, V, mi) in enumerate(blocks):
                    pt = pts[k][bi]
                    P.op("pe", lambda e, pt=pt, V=V, ob=ob, bi=bi: e.matmul(
                        ob[0:65, :], lhsT=V[:, k, :], rhs=pt[:, :], start=(bi == 0), stop=(bi == nb - 1)),
                        reads=[pt, V], writes=[ob], flag=(bi == nb - 1))
                OT = OTs[k % 2]
                P.op("act", lambda e, OT=OT, ob=ob: e.copy(out=OT[0:65, :], in_=ob[0:65, :]), reads=[ob], writes=[OT])
                for gq in range(4):
                    h = 4 * k + gq
                    otb = banks[5 + h // 8]
                    c0 = (h % 8) * 66
                    P.op("pe", lambda e, OT=OT, otb=otb, c0=c0, gq=gq: e.transpose(
                        out=otb[:, :].bitcast(BF16)[:, c0:c0 + 65], in_=OT[0:65, gq * 128:(gq + 1) * 128],
                        identity=ident[0:65, 0:65]), reads=[OT, ident], writes=[otb])

            qk(0)
            for k in range(4):
                if k + 1 < 4:
                    qk(k + 1)
                pv(k)
            return None

        def attn_mid(W, SZ):
            ovs = [banks[5 + j][:, :].bitcast(BF16)[:, 0:8 * 66].rearrange("p (h e) -> p h e", e=66) for j in range(2)]
            for j in range(2):
                P.op("dve", lambda e, j=j: e.tensor_tensor(
                    out=lsum[:, 8 * j:8 * j + 8], in0=ovs[j][:, :, 64], in1=esink[:, 8 * j:8 * j + 8], op=ALU.add),
                    reads=[banks[5 + j], esink], writes=[lsum])
            P.op("dve", lambda e: e.reciprocal(out=rl[:], in_=lsum[:]), reads=[lsum], writes=[rl])
            t1, g = W["t1"], W["g"]
            for j in range(2):
                P.op("dve", lambda e, j=j: e.tensor_tensor(
                    out=t1[:, 8 * j * 64:(8 * j + 8) * 64].rearrange("p (h d) -> p h d", d=64),
                    in0=ovs[j][:, :, 0:64],
                    in1=rl[:, 8 * j:8 * j + 8].unsqueeze(2).broadcast_to([128, 8, 64]), op=ALU.mult),
                    reads=[banks[5 + j], rl], writes=[t1])
            P.op("dve", lambda e: e.tensor_tensor(out=g[:], in0=t1[:], in1=SZ[:], op=ALU.mult),
                 reads=[t1, SZ], writes=[g])

        def attn_tail(W, xt, gate):
            g, gT, ytmp = W["g"], W["gT"], W["ytmp"]
            tb = rot(banks[6:8])
            tbb = tb[:, :].bitcast(BF16)
            for kt in range(8):
                P.op("pe", lambda e, kt=kt: e.transpose(out=tbb[:, kt * 128:(kt + 1) * 128],
                                                        in_=g[:, kt * 128:(kt + 1) * 128], identity=ident[:]),
                     reads=[g, ident], writes=[tb], flag=(kt == 7))
            P.op("act", lambda e: e.copy(out=gT[:].rearrange("p a b -> p (a b)"), in_=tbb), reads=[tb], writes=[gT])
            for half in range(2):
                yb = banks[half]
                for kt in range(8):
                    P.op("pe", lambda e, kt=kt, half=half, yb=yb: e.matmul(
                        yb[:, :], lhsT=gT[:, kt, :], rhs=wout0[:, kt, half * 512:(half + 1) * 512],
                        start=(kt == 0), stop=(kt == 7)), reads=[gT, wout0], writes=[yb], flag=(kt == 7))
                P.op("dve", lambda e, half=half, yb=yb: e.tensor_tensor(
                    out=ytmp[:, half * 512:(half + 1) * 512], in0=yb[:, :], in1=gate[:, half * 512:(half + 1) * 512],
                    op=ALU.mult), reads=[yb, gate], writes=[ytmp])
            P.op("dve", lambda e: e.tensor_tensor(out=xt[:], in0=xt[:], in1=ytmp[:], op=ALU.add),
                 reads=[xt, ytmp], writes=[xt])

        slot = lambda b: (b + 1) % 4
        proj(xh[0], 0, 0, KTs[slot(-1)], Vs[slot(-1)])
        P.op("pool", lambda e: e.memset(Vr[:, :, 64:65], 1.0), writes=[Vr])
        proj(xh[1], 0, 17, KTr, Vr)
        for i in range(2):
            proj(cx[i], 1, 0, KTc[i], Vc[i], QT=W["QT"][i], SZ=W["SZ"][i], rope=False)
        for i in range(2):
            attn(W, W["QT"][i], W["SZ"][i], [(KTc[0], Vc[0], None), (KTc[1], Vc[1], None)], cx[i], gate_ctx, None)
            attn_mid(W, W["SZ"][i])
            attn_tail(W, cx[i], gate_ctx)
        def blk(b):
            if b == 16:
                return KTr, Vr
            return KTs[slot(b)], Vs[slot(b)]
        def do_proj(n):
            proj(xs[n], 0, 1 + n, *blk(n), QT=W["QT"][n % 2], SZ=W["SZ"][n % 2])
        do_proj(0)
        for n in range(NT):
            if n + 1 < NT:
                do_proj(n + 1)
            kp, vp = blk(n - 1)
            ks, vs_ = blk(n)
            kn, vn = blk(n + 1)
            blocks = [(kp, vp, 2 if n == 0 else 0), (ks, vs_, None), (kn, vn, 3 if n == NT - 1 else 1),
                      (KTc[0], Vc[0], None), (KTc[1], Vc[1], None)]
            attn(W, W["QT"][n % 2], W["SZ"][n % 2], blocks, xs[n], gate_lat, None)
            attn_mid(W, W["SZ"][n % 2])
            attn_tail(W, xs[n], gate_lat)

    if do1:
        win1_d = din("ssm_w_in", [D, 2 * D])
        wglu_d = din("ssm_w_glu", [D, 2 * D])
        wout1_d = din("ssm_w_out", [D, D])
        fnw_d = din("fnw", [1, D])
        lamR_d = din("lamR", [128, 128])
        lamI_d = din("lamI", [128, 128])
        logdt_d = din("logdt", [1, 128])
        bA_d = din("bA", [128, 2048])
        bS_d = din("bS", [128, 2048])
        cA_d = din("cA", [128, 2048])
        cS_d = din("cS", [128, 2048])
        dT_d = din("dT", [128, 64])
        mFB_d = din("mFB", [128, 256])
        idst_d = din("idst", [128, 64])
        sgn_d = din("sgn", [128, 8])
        out_d = dout("out", [TOK, D])
        Ud = P.dram("Ud", [128, 64, 288], BF16)
        Yd = P.dram("Yd", [128, 64, 256], BF16)
        Ydg = [Tile(f"Yd{g}", Yd.t.ap()[:, g, :]) for g in range(64)]
        Ein = P.dram("Ein", [128, 64], F32)
        Eall = P.dram("Eall", [256, 64], F32)
        A.carves = [(lo, hi, t) for (lo, hi, t) in A.carves]
        o = 0

        def carve(name, shape, dtype):
            nonlocal o
            n = int(np.prod(shape[1:]))
            w = n if dtype == F32 else (n + 1) // 2
            t = A.carve(name, o, shape, dtype)
            o += w
            return t

        def TT(eng, O, A_, B_, op):
            P.op(eng, lambda e: e.tensor_tensor(out=O[1], in0=A_[1], in1=B_[1], op=op), reads=[A_[0], B_[0]], writes=[O[0]])

        def TS(eng, O, A_, s1, op0, s2=None, op1=None, extra=()):
            if op1 is None:
                P.op(eng, lambda e: e.tensor_scalar(out=O[1], in0=A_[1], scalar1=s1, scalar2=None, op0=op0),
                     reads=[A_[0]] + list(extra), writes=[O[0]])
            else:
                P.op(eng, lambda e: e.tensor_scalar(out=O[1], in0=A_[1], scalar1=s1, scalar2=s2, op0=op0, op1=op1),
                     reads=[A_[0]] + list(extra), writes=[O[0]])

        def STT(eng, O, A_, sc, B_, op0, op1, extra=()):
            P.op(eng, lambda e: e.scalar_tensor_tensor(out=O[1], in0=A_[1], scalar=sc, in1=B_[1], op0=op0, op1=op1),
                 reads=[A_[0], B_[0]] + list(extra), writes=[O[0]])

        def ACTF(O, A_, func, scale=None, bias=None, extra=()):
            kw = {}
            if scale is not None:
                kw["scale"] = scale
            if bias is not None:
                kw["bias"] = bias
            P.op("act", lambda e: e.activation(out=O[1], in_=A_[1], func=func, **kw), reads=[A_[0]] + list(extra), writes=[O[0]])

        def F(t):
            return (t, t[:])

        win1u = carve("win1u", [128, 8, D], BF16)
        uTs = carve("uTs", [128, 8, 8, 288], BF16)
        wchunk1 = carve("wchunk1", [128, 8, 512], BF16)
        brow1 = carve("brow1", [128, D], F32)
        screp1 = carve("screp1", [128, 2, 8, 128], BF16)
        hT1 = [carve(f"hT1_{i}", [128, 8, 512], BF16) for i in range(2)]
        xn1 = carve("xn1", [128, D], BF16)
        o_p1 = o
        for kt in range(8):
            load_cast(win1u, win1u[:, kt, :], win1_d[kt * 128:(kt + 1) * 128, 0:D], 1024)
        make_screp(screp1)
        adaln(1, wchunk1, [(0, gate_lat)], brow1, screp1)
        tiles1 = [(cx[0], 1), (cx[1], 1)] + [(xs[n], 0) for n in range(NT)]
        for t0 in range(0, len(tiles1), 4):
            grp = tiles1[t0:t0 + 4]
            nt = len(grp)
            hT = rot(hT1)
            for ti, (xt, sidx) in enumerate(grp):
                norm_hT(xt, sidx, hT, xn1, c0=ti * 128)
            j0 = t0 * 16
            for ft in range(8):
                acc = rot(banks[0:4])
                for kt in range(8):
                    P.op("pe", lambda e, kt=kt, ft=ft, acc=acc, hT=hT, nt=nt: e.matmul(
                        acc[:, 0:nt * 128], lhsT=win1u[:, kt, ft * 128:(ft + 1) * 128], rhs=hT[:, kt, 0:nt * 128],
                        start=(kt == 0), stop=(kt == 7)), reads=[win1u, hT], writes=[acc], flag=(kt == 7))
                if ft % 2 == 0:
                    P.op("act", lambda e, ft=ft, acc=acc, j0=j0, nt=nt: e.copy(
                        out=uTs[:, ft, :, j0:j0 + 16 * nt], in_=acc[:, 0:nt * 128].rearrange("p (j s) -> p s j", s=8)),
                        reads=[acc], writes=[uTs])
                else:
                    P.op("dve", lambda e, ft=ft, acc=acc, j0=j0, nt=nt: e.tensor_copy(
                        out=uTs[:, ft, :, j0:j0 + 16 * nt], in_=acc[:, 0:nt * 128].rearrange("p (j s) -> p s j", s=8)),
                        reads=[acc], writes=[uTs])
        for g in range(64):
            ft, gl = g // 8, g % 8
            P.dma("sp", lambda e, g=g, ft=ft, gl=gl: e.dma_start(
                out=Ud.t.ap()[:, g, :].rearrange("(s h) j -> h s j", h=16),
                in_=uTs[gl * 16:(gl + 1) * 16, ft, :, :]), reads=[uTs], writes=[Ud], semtile=uTs)

        o = 0
        NCG = 128
        def sm(name, k=1):
            return carve(name, [128, k, NCG] if k > 1 else [128, NCG], F32)
        COLA = sm("COLA", 9); COLB = sm("COLB", 9)
        BPR = carve("BPR", [128, 2, 64, 8], F32); BPI = carve("BPI", [128, 2, 64, 8], F32)
        CPR = carve("CPR", [128, 2, 64, 8], F32); CPI = carve("CPI", [128, 2, 64, 8], F32)
        bbA = carve("bbA", [128, NCG, 16], BF16); bbS = carve("bbS", [128, NCG, 16], BF16)
        ccA = carve("ccA", [128, NCG, 16], BF16); ccS = carve("ccS", [128, NCG, 16], BF16)
        dTt = carve("dTt", [128, 64], F32)
        mFB = carve("mFB", [128, 2, 128], F32)
        identF = carve("identF", [128, 128], F32)
        idst = carve("idst", [128, 64], F32)
        sgn = carve("sgn", [128, 8], F32)
        Eo = carve("Eo", [128, 64], F32)
        Ei = carve("Ei", [128, 64], F32)
        o_tab = o
        LR = sm("LR"); LI = sm("LI"); DT = sm("DT")
        PR = sm("PR", 9); PI = sm("PI", 9); NR = sm("NR", 8); NI = sm("NI", 8)
        QR = sm("QR", 9); QI = sm("QI", 9)
        tmp = [sm(f"tmp{i}") for i in range(8)]
        bAr = carve("bAr", [128, NCG, 16], F32); bSr = carve("bSr", [128, NCG, 16], F32)
        big1 = carve("big1", [128, NCG, 16], F32); big2 = carve("big2", [128, NCG, 16], F32)
        o = o_tab
        NB = 8
        Bs_ = [carve(f"Bs{i}", [128, 8, 16], BF16) for i in range(NB)]
        Ct_ = [carve(f"Ct{i}", [128, 8, 16], BF16) for i in range(NB)]
        BsT_ = [carve(f"BsT{i}", [128, 128], BF16) for i in range(NB)]
        Mg_ = [carve(f"Mg{i}", [128, 128], BF16) for i in range(NB)]
        g1_ = [carve(f"g1{i}", [128, 8, 16], F32) for i in range(NB)]
        g2_ = [carve(f"g2{i}", [128, 8, 16], F32) for i in range(NB)]
        Zs_ = [carve(f"Zs{i}", [128, 290], BF16) for i in range(NB)]
        S_ = [carve(f"S{i}", [128, 290], BF16) for i in range(NB)]
        Pp_ = [carve(f"Pp{i}", [128, 290], BF16) for i in range(NB)]
        R_ = [carve(f"R{i}", [128, 9, 128], BF16) for i in range(NB)]
        Yb_ = [carve(f"Yb{i}", [128, 256], BF16) for i in range(NB)]
        Yp_ = [carve(f"Yp{i}", [128, 256], BF16) for i in range(NB)]
        Ub_ = [carve(f"Ub{i}", [128, 8, 288], BF16) for i in range(2)]
        print("layer1 phase2 arena words:", o)
        o_p2 = o

        for t_, d_ in ((LR, lamR_d), (LI, lamI_d), (dTt, dT_d), (idst, idst_d), (sgn, sgn_d)):
            P.dma("sp", lambda e, t_=t_, d_=d_: e.dma_start(out=t_[:], in_=d_), writes=[t_])
        P.dma("sp", lambda e: e.dma_start(out=mFB[:].rearrange("p a b -> p (a b)"), in_=mFB_d), writes=[mFB])
        P.dma("sp", lambda e: e.dma_start(out=identF[:], in_=ident_d), writes=[identF])
        P.dma("sp", lambda e: e.dma_start(out=DT[:], in_=logdt_d.partition_broadcast(128)), writes=[DT])
        P.dma("sp", lambda e: e.dma_start(out=bAr[:].rearrange("p a b -> p (a b)"), in_=bA_d), writes=[bAr])
        P.dma("sp", lambda e: e.dma_start(out=bSr[:].rearrange("p a b -> p (a b)"), in_=bS_d), writes=[bSr])
        P.dma("pool", lambda e: e.dma_start(out=ccA[:].rearrange("p a b -> p (a b)"), in_=cA_d), writes=[ccA])
        P.dma("pool", lambda e: e.dma_start(out=ccS[:].rearrange("p a b -> p (a b)"), in_=cS_d), writes=[ccS])

        MUL, ADD, SUB = ALU.mult, ALU.add, ALU.subtract
        T0, T1, T2, T3, T4, T5, T6, T7 = tmp

        def cmul(outr, outi, ar_, ai_, br_, bi_):
            TT("dve", F(T6), ar_, br_, MUL)
            TT("pool", F(T7), ai_, bi_, MUL)
            TT("dve", outr, F(T6), F(T7), SUB)
            TT("dve", F(T6), ar_, bi_, MUL)
            TT("pool", F(T7), ai_, br_, MUL)
            TT("dve", outi, F(T6), F(T7), ADD)

        ACTF(F(DT), F(DT), AF.Exp)
        TT("dve", F(T0), F(LR), F(DT), MUL)
        TT("dve", F(T1), F(LI), F(DT), MUL)
        ACTF(F(T2), F(T0), AF.Exp, scale=0.125)
        ACTF(F(T3), F(T1), AF.Sin, scale=0.125)
        ACTF(F(T4), F(T1), AF.Sin, scale=0.0625)
        TT("dve", F(T4), F(T4), F(T4), MUL)
        TS("dve", F(T4), F(T4), -2.0, MUL, 1.0, ADD)
        zr, zi = (PR, PR[:, 1, :]), (PI, PI[:, 1, :])
        TT("dve", F(T0), F(T2), F(T4), MUL)
        TT("dve", F(T1), F(T2), F(T3), MUL)
        cur = (F(T0), F(T1))
        for it in range(3):
            dst = (zr, zi) if it == 2 else (F(T2), F(T3)) if it == 0 else (F(T4), F(T5))
            cmul(dst[0], dst[1], cur[0], cur[1], cur[0], cur[1])
            cur = dst
        P.op("dve", lambda e: e.memset(PR[:, 0, :], 1.0), writes=[PR])
        P.op("dve", lambda e: e.memset(PI[:, 0, :], 0.0), writes=[PI])
        P.op("dve", lambda e: e.memset(NR[:, 0, :], 1.0), writes=[NR])
        P.op("dve", lambda e: e.memset(NI[:, 0, :], 0.0), writes=[NI])
        for k in range(2, 9):
            cmul((PR, PR[:, k, :]), (PI, PI[:, k, :]), (PR, PR[:, k - 1, :]), (PI, PI[:, k - 1, :]), zr, zi)
        TT("dve", F(T0), zr, zr, MUL)
        TT("dve", F(T1), zi, zi, MUL)
        TT("dve", F(T0), F(T0), F(T1), ADD)
        P.op("dve", lambda e: e.reciprocal(out=T0[:], in_=T0[:]), reads=[T0], writes=[T0])
        TT("dve", (NR, NR[:, 1, :]), zr, F(T0), MUL)
        TT("dve", F(T1), zi, F(T0), MUL)
        TS("dve", (NI, NI[:, 1, :]), F(T1), -1.0, MUL)
        for k in range(2, 8):
            cmul((NR, NR[:, k, :]), (NI, NI[:, k, :]), (NR, NR[:, k - 1, :]), (NI, NI[:, k - 1, :]),
                 (NR, NR[:, 1, :]), (NI, NI[:, 1, :]))
        P.op("dve", lambda e: e.tensor_copy(out=QR[:, 0, :], in_=PR[:, 8, :]), reads=[PR], writes=[QR])
        P.op("dve", lambda e: e.tensor_copy(out=QI[:, 0, :], in_=PI[:, 8, :]), reads=[PI], writes=[QI])
        for l in range(1, 9):
            cmul((QR, QR[:, l, :]), (QI, QI[:, l, :]), (QR, QR[:, l - 1, :]), (QI, QI[:, l - 1, :]),
                 (QR, QR[:, l - 1, :]), (QI, QI[:, l - 1, :]))
        for l in range(9):
            TS("dve", F(T0), (QR, QR[:, l, :]), sgn[:, 2:3], MUL, extra=[sgn])
            STT("dve", (COLA, COLA[:, l, :]), (QI, QI[:, l, :]), sgn[:, 4:5], F(T0), MUL, ADD, extra=[sgn])
            TS("dve", F(T1), (QI, QI[:, l, :]), sgn[:, 2:3], MUL, extra=[sgn])
            STT("dve", (COLB, COLB[:, l, :]), (QR, QR[:, l, :]), sgn[:, 3:4], F(T1), MUL, ADD, extra=[sgn])
        TS("dve", F(T0), zr, -1.0, ADD)
        TT("dve", F(T1), F(LR), F(LR), MUL)
        TT("dve", F(T2), F(LI), F(LI), MUL)
        TT("dve", F(T1), F(T1), F(T2), ADD)
        P.op("dve", lambda e: e.reciprocal(out=T1[:], in_=T1[:]), reads=[T1], writes=[T1])
        TT("dve", F(T2), F(T0), F(LR), MUL)
        TT("dve", F(T3), zi, F(LI), MUL)
        TT("dve", F(T2), F(T2), F(T3), ADD)
        TT("dve", F(T2), F(T2), F(T1), MUL)
        TT("dve", F(T3), zi, F(LR), MUL)
        TT("dve", F(T4), F(T0), F(LI), MUL)
        TT("dve", F(T3), F(T3), F(T4), SUB)
        TT("dve", F(T3), F(T3), F(T1), MUL)
        TS("dve", F(T4), F(T3), sgn[:, 0:1], MUL, extra=[sgn])
        TS("dve", F(T5), F(T3), sgn[:, 1:2], MUL, extra=[sgn])
        def bc16(t):
            return (t, t[:].unsqueeze(2).broadcast_to([128, NCG, 16]))
        TT("dve", F(big1), F(bAr), bc16(T2), MUL)
        TT("pool", F(big2), F(bSr), bc16(T4), MUL)
        TT("dve", F(bbA), F(big1), F(big2), ADD)
        TT("dve", F(big1), F(bSr), bc16(T2), MUL)
        TT("pool", F(big2), F(bAr), bc16(T5), MUL)
        TT("dve", F(bbS), F(big1), F(big2), ADD)
        for c in range(2):
            cs = slice(c * 64, (c + 1) * 64)
            for k in range(8):
                ks = 7 - k if c == 0 else k
                P.op("dve", lambda e, cs=cs, c=c, k=k, ks=ks: e.tensor_copy(
                    out=BPR[:, c, :, k], in_=PR[:, ks, cs]), reads=[PR], writes=[BPR])
                P.op("dve", lambda e, cs=cs, c=c, k=k, ks=ks: e.tensor_scalar(
                    out=BPI[:, c, :, k], in0=PI[:, ks, cs], scalar1=sgn[:, 0:1], scalar2=None, op0=MUL),
                    reads=[PI, sgn], writes=[BPI])
                P.op("dve", lambda e, cs=cs, c=c, k=k, ks=ks: e.tensor_scalar(
                    out=CPR[:, c, :, k], in0=NR[:, ks, cs], scalar1=sgn[:, 1:2], scalar2=None, op0=MUL),
                    reads=[NR, sgn], writes=[CPR])
                P.op("dve", lambda e, cs=cs, c=c, k=k, ks=ks: e.tensor_scalar(
                    out=CPI[:, c, :, k], in0=NI[:, ks, cs], scalar1=-1.0, scalar2=None, op0=MUL),
                    reads=[NI], writes=[CPI])

        def prep(c, g, Ub, gi, k):
            cg = c * 64 + g
            Bs, Ct, BsT, Mg, g1, g2, Zs, R = Bs_[k], Ct_[k], BsT_[k], Mg_[k], g1_[k], g2_[k], Zs_[k], R_[k]
            e1, e2 = ("dve", "pool") if k % 2 == 0 else ("pool", "dve")
            P.op(e1, lambda e: e.tensor_tensor(out=g1[:], in0=BPR[:, c, g, :].unsqueeze(2).broadcast_to([128, 8, 16]),
                                               in1=bbA[:, cg, :].unsqueeze(1).broadcast_to([128, 8, 16]), op=MUL),
                 reads=[BPR, bbA], writes=[g1])
            P.op(e2, lambda e: e.tensor_tensor(out=g2[:], in0=BPI[:, c, g, :].unsqueeze(2).broadcast_to([128, 8, 16]),
                                               in1=bbS[:, cg, :].unsqueeze(1).broadcast_to([128, 8, 16]), op=MUL),
                 reads=[BPI, bbS], writes=[g2])
            P.op(e1, lambda e: e.tensor_tensor(out=Bs[:], in0=g1[:], in1=g2[:], op=ADD), reads=[g1, g2], writes=[Bs])
            P.op(e2, lambda e: e.tensor_tensor(out=g1[:], in0=CPR[:, c, g, :].unsqueeze(2).broadcast_to([128, 8, 16]),
                                               in1=ccA[:, cg, :].unsqueeze(1).broadcast_to([128, 8, 16]), op=MUL),
                 reads=[CPR, ccA], writes=[g1])
            P.op(e1, lambda e: e.tensor_tensor(out=g2[:], in0=CPI[:, c, g, :].unsqueeze(2).broadcast_to([128, 8, 16]),
                                               in1=ccS[:, cg, :].unsqueeze(1).broadcast_to([128, 8, 16]), op=MUL),
                 reads=[CPI, ccS], writes=[g2])
            P.op(e2, lambda e: e.tensor_tensor(out=Ct[:], in0=g1[:], in1=g2[:], op=ADD), reads=[g1, g2], writes=[Ct])
            P.op("dve", lambda e: e.tensor_tensor(
                out=R[:, :, 0:64], in0=idst[:].unsqueeze(1).broadcast_to([128, 9, 64]),
                in1=COLA[:, :, cg:cg + 1].broadcast_to([128, 9, 64]), op=MUL), reads=[idst, COLA], writes=[R])
            P.op("pool", lambda e: e.tensor_tensor(
                out=R[:, :, 64:128], in0=idst[:].unsqueeze(1).broadcast_to([128, 9, 64]),
                in1=COLB[:, :, cg:cg + 1].broadcast_to([128, 9, 64]), op=MUL), reads=[idst, COLB], writes=[R])
            Bs2 = Bs[:].rearrange("p a b -> p (a b)")
            Ct2 = Ct[:].rearrange("p a b -> p (a b)")
            tb = rot(banks[6:8])
            tbb = tb[:, :].bitcast(BF16)
            P.op("pe", lambda e: e.transpose(out=tbb[:, 0:128], in_=Bs2, identity=ident[:]), reads=[Bs, ident], writes=[tb])
            P.op("act", lambda e: e.copy(out=BsT[:], in_=tbb[:, 0:128]), reads=[tb], writes=[BsT])
            mb = banks[5]
            P.op("pe", lambda e: e.matmul(mb[:, 0:128], lhsT=Bs2, rhs=Ct2, start=True, stop=True), reads=[Bs, Ct], writes=[mb])
            if c == 0:
                P.op("dve", lambda e: e.tensor_tensor(out=Mg[:], in0=mb[:, 0:128], in1=mFB[:, 0, :], op=MUL),
                     reads=[mb, mFB], writes=[Mg])
            else:
                P.op("dve", lambda e: e.tensor_tensor(out=g1[:].rearrange("p a b -> p (a b)"), in0=mb[:, 0:128],
                                                      in1=mFB[:, 1, :], op=MUL), reads=[mb, mFB], writes=[g1])
                P.op("dve", lambda e: e.scalar_tensor_tensor(out=Mg[:], in0=identF[:], scalar=dTt[:, g:g + 1],
                                                             in1=g1[:].rearrange("p a b -> p (a b)"), op0=MUL, op1=ADD),
                     reads=[identF, dTt, g1], writes=[Mg])
            zb = rot(banks[0:2])
            ncol = 288 if c == 0 else 256
            ucols = slice(0, 288) if c == 0 else slice(32, 288)
            P.op("pe", lambda e: e.matmul(zb[:, 0:ncol], lhsT=BsT[:], rhs=Ub[:, gi, ucols], start=True, stop=True),
                 reads=[BsT, Ub], writes=[zb])
            P.op("act", lambda e: e.copy(out=Zs[:, 0:ncol], in_=zb[:, 0:ncol]), reads=[zb], writes=[Zs])
            if c == 1:
                P.op("act", lambda e: e.copy(out=Zs[:, 256:257], in_=Ei[:, g:g + 1]), reads=[Ei], writes=[Zs])
                P.dma("sp", lambda e: e.dma_start(out=Yp_[k][:], in_=Ydg[g][:]), reads=[Ydg[g]], writes=[Yp_[k]])

        def scan(c, ks):
            n = 288 if c == 0 else 257
            for l in range(9):
                sh = 1 << l
                if sh >= n:
                    break
                use_act = (l % 2 == 1)
                sbs = {}
                for k in ks:
                    Zs, S, R = Zs_[k], S_[k], R_[k]
                    src = Zs if l == 0 else S
                    sb = rot(banks[1:5])
                    sbs[k] = sb
                    rng = slice(0, n - sh) if c == 0 else slice(sh, n)
                    orng = slice(sh, n) if c == 0 else slice(0, n - sh)
                    if use_act:
                        P.op("pe", lambda e, src=src, sb=sb: e.matmul(
                            sb[:, 0:n], lhsT=ident[:], rhs=src[:, 0:n], start=True, stop=False),
                            reads=[ident, src], writes=[sb], flag=False)
                        P.op("pe", lambda e, l=l, src=src, sb=sb, R=R, rng=rng, orng=orng: e.matmul(
                            sb[:, orng], lhsT=R[:, l, :], rhs=src[:, rng], start=False, stop=True),
                            reads=[R, src], writes=[sb])
                    else:
                        P.op("pe", lambda e, l=l, sh=sh, src=src, sb=sb, R=R, rng=rng: e.matmul(
                            sb[:, 0:n - sh], lhsT=R[:, l, :], rhs=src[:, rng], start=True, stop=True),
                            reads=[R, src], writes=[sb])
                for k in ks:
                    Zs, S = Zs_[k], S_[k]
                    src = Zs if l == 0 else S
                    sb = sbs[k]
                    orng = slice(sh, n) if c == 0 else slice(0, n - sh)
                    if use_act:
                        P.op("act", lambda e, sb=sb, S=S: e.copy(out=S[:, 0:n], in_=sb[:, 0:n]), reads=[sb], writes=[S])
                    else:
                        P.op("dve", lambda e, sh=sh, src=src, sb=sb, S=S, orng=orng: e.tensor_tensor(
                            out=S[:, orng], in0=sb[:, 0:n - sh], in1=src[:, orng], op=ADD), reads=[sb, src], writes=[S])
                        if l == 0:
                            edge = slice(0, 1) if c == 0 else slice(n - 1, n)
                            P.op("act", lambda e, S=S, Zs=Zs, edge=edge: e.copy(out=S[:, edge], in_=Zs[:, edge]),
                                 reads=[Zs], writes=[S])

        def finish(c, g, Ub, gi, k):
            Ct, Mg, Zs, S, Pp, Yb = Ct_[k], Mg_[k], Zs_[k], S_[k], Pp_[k], Yb_[k]
            n = 288 if c == 0 else 257
            Ct2 = Ct[:].rearrange("p a b -> p (a b)")
            P.op("pool", lambda e: e.tensor_tensor(out=Pp[:, 0:n], in0=S[:, 0:n], in1=Zs[:, 0:n], op=SUB),
                 reads=[S, Zs], writes=[Pp])
            if c == 0:
                P.op("act", lambda e: e.copy(out=Eo[:, g:g + 1], in_=S[:, 287:288]), reads=[S], writes=[Eo])
            yb = rot(banks[5:6] + banks[0:1])
            pc = slice(32, 288) if c == 0 else slice(0, 256)
            P.op("pe", lambda e: e.matmul(yb[:, 0:256], lhsT=Mg[:], rhs=Ub[:, gi, 32:288], start=True, stop=False),
                 reads=[Mg, Ub], writes=[yb], flag=False)
            P.op("pe", lambda e: e.matmul(yb[:, 0:256], lhsT=Ct2, rhs=Pp[:, pc], start=False, stop=True),
                 reads=[Ct, Pp], writes=[yb])
            if c == 0:
                P.op("act", lambda e: e.copy(out=Yb[:], in_=yb[:, 0:256]), reads=[yb], writes=[Yb])
            else:
                P.op("dve", lambda e: e.tensor_tensor(out=Yb[:], in0=yb[:, 0:256], in1=Yp_[k][:], op=ADD),
                     reads=[yb, Yp_[k]], writes=[Yb])
            P.dma("sp", lambda e: e.dma_start(out=Ydg[g][:], in_=Yb[:]), reads=[Yb], writes=[Ydg[g]], semtile=Yb)

        UBS = 4
        for c in range(2):
            if c == 1:
                P.dma("sp", lambda e: e.dma_start(out=Ein.t.ap(), in_=Eo[:]), reads=[Eo], writes=[Ein], semtile=Eo)
                P.dma("pool", lambda e: e.collective_compute("AllGather", ALU.bypass, [[0, 1], [2, 3], [4, 5], [6, 7]],
                                                             ins=[Ein.t.ap()], outs=[Eall.t.ap()]),
                      reads=[Ein], writes=[Eall], semtile=Eall, inc=1)
                Et = g1_[0]
                P.dma("sp", lambda e: e.dma_start(out=Et[:].rearrange("p a b -> p (a b)").rearrange("p (r f) -> p r f", r=2),
                                                  in_=Eall.t.ap().rearrange("(r p) f -> p r f", p=128)),
                      reads=[Eall], writes=[Et])
                Et2 = Et[:].rearrange("p a b -> p (a b)")
                P.op("dve", lambda e: e.tensor_scalar(out=Ei[:], in0=Et2[:, 0:64], scalar1=sgn[:, 6:7], scalar2=None, op0=MUL),
                     reads=[Et, sgn], writes=[Ei])
                P.op("dve", lambda e: e.scalar_tensor_tensor(out=Ei[:], in0=Et2[:, 64:128], scalar=sgn[:, 7:8], in1=Ei[:],
                                                             op0=MUL, op1=ADD), reads=[Et, sgn, Ei], writes=[Ei])
            batches = []
            for g0 in range(0, 64, 8):
                for h2 in range(2):
                    batches.append((g0, h2))
            Ubs = {}
            def get_ub(g0):
                if g0 not in Ubs:
                    Ub = Ub_[(g0 // 8) % 2]
                    P.dma("sp", lambda e, g0=g0, Ub=Ub: e.dma_start(out=Ub[:], in_=Ud.t.ap()[:, g0:g0 + 8, :]),
                          reads=[Ud], writes=[Ub])
                    Ubs[g0] = Ub
                return Ubs[g0]
            def do_prep(bi):
                g0, h2 = batches[bi]
                Ub = get_ub(g0)
                for u in range(UBS):
                    gi = h2 * UBS + u
                    prep(c, g0 + gi, Ub, gi, (bi % 2) * UBS + u)
            do_prep(0)
            for bi in range(len(batches)):
                if bi + 1 < len(batches):
                    do_prep(bi + 1)
                g0, h2 = batches[bi]
                ks = [(bi % 2) * UBS + u for u in range(UBS)]
                scan(c, ks)
                for u in range(UBS):
                    gi = h2 * UBS + u
                    finish(c, g0 + gi, Ubs[g0], gi, (bi % 2) * UBS + u)

        o = 0
        wglu = carve("wglu", [128, 8, 2 * D], BF16)
        win1z = carve("win1z", [128, 8, D], BF16)
        wout1 = carve("wout1", [128, 8, D], BF16)
        yTs = carve("yTs", [128, 8, 8, 128], BF16)
        fnw = carve("fnw", [128, D], F32)
        hT3 = [carve(f"hT3_{i}", [128, 8, 128], BF16) for i in range(2)]
        xn3 = carve("xn3", [128, D], BF16)
        gyT = [carve(f"gyT{i}", [128, 8, 128], BF16) for i in range(2)]
        sg = carve("sg", [128, D], BF16)
        sz = carve("sz", [128, D], BF16)
        tm = carve("tm", [128, D], BF16)
        mm = carve("mm", [128, D], BF16)
        mT = carve("mT", [128, 8, 128], BF16)
        ytmp3 = carve("ytmp3", [128, D], F32)
        outv = [ytmp3]
        print("layer1 phase3 arena words:", o)
        for kt in range(8):
            load_cast(win1z, win1z[:, kt, :], win1_d[kt * 128:(kt + 1) * 128, D:2 * D], 1024)
        for kt in range(8):
            load_cast(wglu, wglu[:, kt, 0:1024], wglu_d[kt * 128:(kt + 1) * 128, 0:1024], 1024)
            load_cast(wglu, wglu[:, kt, 1024:2048], wglu_d[kt * 128:(kt + 1) * 128, 1024:2048], 1024)
        for kt in range(8):
            load_cast(wout1, wout1[:, kt, :], wout1_d[kt * 128:(kt + 1) * 128, :], 1024)
        P.dma("sp", lambda e: e.dma_start(out=fnw[:], in_=fnw_d.partition_broadcast(128)), writes=[fnw])
        gys, hTs3 = {}, {}

        def pre3(n):
            if n % 8 == 0:
                hh = n // 8
                for g in range(64):
                    ft, gl = g // 8, g % 8
                    P.dma("sp", lambda e, g=g, ft=ft, gl=gl, hh=hh: e.dma_start(
                        out=yTs[gl * 16:(gl + 1) * 16, ft, :, :],
                        in_=Ydg[g][:, hh * 128:(hh + 1) * 128].rearrange("(t h) j -> h t j", h=16)),
                        reads=[Ydg[g]], writes=[yTs])
            gy = gyT[n % 2]
            for ft in range(8):
                P.op("act", lambda e, ft=ft, gy=gy, n=n: e.activation(
                    out=gy[:, ft, :].rearrange("p (j t) -> p j t", t=8),
                    in_=yTs[:, ft, :, (n % 8) * 16:(n % 8 + 1) * 16].rearrange("p t j -> p j t"), func=AF.Gelu),
                    reads=[yTs], writes=[gy])
            hT = hT3[n % 2]
            norm_hT(xs[n], 0, hT, xn3)
            gys[n], hTs3[n] = gy, hT

        pre3(0)
        for n in range(NT):
            xt = xs[n]
            gy, hT = gys[n], hTs3[n]
            for half in range(2):
                zb = banks[half]
                for kt in range(8):
                    P.op("pe", lambda e, kt=kt, half=half, zb=zb, hT=hT: e.matmul(
                        zb[:, :], lhsT=hT[:, kt, :], rhs=win1z[:, kt, half * 512:(half + 1) * 512],
                        start=(kt == 0), stop=(kt == 7)), reads=[win1z, hT], writes=[zb], flag=(kt == 7))
                P.op("act", lambda e, half=half, zb=zb: e.activation(out=sz[:, half * 512:(half + 1) * 512], in_=zb[:, :],
                                                                     func=AF.Silu), reads=[zb], writes=[sz])
            for q4 in range(4):
                gb = banks[2 + q4]
                for kt in range(8):
                    P.op("pe", lambda e, kt=kt, q4=q4, gb=gb, gy=gy: e.matmul(
                        gb[:, :], lhsT=gy[:, kt, :], rhs=wglu[:, kt, q4 * 512:(q4 + 1) * 512],
                        start=(kt == 0), stop=(kt == 7)), reads=[wglu, gy], writes=[gb], flag=(kt == 7))
            if n + 1 < NT and n % 8 != 7:
                pre3(n + 1)
            for half in range(2):
                P.op("act", lambda e, half=half: e.activation(out=sg[:, half * 512:(half + 1) * 512], in_=banks[4 + half][:, :],
                                                              func=AF.Sigmoid), reads=[banks[4 + half]], writes=[sg])
                P.op("dve", lambda e, half=half: e.tensor_tensor(out=tm[:, half * 512:(half + 1) * 512], in0=banks[2 + half][:, :],
                                                                 in1=sg[:, half * 512:(half + 1) * 512], op=ALU.mult),
                     reads=[banks[2 + half], sg], writes=[tm])
            P.op("dve", lambda e: e.tensor_tensor(out=mm[:], in0=tm[:], in1=sz[:], op=ALU.mult), reads=[tm, sz], writes=[mm])
            tb = banks[6]
            tbb = tb[:, :].bitcast(BF16)
            for kt in range(8):
                P.op("pe", lambda e, kt=kt, tbb=tbb: e.transpose(out=tbb[:, kt * 128:(kt + 1) * 128],
                                                                 in_=mm[:, kt * 128:(kt + 1) * 128], identity=ident[:]),
                     reads=[mm, ident], writes=[tb], flag=(kt == 7))
            P.op("act", lambda e, tbb=tbb: e.copy(out=mT[:].rearrange("p a b -> p (a b)"), in_=tbb), reads=[tb], writes=[mT])
            for half in range(2):
                yb = banks[half]
                for kt in range(8):
                    P.op("pe", lambda e, kt=kt, half=half, yb=yb: e.matmul(
                        yb[:, :], lhsT=mT[:, kt, :], rhs=wout1[:, kt, half * 512:(half + 1) * 512],
                        start=(kt == 0), stop=(kt == 7)), reads=[mT, wout1], writes=[yb], flag=(kt == 7))
                P.op("dve", lambda e, half=half, yb=yb: e.tensor_tensor(
                    out=ytmp3[:, half * 512:(half + 1) * 512], in0=yb[:, :], in1=gate_lat[:, half * 512:(half + 1) * 512],
                    op=ALU.mult), reads=[yb, gate_lat], writes=[ytmp3])
            P.op("dve", lambda e, xt=xt: e.tensor_tensor(out=xt[:], in0=xt[:], in1=ytmp3[:], op=ALU.add),
                 reads=[xt, ytmp3], writes=[xt])
            sst = rot(ss)
            ov = rot(outv)
            P.op("act", lambda e, xt=xt, sst=sst: e.activation(out=sg[:], in_=xt[:], func=AF.Square, accum_out=sst[:, 0:1]),
                 reads=[xt], writes=[sg, sst])
            P.op("act", lambda e, sst=sst: e.activation(out=sst[:, 1:2], in_=sst[:, 0:1], func=AF.Sqrt, scale=1.0 / D,
                                                        bias=small[:, 0:1]), reads=[sst, small], writes=[sst])
            P.op("dve", lambda e, sst=sst: e.reciprocal(out=sst[:, 2:3], in_=sst[:, 1:2]), reads=[sst], writes=[sst])
            P.op("dve", lambda e, xt=xt, sst=sst, ov=ov: e.scalar_tensor_tensor(
                out=ov[:], in0=xt[:], scalar=sst[:, 2:3], in1=fnw[:], op0=ALU.mult, op1=ALU.mult),
                reads=[xt, sst, fnw], writes=[ov])
            P.out_evs.append(P.dma("sp", lambda e, n=n, ov=ov: e.dma_start(out=out_d[n * 128:(n + 1) * 128, :], in_=ov[:]),
                                   reads=[ov]))
            if n + 1 < NT and n % 8 == 7:
                pre3(n + 1)

    if mode == "l0":
        for n in range(NT):
            P.out_evs.append(P.dma("sp", lambda e, n=n: e.dma_start(out=x1_d[n * 128:(n + 1) * 128, :], in_=xs[n][:]),
                                   reads=[xs[n]]))
        for n in range(2):
            P.out_evs.append(P.dma("sp", lambda e, n=n: e.dma_start(out=ctx1_d[n * 128:(n + 1) * 128, :], in_=cx[n][:]),
                                   reads=[cx[n]]))

    P.wait_all("sp", P.out_evs)
    P.emit()
    return nc


def _in_maps(inputs, mode):
    x = np.asarray(inputs["x"], np.float32)
    c = np.asarray(inputs["c"], np.float32)
    ctx = np.asarray(inputs["ctx"], np.float32)
    c_ctx = np.asarray(inputs["c_ctx"], np.float32)
    nwT = np.concatenate([_colT(inputs["norm_w"][l]) for l in range(2)], 1)
    badaT = np.concatenate([_colT(inputs["b_ada"][l]) for l in range(2)], 1)
    maps = []
    for core in range(NCORES):
        b, hf = core // 2, core % 2
        cs = _consts(hf)
        xh = np.zeros((256, D), np.float32)
        if hf == 0:
            xl = x[b, 0:TOK]
            xh[128:256] = x[b, TOK:TOK + 128]
            cl = ctx[b]
        else:
            xl = x[b, TOK:2 * TOK][::-1]
            xh[128:256] = x[b, TOK - 128:TOK][::-1]
            cl = ctx[b][::-1]
        cT = np.stack([_colT(c[b]), _colT(c_ctx)], 2).reshape(128, 16)
        m = dict(x=np.ascontiguousarray(xl), ctx=np.ascontiguousarray(cl),
                 cT=np.ascontiguousarray(cT), nwT=nwT, badaT=badaT,
                 w_ada=np.asarray(inputs["w_ada"], np.float32), b_ada=np.asarray(inputs["b_ada"], np.float32),
                 ident=cs["ident"])
        if mode in ("full", "l0"):
            m.update(xh=xh, attn_w_in=np.asarray(inputs["attn_w_in"][0], np.float32),
                     attn_w_out=np.asarray(inputs["attn_w_out"][0], np.float32),
                     attn_sink=np.asarray(inputs["attn_sink"], np.float32).reshape(1, 16),
                     cosT=cs["cosT"], sinT=cs["sinT"], masks=cs["masks"], psw=cs["psw"])
        if mode in ("full", "l1"):
            dirs = (hf, 1 - hf)
            def dup(a):
                t = np.concatenate([np.asarray(a[0][d_], np.float32).T for d_ in dirs], 1)
                return np.ascontiguousarray(np.concatenate([t, t], 0))
            bre, bim = inputs["ssm_b_re"][0], inputs["ssm_b_im"][0]
            cre, cim = inputs["ssm_c_re"][0], inputs["ssm_c_im"][0]
            def bl(a):
                return np.concatenate([np.asarray(a[d_], np.float32).transpose(1, 0, 2).reshape(64, 64 * 16) for d_ in dirs], 1)
            def cl_(a):
                return np.concatenate([np.asarray(a[d_], np.float32).transpose(2, 0, 1).reshape(64, 64 * 16) for d_ in dirs], 1)
            dvec = np.asarray(inputs["ssm_d"][0], np.float32).reshape(64, 16)
            dT = np.ascontiguousarray(np.tile(dvec.T, (8, 1)))
            sidx = np.arange(128) // 16
            mF = (sidx[None, :] >= sidx[:, None]).astype(np.float32)
            mB = (sidx[:, None] >= sidx[None, :]).astype(np.float32)
            sg = np.zeros((128, 8), np.float32)
            top = np.arange(128) < 64
            sg[:, 0] = np.where(top, -1.0, 1.0); sg[:, 1] = np.where(top, 1.0, -1.0)
            sg[:, 2] = top; sg[:, 3] = ~top; sg[:, 4] = np.where(top, 0.0, -1.0); sg[:, 5] = -1.0
            sg[:, 6] = 1.0 if hf == 1 else 0.0
            sg[:, 7] = 1.0 if hf == 0 else 0.0
            m.update(ssm_w_in=np.asarray(inputs["ssm_w_in"][0], np.float32),
                     ssm_w_glu=np.asarray(inputs["ssm_w_glu"][0], np.float32),
                     ssm_w_out=np.asarray(inputs["ssm_w_out"][0], np.float32),
                     fnw=np.asarray(inputs["final_norm_w"], np.float32).reshape(1, D),
                     lamR=dup(inputs["ssm_lam_re"]), lamI=dup(inputs["ssm_lam_im"]),
                     logdt=np.ascontiguousarray(np.concatenate([np.asarray(inputs["ssm_log_dt"][0][d_], np.float32) for d_ in dirs]).reshape(1, 128)),
                     bA=np.ascontiguousarray(np.concatenate([bl(bre), bl(bim)], 0)),
                     bS=np.ascontiguousarray(np.concatenate([bl(bim), bl(bre)], 0)),
                     cA=np.ascontiguousarray(np.concatenate([cl_(cre), cl_(cim)], 0)),
                     cS=np.ascontiguousarray(np.concatenate([cl_(cim), cl_(cre)], 0)),
                     dT=dT, mFB=np.ascontiguousarray(np.concatenate([mF, mB], 1)),
                     idst=np.ascontiguousarray(np.concatenate([np.eye(64, dtype=np.float32)] * 2, 0)), sgn=sg)
        maps.append(m)
    return maps


def gather(results, key, n=TOK):
    out = np.zeros((4, 2 * n, D), np.float32)
    for core in range(NCORES):
        b, hf = core // 2, core % 2
        r = results[core][key]
        out[b, hf * n:(hf + 1) * n] = r if hf == 0 else r[::-1]
    return out


def run_mode(inputs, mode):
    nc = build(mode)
    res = run_bass_kernel_spmd(nc, _in_maps(inputs, mode), core_ids=list(range(NCORES)))
    return res.results


def kernel(**inputs):
    r = run_mode(inputs, "full")
    return gather(r, "out")
```

```python
import os
import numpy as np
import concourse.bass as bass
import concourse.mybir as mybir
from concourse.bass_utils import run_bass_kernel_spmd

F32 = mybir.dt.float32
BF16 = mybir.dt.bfloat16
AF = mybir.ActivationFunctionType
ALU = mybir.AluOpType
AX = mybir.AxisListType

EPOCH = 3000
NCORES = 8
D = 1024
TOK = 2048
NT = 16
EPS = 1e-6


class Tile:
    def __init__(self, name, t):
        self.name = name
        self.t = t
        self.wev = None
        self.revs = {}
        self.dsem = None
        self.dcnt = 0
        self.aliases = []

    def __getitem__(self, k):
        return self.t[k]


class Prog:
    ENGS = ("pe", "act", "dve", "pool", "sp")

    def __init__(self, nc, same_engine_sync=True):
        self.nc = nc
        self.same_sync = same_engine_sync
        self.stream = {e: [] for e in self.ENGS}
        self.sems = {e: [] for e in self.ENGS}
        self.cnt = {e: 0 for e in self.ENGS}
        self.waited = {e: {} for e in self.ENGS}
        self.own = {e: set() for e in self.ENGS}
        self.nsem = 0
        for e in self.ENGS:
            self._new_epoch(e)
        self.tiles = []
        self.out_evs = []

    def _sem(self, name):
        self.nsem += 1
        return self.nc.alloc_semaphore(name=f"{name}_{self.nsem}")

    def _new_epoch(self, e):
        s = self._sem(f"e_{e}")
        self.sems[e].append(s)
        self.own[e].add(id(s))
        self.cnt[e] = 0

    def sbuf(self, name, shape, dtype):
        t = self.nc.alloc_sbuf_tensor("s_" + name, list(shape), dtype)
        tl = Tile(name, t)
        self.tiles.append(tl)
        return tl

    def psum(self, name, shape, dtype=F32):
        t = self.nc.alloc_psum_tensor("p_" + name, list(shape), dtype)
        tl = Tile(name, t)
        self.tiles.append(tl)
        return tl

    def dram(self, name, shape, dtype, **kw):
        t = self.nc.dram_tensor(name, list(shape), dtype, **kw)
        tl = Tile(name, t)
        self.tiles.append(tl)
        return tl

    def _need(self, eng, waits, ev):
        if ev is None:
            return
        sem, val = ev
        k = id(sem)
        if k in self.own[eng] and (eng == "pe" or not self.same_sync):
            return
        if self.waited[eng].get(k, 0) >= val:
            return
        if waits.get(k, (None, 0))[1] < val:
            waits[k] = (sem, val)

    def _deps(self, eng, reads, writes):
        waits = {}
        for t in reads:
            self._need(eng, waits, t.wev)
        for t0 in writes:
            for t in [t0] + t0.aliases:
                self._need(eng, waits, t.wev)
                for ev in t.revs.values():
                    self._need(eng, waits, ev)
        for t0 in reads:
            for t in t0.aliases:
                pass
        wl = list(waits.values())
        for sem, val in wl:
            self.waited[eng][id(sem)] = val
        return wl

    def _mark(self, ev, reads, writes):
        sem, val = ev
        for t in reads:
            old = t.revs.get(id(sem))
            if old is None or old[1] < val:
                t.revs[id(sem)] = ev
        for t in writes:
            t.wev = ev
            t.revs = {}

    def op(self, eng, fn, reads=(), writes=(), flag=True):
        wl = self._deps(eng, reads, writes)
        sem = self.sems[eng][-1]
        ev = (sem, self.cnt[eng] + 1)
        if flag:
            self.cnt[eng] += 1
        self.stream[eng].append((wl, fn, ev if flag else None, 1))
        self._mark(ev, reads, writes)
        if flag and self.cnt[eng] >= EPOCH:
            self._new_epoch(eng)
        return ev

    def dma(self, q, fn, reads=(), writes=(), semtile=None, inc=16):
        wl = self._deps(q, reads, writes)
        st = semtile or (writes[0] if writes else reads[0])
        if st.dsem is None:
            st.dsem = self._sem("d_" + st.name)
        st.dcnt += inc
        ev = (st.dsem, st.dcnt)
        self.stream[q].append((wl, fn, ev, inc))
        self._mark(ev, reads, writes)
        return ev

    def wait_all(self, eng, evs):
        waits = {}
        for ev in evs:
            self._need(eng, waits, ev)
        wl = list(waits.values())
        for sem, val in wl:
            self.waited[eng][id(sem)] = val
        self.stream[eng].append((wl, None, None, 0))

    def emit(self):
        nc = self.nc
        with nc.Block() as block:
            def replay(name):
                def f(e):
                    for wl, fn, ev, inc in self.stream[name]:
                        for sem, val in wl:
                            e.wait_ge(sem, val)
                        if fn is None:
                            continue
                        ins = fn(e)
                        if ev is not None:
                            ins.then_inc(ev[0], inc)
                return f
            block.tensor(replay("pe"))
            block.scalar(replay("act"))
            block.vector(replay("dve"))
            block.gpsimd(replay("pool"))
            block.sync(replay("sp"))


class Arena:
    def __init__(self, P, name, words):
        self.P = P
        self.name = name
        self.words = words
        self.base = P.nc.alloc_sbuf_tensor("s_" + name, [128, words], F32)
        self.carves = []

    def carve(self, name, off, shape, dtype):
        n = int(np.prod(shape[1:]))
        w = n if dtype == F32 else (n + 1) // 2
        assert off + w <= self.words, (name, off, w, self.words)
        ap = self.base[:, off:off + w]
        if dtype != F32:
            ap = ap.bitcast(dtype)
        if len(shape) == 3:
            ap = ap.rearrange("p (a b) -> p a b", a=shape[1])
        elif len(shape) == 4:
            ap = ap.rearrange("p (a b c) -> p a b c", a=shape[1], b=shape[2])
        tl = Tile(name, ap)
        for lo, hi, old in self.carves:
            if lo < off + w and off < hi:
                tl.aliases.append(old)
                old.aliases.append(tl)
        self.carves.append((off, off + w, tl))
        self.P.tiles.append(tl)
        return tl


def _rope_tables(hf):
    inv = (10000.0 ** (-np.arange(16, dtype=np.float32) / 16)).astype(np.float32)
    j = np.arange(2304)
    g = (j - 128) if hf == 0 else (4095 - (j - 128))
    g = np.clip(g, 0, 4095)
    row = (g // 64).astype(np.float32)
    col = (g % 64).astype(np.float32)
    cosT = np.zeros((2304, 64), np.float32)
    sinT = np.zeros((2304, 64), np.float32)
    for d in range(64):
        axis, half, f = d // 32, (d % 32) // 16, d % 16
        ang = ((row if axis == 0 else col) * inv[f]).astype(np.float32)
        cosT[:, d] = np.cos(ang)
        sinT[:, d] = np.sin(ang) * (-1.0 if half == 0 else 1.0)
    cosT = np.ascontiguousarray(cosT.reshape(18, 128, 64).transpose(1, 0, 2).reshape(128, 18 * 64))
    sinT = np.ascontiguousarray(sinT.reshape(18, 128, 64).transpose(1, 0, 2).reshape(128, 18 * 64))
    return cosT, sinT


def _consts(hf):
    cosT, sinT = _rope_tables(hf)
    jj = np.arange(128)[:, None]
    ii = np.arange(128)[None, :]
    mP = (jj >= ii).astype(np.float32)
    mN = (jj <= ii).astype(np.float32)
    masks = np.stack([mP, mN, mP * 0.0, mN], 1)
    ident = np.eye(128, dtype=np.float32)
    psw = np.zeros((128, 128), np.float32)
    for m in range(128):
        d = m % 64
        sw = d + 16 if (d % 32) < 16 else d - 16
        psw[(m // 64) * 64 + sw, m] = 1.0
    return dict(cosT=cosT, sinT=sinT, masks=np.ascontiguousarray(masks.reshape(128, 4 * 128)),
                ident=ident, psw=psw)


def _colT(v):
    return np.ascontiguousarray(np.asarray(v, np.float32).reshape(-1, 128).T)


def build(mode="full"):
    do0 = mode in ("full", "l0")
    do1 = mode in ("full", "l1")
    nc = bass.Bass("TRN2", target_bir_lowering=False, dynamic_dma_scratch_size=4096)
    P = Prog(nc)

    def din(name, shape):
        return nc.dram_tensor(name, list(shape), F32, kind="ExternalInput").ap()

    def dout(name, shape):
        return nc.dram_tensor(name, list(shape), F32, kind="ExternalOutput").ap()

    x_d = din("x", [TOK, D])
    ctx_d = din("ctx", [256, D])
    cT_d = din("cT", [128, 16])
    nwT_d = din("nwT", [128, 16])
    badaT_d = din("badaT", [128, 48])
    wada_d = din("w_ada", [2, D, 3 * D])
    bada_d = din("b_ada", [2, 3 * D])
    ident_d = din("ident", [128, 128])
    if do0:
        xh_d = din("xh", [256, D])
        win0_d = din("attn_w_in", [D, 2560])
        wout0_d = din("attn_w_out", [D, D])
        sink_d = din("attn_sink", [1, 16])
        cos_d = din("cosT", [128, 1152])
        sin_d = din("sinT", [128, 1152])
        masks_d = din("masks", [128, 512])
        psw_d = din("psw", [128, 128])
    if mode == "l0":
        x1_d = dout("x1", [TOK, D])
        ctx1_d = dout("ctx1", [256, D])

    xs = [P.sbuf(f"xs{n}", [128, D], F32) for n in range(NT)]
    cx = [P.sbuf(f"cx{n}", [128, D], F32) for n in range(2)]
    ident = P.sbuf("ident", [128, 128], BF16)
    small = P.sbuf("small", [128, 256], F32)
    cTs = P.sbuf("cTs", [128, 16], F32)
    scbf = P.sbuf("scbf", [128, 8, 2], BF16)
    nwT = P.sbuf("nwT", [128, 16], F32)
    badaT = P.sbuf("badaT", [128, 48], F32)
    modT = P.sbuf("modT", [128, 16, 2], F32)
    wmT = P.sbuf("wmT", [128, 8, 2], F32)
    gate_lat = P.sbuf("gate_lat", [128, D], F32)
    banks = [P.psum(f"B{i}", [128, 512], F32) for i in range(8)]
    ss = [P.sbuf(f"ss{i}", [128, 4], F32) for i in range(2)]

    A = Arena(P, "arena", 28200 + 2 * 1280)
    stages = [A.carve(f"stg{i}", 28200 + i * 1280, [128, 1280], F32) for i in range(2)]
    cast_rr = {"n": 0}

    def load_cast(dst_tile, dst_ap, src_ap, ncols, engines=("act", "dve", "pool"), view=None):
        st = stages[cast_rr["n"] % 2]
        eng = engines[cast_rr["n"] % len(engines)]
        cast_rr["n"] += 1
        P.dma("sp", lambda e: e.dma_start(out=st[:, 0:ncols], in_=src_ap), writes=[st])
        src = st[:, 0:ncols] if view is None else view(st[:, 0:ncols])
        if eng == "act":
            P.op("act", lambda e: e.copy(out=dst_ap, in_=src), reads=[st], writes=[dst_tile])
        else:
            P.op(eng, lambda e: e.tensor_copy(out=dst_ap, in_=src), reads=[st], writes=[dst_tile])

    P.dma("pool", lambda e: e.dma_start(out=ident[:], in_=ident_d), writes=[ident])
    P.dma("sp", lambda e: e.dma_start(out=cTs[:], in_=cT_d), writes=[cTs])
    P.dma("sp", lambda e: e.dma_start(out=nwT[:], in_=nwT_d), writes=[nwT])
    P.dma("sp", lambda e: e.dma_start(out=badaT[:], in_=badaT_d), writes=[badaT])
    for n in range(NT):
        P.dma("sp", lambda e, n=n: e.dma_start(out=xs[n][:], in_=x_d[n * 128:(n + 1) * 128, :]),
              writes=[xs[n]])
    for n in range(2):
        P.dma("sp", lambda e, n=n: e.dma_start(out=cx[n][:], in_=ctx_d[n * 128:(n + 1) * 128, :]),
              writes=[cx[n]])
    P.op("act", lambda e: e.activation(out=scbf[:].rearrange("p t s -> p (t s)"), in_=cTs[:], func=AF.Silu),
         reads=[cTs], writes=[scbf])

    def make_screp(screp):
        for s in range(2):
            P.op("dve", lambda e, s=s: e.tensor_copy(
                out=screp[:, s, :, :], in_=scbf[:, :, s:s + 1].broadcast_to([128, 8, 128])),
                reads=[scbf], writes=[screp])

    cnt = {"n": 0}

    def rot(lst):
        cnt["n"] += 1
        return lst[cnt["n"] % len(lst)]

    def adaln(layer, wchunk, gate_tiles, brow, screp):
        accb = banks[7]
        for j6 in range(6):
            for kt in range(0, 8, 2):
                load_cast(wchunk, wchunk[:, kt:kt + 2, :],
                          wada_d[layer, kt * 128:(kt + 2) * 128, j6 * 512:(j6 + 1) * 512].rearrange("(a p) c -> p a c", p=128),
                          1024, view=lambda v: v.rearrange("p (a c) -> p a c", a=2))
            if j6 < 4:
                for jj in range(4):
                    j = j6 * 4 + jj
                    for kt in range(8):
                        P.op("pe", lambda e, kt=kt, jj=jj, j=j: e.matmul(
                            accb[:, 2 * j:2 * j + 2], lhsT=wchunk[:, kt, jj * 128:(jj + 1) * 128],
                            rhs=scbf[:, kt, :], start=(kt == 0), stop=(kt == 7)),
                            reads=[wchunk, scbf], writes=[accb], flag=(kt == 7))
            else:
                for (s, gt) in gate_tiles:
                    gb = banks[5 + s]
                    for kt in range(8):
                        P.op("pe", lambda e, kt=kt, s=s, gb=gb: e.matmul(
                            gb[:, :], lhsT=screp[:, s, kt, :], rhs=wchunk[:, kt, :],
                            start=(kt == 0), stop=(kt == 7)),
                            reads=[wchunk, screp], writes=[gb], flag=(kt == 7))
                    c0 = (j6 - 4) * 512
                    if s == gate_tiles[0][0]:
                        P.dma("sp", lambda e, c0=c0: e.dma_start(
                            out=brow[:, c0:c0 + 512],
                            in_=bada_d[layer:layer + 1, 2048 + c0:2048 + c0 + 512].partition_broadcast(128)),
                            writes=[brow])
                    P.op("dve", lambda e, gb=gb, gt=gt, c0=c0: e.tensor_tensor(
                        out=gt[:, c0:c0 + 512], in0=gb[:, :], in1=brow[:, c0:c0 + 512], op=ALU.add),
                        reads=[gb, brow], writes=[gt])
        P.op("dve", lambda e: e.tensor_tensor(
            out=modT[:], in0=accb[:, 0:32].rearrange("p (j s) -> p j s", s=2),
            in1=badaT[:, layer * 24:layer * 24 + 16].unsqueeze(2).broadcast_to([128, 16, 2]), op=ALU.add),
            reads=[accb, badaT], writes=[modT])
        P.op("dve", lambda e: e.tensor_scalar(out=wmT[:], in0=modT[:, 8:16, :], scalar1=1.0, scalar2=None,
                                              op0=ALU.add), reads=[modT], writes=[wmT])
        P.op("dve", lambda e: e.tensor_tensor(
            out=wmT[:], in0=wmT[:], in1=nwT[:, layer * 8:layer * 8 + 8].unsqueeze(2).broadcast_to([128, 8, 2]),
            op=ALU.mult), reads=[wmT, nwT], writes=[wmT])

    def norm_hT(xt, s, hT, xn, c0=0):
        sst = rot(ss)
        tb = rot(banks[6:8])
        P.op("act", lambda e: e.activation(out=xn[:], in_=xt[:], func=AF.Square, accum_out=sst[:, 0:1]),
             reads=[xt], writes=[xn, sst])
        P.op("act", lambda e: e.activation(out=sst[:, 1:2], in_=sst[:, 0:1], func=AF.Sqrt, scale=1.0 / D,
                                           bias=small[:, 0:1]), reads=[sst, small], writes=[sst])
        P.op("dve", lambda e: e.reciprocal(out=sst[:, 2:3], in_=sst[:, 1:2]), reads=[sst], writes=[sst])
        P.op("act", lambda e: e.activation(out=xn[:], in_=xt[:], func=AF.Identity, scale=sst[:, 2:3]),
             reads=[xt, sst], writes=[xn])
        tbb = tb[:, :].bitcast(BF16)
        for kt in range(8):
            P.op("pe", lambda e, kt=kt: e.transpose(out=tbb[:, kt * 128:(kt + 1) * 128],
                                                    in_=xn[:, kt * 128:(kt + 1) * 128], identity=ident[:]),
                 reads=[xn, ident], writes=[tb], flag=(kt == 7))
        for kt in range(8):
            if kt % 2 == 0:
                P.op("dve", lambda e, kt=kt: e.tensor_scalar(
                    out=hT[:, kt, c0:c0 + 128], in0=tbb[:, kt * 128:(kt + 1) * 128], scalar1=wmT[:, kt, s:s + 1],
                    scalar2=modT[:, kt, s:s + 1], op0=ALU.mult, op1=ALU.add),
                    reads=[tb, wmT, modT], writes=[hT])
            else:
                P.op("act", lambda e, kt=kt: e.activation(
                    out=hT[:, kt, c0:c0 + 128], in_=tbb[:, kt * 128:(kt + 1) * 128], func=AF.Identity,
                    scale=wmT[:, kt, s:s + 1], bias=modT[:, kt, s:s + 1]),
                    reads=[tb, wmT, modT], writes=[hT])

    P.op("dve", lambda e: e.memset(small[:, 0:1], EPS), writes=[small])

    if do0:
        o = 0
        def carve(name, shape, dtype):
            nonlocal o
            n = int(np.prod(shape[1:]))
            w = n if dtype == F32 else (n + 1) // 2
            t = A.carve(name, o, shape, dtype)
            o += w
            return t
        win0 = carve("win0", [128, 8, 2560], BF16)
        wout0 = carve("wout0", [128, 8, D], BF16)
        cosT = carve("cosT", [128, 18, 64], BF16)
        sinT = carve("sinT", [128, 18, 64], BF16)
        masks = carve("masks", [128, 4, 128], BF16)
        psw = carve("psw", [128, 128], BF16)
        esink = carve("esink", [128, 16], F32)
        gate_ctx = carve("gate_ctx", [128, D], F32)
        KTs = [carve(f"KT{i}", [128, 2, 128], BF16) for i in range(4)]
        Vs = [carve(f"V{i}", [128, 4, 65], BF16) for i in range(4)]
        KTc = [carve(f"KTc{i}", [128, 2, 128], BF16) for i in range(2)]
        Vc = [carve(f"Vc{i}", [128, 4, 65], BF16) for i in range(2)]
        o_scr = o
        wchunk = carve("wchunk", [128, 8, 512], BF16)
        brow = carve("brow", [128, D], F32)
        xh = [carve(f"xh{i}", [128, D], F32) for i in range(2)]
        screp = carve("screp", [128, 2, 8, 128], BF16)
        o_early_end = o
        o = o_scr
        W = {}
        W["QT"] = [carve(f"QT{i}", [128, 2, 4, 128], BF16) for i in range(2)]
        W["SZ"] = [carve(f"SZ{i}", [128, D], BF16) for i in range(2)]
        W["PT"] = [carve(f"PT{i}", [128, 512], BF16) for i in range(10)]
        o_t1 = o
        W["t1"] = carve("t1", [128, D], BF16)
        W["g"] = carve("g", [128, D], BF16)
        o_w = o
        o = o_t1
        W["ytmp"] = carve("ytmp", [128, D], F32)
        o = max(o, o_w)
        W["gT"] = carve("gT", [128, 8, 128], BF16)
        o = max(o, o_early_end)
        hTs = [carve(f"hT{i}", [128, 8, 128], BF16) for i in range(2)]
        xn0 = carve("xn0", [128, D], BF16)
        rA = carve("rA", [128, 512], F32)
        rB = carve("rB", [128, 512], F32)
        qrot = carve("qrot", [128, D], BF16)
        krot = carve("krot", [128, 256], BF16)
        lsum = carve("lsum", [128, 16], F32)
        rl = carve("rl", [128, 16], F32)
        KTr = carve("KTr", [128, 2, 128], BF16)
        Vr = carve("Vr", [128, 4, 65], BF16)
        print("layer0 arena words used:", o)

        make_screp(screp)

        adaln(0, wchunk, [(0, gate_lat), (1, gate_ctx)], brow, screp)
        for kt in range(8):
            rows = slice(kt * 128, (kt + 1) * 128)
            for pr in range(2):
                cast_rr["n"] += 0
            st_q = None
            st = stages[cast_rr["n"] % 2]
            cast_rr["n"] += 1
            P.dma("sp", lambda e, st=st, rows=rows: e.dma_start(out=st[:, 0:1280], in_=win0_d[rows, 0:1280]), writes=[st])
            for pr in range(2):
                eng = ("act", "dve")[pr]
                src = st[:, pr * 512:(pr + 1) * 512].rearrange("p (s g d) -> p g s d", s=2, g=4)
                dst = win0[:, kt, pr * 512:(pr + 1) * 512].rearrange("p (g s d) -> p g s d", g=4, s=2)
                if eng == "act":
                    P.op("act", lambda e, src=src, dst=dst: e.copy(out=dst, in_=src), reads=[st], writes=[win0])
                else:
                    P.op("dve", lambda e, src=src, dst=dst: e.tensor_copy(out=dst, in_=src), reads=[st], writes=[win0])
            P.op("pool", lambda e, st=st, kt=kt: e.tensor_copy(out=win0[:, kt, 1024:1280], in_=st[:, 1024:1280]),
                 reads=[st], writes=[win0])
            load_cast(win0, win0[:, kt, 1280:2560], win0_d[rows, 1280:2560], 1280)
        for kt in range(8):
            load_cast(wout0, wout0[:, kt, :], wout0_d[kt * 128:(kt + 1) * 128, :], 1024)
        P.dma("pool", lambda e: e.dma_start(out=cosT[:].rearrange("p a b -> p (a b)"), in_=cos_d), writes=[cosT])
        P.dma("pool", lambda e: e.dma_start(out=sinT[:].rearrange("p a b -> p (a b)"), in_=sin_d), writes=[sinT])
        P.dma("pool", lambda e: e.dma_start(out=masks[:].rearrange("p a b -> p (a b)"), in_=masks_d),
              writes=[masks])
        P.dma("pool", lambda e: e.dma_start(out=psw[:], in_=psw_d), writes=[psw])
        P.dma("sp", lambda e: e.dma_start(out=esink[:], in_=sink_d.partition_broadcast(128)), writes=[esink])
        P.op("act", lambda e: e.activation(out=esink[:], in_=esink[:], func=AF.Exp), reads=[esink], writes=[esink])
        for i in range(2):
            P.dma("sp", lambda e, i=i: e.dma_start(out=xh[i][:], in_=xh_d[i * 128:(i + 1) * 128, :]), writes=[xh[i]])
        for i in range(4):
            P.op("pool", lambda e, i=i: e.memset(Vs[i][:, :, 64:65], 1.0), writes=[Vs[i]])
        for i in range(2):
            P.op("pool", lambda e, i=i: e.memset(Vc[i][:, :, 64:65], 1.0), writes=[Vc[i]])


        def rope_tok(src, ncols, ti, dst_tile, dst_ap, rope):
            nh = ncols // 64
            bank = src[0]
            sap = src[1]
            if not rope:
                P.op("act", lambda e: e.copy(out=dst_ap, in_=sap), reads=[bank], writes=[dst_tile])
                return
            P.op("act", lambda e: e.copy(out=rA[:, 0:ncols], in_=sap), reads=[bank], writes=[rA])
            rA3 = rA[:, 0:ncols].rearrange("p (h d) -> p h d", d=64)
            rB3 = rB[:, 0:ncols].rearrange("p (h d) -> p h d", d=64)
            for a_ in range(2):
                for half in range(2):
                    oo = a_ * 32 + half * 16
                    io = a_ * 32 + (1 - half) * 16
                    P.op("dve", lambda e, oo=oo, io=io: e.tensor_tensor(
                        out=rB3[:, :, oo:oo + 16], in0=rA3[:, :, io:io + 16],
                        in1=sinT[:, ti:ti + 1, oo:oo + 16].broadcast_to([128, nh, 16]),
                        op=ALU.mult), reads=[rA, sinT], writes=[rB])
            P.op("dve", lambda e: e.tensor_tensor(out=rA3, in0=rA3, in1=cosT[:, ti:ti + 1, :].broadcast_to([128, nh, 64]),
                                                  op=ALU.mult), reads=[rA, cosT], writes=[rA])
            P.op("pool", lambda e: e.tensor_tensor(out=dst_ap, in0=rA[:, 0:ncols], in1=rB[:, 0:ncols], op=ALU.add),
                 reads=[rA, rB], writes=[dst_tile])

        def proj(xt, s, ti, KT, V, QT=None, SZ=None, rope=True):
            hT = rot(hTs)
            norm_hT(xt, s, hT, xn0)
            kvb = banks[2]
            for kt in range(8):
                P.op("pe", lambda e, kt=kt: e.matmul(kvb[:, :], lhsT=hT[:, kt, :], rhs=win0[:, kt, 1024:1536],
                                                     start=(kt == 0), stop=(kt == 7)),
                     reads=[win0, hT], writes=[kvb], flag=(kt == 7))
            if os.environ.get("V_ENG") == "dve":
                P.op("dve", lambda e: e.tensor_copy(out=V[:, :, 0:64], in_=kvb[:, 256:512].rearrange("p (k d) -> p k d", k=4)),
                     reads=[kvb], writes=[V])
            else:
                P.op("act", lambda e: e.copy(out=V[:, :, 0:64], in_=kvb[:, 256:512].rearrange("p (k d) -> p k d", k=4)),
                     reads=[kvb], writes=[V])
            rope_tok((kvb, kvb[:, 0:256]), 256, ti, krot, krot[:], rope)
            tb = rot(banks[6:8])
            tbb = tb[:, :].bitcast(BF16)
            for kt2 in range(2):
                P.op("pe", lambda e, kt2=kt2, tbb=tbb: e.transpose(out=tbb[:, kt2 * 128:(kt2 + 1) * 128],
                                                                   in_=krot[:, kt2 * 128:(kt2 + 1) * 128], identity=ident[:]),
                     reads=[krot, ident], writes=[tb], flag=(kt2 == 1))
            P.op("act", lambda e, tbb=tbb: e.copy(out=KT[:].rearrange("p a b -> p (a b)"), in_=tbb[:, 0:256]),
                 reads=[tb], writes=[KT])
            if QT is None:
                return
            for half in range(2):
                qb = banks[half]
                for kt in range(8):
                    P.op("pe", lambda e, kt=kt, half=half, qb=qb: e.matmul(
                        qb[:, :], lhsT=hT[:, kt, :], rhs=win0[:, kt, half * 512:(half + 1) * 512],
                        start=(kt == 0), stop=(kt == 7)), reads=[win0, hT], writes=[qb], flag=(kt == 7))
            for half in range(2):
                zb = banks[4 + half]
                for kt in range(8):
                    P.op("pe", lambda e, kt=kt, half=half, zb=zb: e.matmul(
                        zb[:, :], lhsT=hT[:, kt, :], rhs=win0[:, kt, 1536 + half * 512:1536 + (half + 1) * 512],
                        start=(kt == 0), stop=(kt == 7)), reads=[win0, hT], writes=[zb], flag=(kt == 7))
            for half in range(2):
                qb = banks[half]
                rope_tok((qb, qb[:, :]), 512, ti, qrot, qrot[:, half * 512:(half + 1) * 512], rope)
            for half in range(2):
                zb = banks[4 + half]
                P.op("act", lambda e, half=half, zb=zb: e.activation(
                    out=SZ[:, half * 512:(half + 1) * 512], in_=zb[:, :], func=AF.Silu), reads=[zb], writes=[SZ])
            tb = rot(banks[6:8])
            tbb = tb[:, :].bitcast(BF16)
            for qi in range(8):
                P.op("pe", lambda e, qi=qi, tbb=tbb: e.transpose(out=tbb[:, qi * 128:(qi + 1) * 128],
                                                                 in_=qrot[:, qi * 128:(qi + 1) * 128], identity=ident[:]),
                     reads=[qrot, ident], writes=[tb], flag=(qi == 7))
            P.op("act", lambda e, tbb=tbb: e.copy(out=QT[:].rearrange("p a b c -> p (a b c)"), in_=tbb),
                 reads=[tb], writes=[QT])

        def attn(W, QT, SZ, blocks, xt, gate, mask_of):
            PT = W["PT"]
            nb = len(blocks)
            Ob = banks[3:6]
            pts = {}

            def qk(k):
                rows = slice(0, 64) if k % 2 == 0 else slice(64, 128)
                lst = []
                for bi, (KT, V, mi) in enumerate(blocks):
                    sb = rot(banks[0:3])
                    pt = PT[(k % 2) * 5 + bi]
                    P.op("pe", lambda e, KT=KT, sb=sb, rows=rows, k=k: e.matmul(
                        sb[:, :], lhsT=KT[rows, k // 2, :], rhs=QT[rows, k // 2, :, :], start=True, stop=True),
                        reads=[KT, QT], writes=[sb])
                    P.op("act", lambda e, sb=sb, pt=pt: e.activation(out=pt[:], in_=sb[:, :], func=AF.Exp, scale=0.125),
                         reads=[sb], writes=[pt])
                    if mi is not None:
                        eng = "pool" if bi == 0 else "dve"
                        P.op(eng, lambda e, pt=pt, mi=mi: e.tensor_tensor(
                            out=pt[:].rearrange("p (g q) -> p g q", g=4), in0=pt[:].rearrange("p (g q) -> p g q", g=4),
                            in1=masks[:, mi:mi + 1, :].broadcast_to([128, 4, 128]), op=ALU.mult),
                            reads=[pt, masks], writes=[pt])
                    lst.append(pt)
                pts[k] = lst

            def pv(k):
                for gq in range(4):
                    h = 4 * k + gq
                    ob = Ob[h // 7]
                    c0 = (h % 7) * 65
                    for bi, (KT, V, mi) in enumerate(blocks):
                        pt = pts[k][bi]
                        P.op("pe", lambda e, pt=pt, V=V, gq=gq, ob=ob, c0=c0, bi=bi, k=k: e.matmul(
                            ob[:, c0:c0 + 65], lhsT=pt[:, gq * 128:(gq + 1) * 128], rhs=V[:, k, :],
                            start=(bi == 0), stop=(bi == nb - 1)), reads=[pt, V], writes=[ob], flag=(bi == nb - 1))

            qk(0)
            for k in range(4):
                if k + 1 < 4:
                    qk(k + 1)
                pv(k)
            return None

        def attn_mid(W, SZ):
            Ob = banks[3:6]
            for j in range(3):
                nh = 7 if j < 2 else 2
                P.op("dve", lambda e, j=j, nh=nh: e.tensor_tensor(
                    out=lsum[:, 7 * j:7 * j + nh],
                    in0=Ob[j][:, 0:nh * 65].rearrange("p (h e) -> p h e", e=65)[:, :, 64],
                    in1=esink[:, 7 * j:7 * j + nh], op=ALU.add), reads=[Ob[j], esink], writes=[lsum])
            P.op("dve", lambda e: e.reciprocal(out=rl[:], in_=lsum[:]), reads=[lsum], writes=[rl])
            t1, g = W["t1"], W["g"]
            for j in range(3):
                nh = 7 if j < 2 else 2
                P.op("dve", lambda e, j=j, nh=nh: e.tensor_tensor(
                    out=t1[:, 7 * j * 64:(7 * j + nh) * 64].rearrange("p (h d) -> p h d", d=64),
                    in0=Ob[j][:, 0:nh * 65].rearrange("p (h e) -> p h e", e=65)[:, :, 0:64],
                    in1=rl[:, 7 * j:7 * j + nh].unsqueeze(2).broadcast_to([128, nh, 64]), op=ALU.mult),
                    reads=[Ob[j], rl], writes=[t1])
            P.op("dve", lambda e: e.tensor_tensor(out=g[:], in0=t1[:], in1=SZ[:], op=ALU.mult),
                 reads=[t1, SZ], writes=[g])

        def attn_tail(W, xt, gate):
            g, gT, ytmp = W["g"], W["gT"], W["ytmp"]
            tb = rot(banks[6:8])
            tbb = tb[:, :].bitcast(BF16)
            for kt in range(8):
                P.op("pe", lambda e, kt=kt: e.transpose(out=tbb[:, kt * 128:(kt + 1) * 128],
                                                        in_=g[:, kt * 128:(kt + 1) * 128], identity=ident[:]),
                     reads=[g, ident], writes=[tb], flag=(kt == 7))
            P.op("act", lambda e: e.copy(out=gT[:].rearrange("p a b -> p (a b)"), in_=tbb), reads=[tb], writes=[gT])
            for half in range(2):
                yb = banks[half]
                for kt in range(8):
                    P.op("pe", lambda e, kt=kt, half=half, yb=yb: e.matmul(
                        yb[:, :], lhsT=gT[:, kt, :], rhs=wout0[:, kt, half * 512:(half + 1) * 512],
                        start=(kt == 0), stop=(kt == 7)), reads=[gT, wout0], writes=[yb], flag=(kt == 7))
                P.op("dve", lambda e, half=half, yb=yb: e.tensor_tensor(
                    out=ytmp[:, half * 512:(half + 1) * 512], in0=yb[:, :], in1=gate[:, half * 512:(half + 1) * 512],
                    op=ALU.mult), reads=[yb, gate], writes=[ytmp])
            P.op("dve", lambda e: e.tensor_tensor(out=xt[:], in0=xt[:], in1=ytmp[:], op=ALU.add),
                 reads=[xt, ytmp], writes=[xt])

        slot = lambda b: (b + 1) % 4
        proj(xh[0], 0, 0, KTs[slot(-1)], Vs[slot(-1)])
        P.op("pool", lambda e: e.memset(Vr[:, :, 64:65], 1.0), writes=[Vr])
        proj(xh[1], 0, 17, KTr, Vr)
        for i in range(2):
            proj(cx[i], 1, 0, KTc[i], Vc[i], QT=W["QT"][i], SZ=W["SZ"][i], rope=False)
        for i in range(2):
            attn(W, W["QT"][i], W["SZ"][i], [(KTc[0], Vc[0], None), (KTc[1], Vc[1], None)], cx[i], gate_ctx, None)
            attn_mid(W, W["SZ"][i])
            attn_tail(W, cx[i], gate_ctx)
        def blk(b):
            if b == 16:
                return KTr, Vr
            return KTs[slot(b)], Vs[slot(b)]
        def do_proj(n):
            proj(xs[n], 0, 1 + n, *blk(n), QT=W["QT"][n % 2], SZ=W["SZ"][n % 2])
        do_proj(0)
        for n in range(NT):
            if n + 1 < NT:
                do_proj(n + 1)
            kp, vp = blk(n - 1)
            ks, vs_ = blk(n)
            kn, vn = blk(n + 1)
            blocks = [(kp, vp, 2 if n == 0 else 0), (ks, vs_, None), (kn, vn, 3 if n == NT - 1 else 1),
                      (KTc[0], Vc[0], None), (KTc[1], Vc[1], None)]
            attn(W, W["QT"][n % 2], W["SZ"][n % 2], blocks, xs[n], gate_lat, None)
            attn_mid(W, W["SZ"][n % 2])
            attn_tail(W, xs[n], gate_lat)

    if do1:
        win1_d = din("ssm_w_in", [D, 2 * D])
        wglu_d = din("ssm_w_glu", [D, 2 * D])
        wout1_d = din("ssm_w_out", [D, D])
        fnw_d = din("fnw", [1, D])
        lamR_d = din("lamR", [128, 128])
        lamI_d = din("lamI", [128, 128])
        logdt_d = din("logdt", [1, 128])
        bA_d = din("bA", [128, 2048])
        bS_d = din("bS", [128, 2048])
        cA_d = din("cA", [128, 2048])
        cS_d = din("cS", [128, 2048])
        dT_d = din("dT", [128, 64])
        mFB_d = din("mFB", [128, 256])
        idst_d = din("idst", [128, 64])
        sgn_d = din("sgn", [128, 8])
        out_d = dout("out", [TOK, D])
        Ud = P.dram("Ud", [128, 64, 288], BF16)
        Yd = P.dram("Yd", [128, 64, 256], BF16)
        Ydg = [Tile(f"Yd{g}", Yd.t.ap()[:, g, :]) for g in range(64)]
        Ein = P.dram("Ein", [128, 64], F32)
        Eall = P.dram("Eall", [256, 64], F32)
        A.carves = [(lo, hi, t) for (lo, hi, t) in A.carves]
        o = 0

        def carve(name, shape, dtype):
            nonlocal o
            n = int(np.prod(shape[1:]))
            w = n if dtype == F32 else (n + 1) // 2
            t = A.carve(name, o, shape, dtype)
            o += w
            return t

        def TT(eng, O, A_, B_, op):
            P.op(eng, lambda e: e.tensor_tensor(out=O[1], in0=A_[1], in1=B_[1], op=op), reads=[A_[0], B_[0]], writes=[O[0]])

        def TS(eng, O, A_, s1, op0, s2=None, op1=None, extra=()):
            if op1 is None:
                P.op(eng, lambda e: e.tensor_scalar(out=O[1], in0=A_[1], scalar1=s1, scalar2=None, op0=op0),
                     reads=[A_[0]] + list(extra), writes=[O[0]])
            else:
                P.op(eng, lambda e: e.tensor_scalar(out=O[1], in0=A_[1], scalar1=s1, scalar2=s2, op0=op0, op1=op1),
                     reads=[A_[0]] + list(extra), writes=[O[0]])

        def STT(eng, O, A_, sc, B_, op0, op1, extra=()):
            P.op(eng, lambda e: e.scalar_tensor_tensor(out=O[1], in0=A_[1], scalar=sc, in1=B_[1], op0=op0, op1=op1),
                 reads=[A_[0], B_[0]] + list(extra), writes=[O[0]])

        def ACTF(O, A_, func, scale=None, bias=None, extra=()):
            kw = {}
            if scale is not None:
                kw["scale"] = scale
            if bias is not None:
                kw["bias"] = bias
            P.op("act", lambda e: e.activation(out=O[1], in_=A_[1], func=func, **kw), reads=[A_[0]] + list(extra), writes=[O[0]])

        def F(t):
            return (t, t[:])

        win1u = carve("win1u", [128, 8, D], BF16)
        uTs = carve("uTs", [128, 8, 8, 288], BF16)
        wchunk1 = carve("wchunk1", [128, 8, 512], BF16)
        brow1 = carve("brow1", [128, D], F32)
        screp1 = carve("screp1", [128, 2, 8, 128], BF16)
        hT1 = [carve(f"hT1_{i}", [128, 8, 512], BF16) for i in range(2)]
        xn1 = carve("xn1", [128, D], BF16)
        o_p1 = o
        for kt in range(8):
            load_cast(win1u, win1u[:, kt, :], win1_d[kt * 128:(kt + 1) * 128, 0:D], 1024)
        make_screp(screp1)
        adaln(1, wchunk1, [(0, gate_lat)], brow1, screp1)
        tiles1 = [(cx[0], 1), (cx[1], 1)] + [(xs[n], 0) for n in range(NT)]
        for t0 in range(0, len(tiles1), 4):
            grp = tiles1[t0:t0 + 4]
            nt = len(grp)
            hT = rot(hT1)
            for ti, (xt, sidx) in enumerate(grp):
                norm_hT(xt, sidx, hT, xn1, c0=ti * 128)
            j0 = t0 * 16
            for ft in range(8):
                acc = rot(banks[0:4])
                for kt in range(8):
                    P.op("pe", lambda e, kt=kt, ft=ft, acc=acc, hT=hT, nt=nt: e.matmul(
                        acc[:, 0:nt * 128], lhsT=win1u[:, kt, ft * 128:(ft + 1) * 128], rhs=hT[:, kt, 0:nt * 128],
                        start=(kt == 0), stop=(kt == 7)), reads=[win1u, hT], writes=[acc], flag=(kt == 7))
                if ft % 2 == 0:
                    P.op("act", lambda e, ft=ft, acc=acc, j0=j0, nt=nt: e.copy(
                        out=uTs[:, ft, :, j0:j0 + 16 * nt], in_=acc[:, 0:nt * 128].rearrange("p (j s) -> p s j", s=8)),
                        reads=[acc], writes=[uTs])
                else:
                    P.op("dve", lambda e, ft=ft, acc=acc, j0=j0, nt=nt: e.tensor_copy(
                        out=uTs[:, ft, :, j0:j0 + 16 * nt], in_=acc[:, 0:nt * 128].rearrange("p (j s) -> p s j", s=8)),
                        reads=[acc], writes=[uTs])
        for g in range(64):
            ft, gl = g // 8, g % 8
            P.dma("sp", lambda e, g=g, ft=ft, gl=gl: e.dma_start(
                out=Ud.t.ap()[:, g, :].rearrange("(s h) j -> h s j", h=16),
                in_=uTs[gl * 16:(gl + 1) * 16, ft, :, :]), reads=[uTs], writes=[Ud], semtile=uTs)

        o = 0
        NCG = 128
        def sm(name, k=1):
            return carve(name, [128, k, NCG] if k > 1 else [128, NCG], F32)
        COLA = sm("COLA", 9); COLB = sm("COLB", 9)
        BPR = carve("BPR", [128, 2, 64, 8], F32); BPI = carve("BPI", [128, 2, 64, 8], F32)
        CPR = carve("CPR", [128, 2, 64, 8], F32); CPI = carve("CPI", [128, 2, 64, 8], F32)
        bbA = carve("bbA", [128, NCG, 16], BF16); bbS = carve("bbS", [128, NCG, 16], BF16)
        ccA = carve("ccA", [128, NCG, 16], BF16); ccS = carve("ccS", [128, NCG, 16], BF16)
        dTt = carve("dTt", [128, 64], F32)
        mFB = carve("mFB", [128, 2, 128], F32)
        identF = carve("identF", [128, 128], F32)
        idst = carve("idst", [128, 64], F32)
        sgn = carve("sgn", [128, 8], F32)
        Eo = carve("Eo", [128, 64], F32)
        Ei = carve("Ei", [128, 64], F32)
        o_tab = o
        LR = sm("LR"); LI = sm("LI"); DT = sm("DT")
        PR = sm("PR", 9); PI = sm("PI", 9); NR = sm("NR", 8); NI = sm("NI", 8)
        QR = sm("QR", 9); QI = sm("QI", 9)
        tmp = [sm(f"tmp{i}") for i in range(8)]
        bAr = carve("bAr", [128, NCG, 16], F32); bSr = carve("bSr", [128, NCG, 16], F32)
        big1 = carve("big1", [128, NCG, 16], F32); big2 = carve("big2", [128, NCG, 16], F32)
        o = o_tab
        NB = 8
        Bs_ = [carve(f"Bs{i}", [128, 8, 16], BF16) for i in range(NB)]
        Ct_ = [carve(f"Ct{i}", [128, 8, 16], BF16) for i in range(NB)]
        BsT_ = [carve(f"BsT{i}", [128, 128], BF16) for i in range(NB)]
        Mg_ = [carve(f"Mg{i}", [128, 128], BF16) for i in range(NB)]
        g1_ = [carve(f"g1{i}", [128, 8, 16], F32) for i in range(NB)]
        g2_ = [carve(f"g2{i}", [128, 8, 16], F32) for i in range(NB)]
        Zs_ = [carve(f"Zs{i}", [128, 290], BF16) for i in range(NB)]
        S_ = [carve(f"S{i}", [128, 290], BF16) for i in range(NB)]
        Pp_ = [carve(f"Pp{i}", [128, 290], BF16) for i in range(NB)]
        R_ = [carve(f"R{i}", [128, 9, 128], BF16) for i in range(NB)]
        Yb_ = [carve(f"Yb{i}", [128, 256], BF16) for i in range(NB)]
        Yp_ = [carve(f"Yp{i}", [128, 256], BF16) for i in range(NB)]
        Ub_ = [carve(f"Ub{i}", [128, 8, 288], BF16) for i in range(2)]
        print("layer1 phase2 arena words:", o)
        o_p2 = o

        for t_, d_ in ((LR, lamR_d), (LI, lamI_d), (dTt, dT_d), (idst, idst_d), (sgn, sgn_d)):
            P.dma("sp", lambda e, t_=t_, d_=d_: e.dma_start(out=t_[:], in_=d_), writes=[t_])
        P.dma("sp", lambda e: e.dma_start(out=mFB[:].rearrange("p a b -> p (a b)"), in_=mFB_d), writes=[mFB])
        P.dma("sp", lambda e: e.dma_start(out=identF[:], in_=ident_d), writes=[identF])
        P.dma("sp", lambda e: e.dma_start(out=DT[:], in_=logdt_d.partition_broadcast(128)), writes=[DT])
        P.dma("sp", lambda e: e.dma_start(out=bAr[:].rearrange("p a b -> p (a b)"), in_=bA_d), writes=[bAr])
        P.dma("sp", lambda e: e.dma_start(out=bSr[:].rearrange("p a b -> p (a b)"), in_=bS_d), writes=[bSr])
        P.dma("pool", lambda e: e.dma_start(out=ccA[:].rearrange("p a b -> p (a b)"), in_=cA_d), writes=[ccA])
        P.dma("pool", lambda e: e.dma_start(out=ccS[:].rearrange("p a b -> p (a b)"), in_=cS_d), writes=[ccS])

        MUL, ADD, SUB = ALU.mult, ALU.add, ALU.subtract
        T0, T1, T2, T3, T4, T5, T6, T7 = tmp

        def cmul(outr, outi, ar_, ai_, br_, bi_):
            TT("dve", F(T6), ar_, br_, MUL)
            TT("pool", F(T7), ai_, bi_, MUL)
            TT("dve", outr, F(T6), F(T7), SUB)
            TT("dve", F(T6), ar_, bi_, MUL)
            TT("pool", F(T7), ai_, br_, MUL)
            TT("dve", outi, F(T6), F(T7), ADD)

        ACTF(F(DT), F(DT), AF.Exp)
        TT("dve", F(T0), F(LR), F(DT), MUL)
        TT("dve", F(T1), F(LI), F(DT), MUL)
        ACTF(F(T2), F(T0), AF.Exp, scale=0.125)
        ACTF(F(T3), F(T1), AF.Sin, scale=0.125)
        ACTF(F(T4), F(T1), AF.Sin, scale=0.0625)
        TT("dve", F(T4), F(T4), F(T4), MUL)
        TS("dve", F(T4), F(T4), -2.0, MUL, 1.0, ADD)
        zr, zi = (PR, PR[:, 1, :]), (PI, PI[:, 1, :])
        TT("dve", F(T0), F(T2), F(T4), MUL)
        TT("dve", F(T1), F(T2), F(T3), MUL)
        cur = (F(T0), F(T1))
        for it in range(3):
            dst = (zr, zi) if it == 2 else (F(T2), F(T3)) if it == 0 else (F(T4), F(T5))
            cmul(dst[0], dst[1], cur[0], cur[1], cur[0], cur[1])
            cur = dst
        P.op("dve", lambda e: e.memset(PR[:, 0, :], 1.0), writes=[PR])
        P.op("dve", lambda e: e.memset(PI[:, 0, :], 0.0), writes=[PI])
        P.op("dve", lambda e: e.memset(NR[:, 0, :], 1.0), writes=[NR])
        P.op("dve", lambda e: e.memset(NI[:, 0, :], 0.0), writes=[NI])
        for k in range(2, 9):
            cmul((PR, PR[:, k, :]), (PI, PI[:, k, :]), (PR, PR[:, k - 1, :]), (PI, PI[:, k - 1, :]), zr, zi)
        TT("dve", F(T0), zr, zr, MUL)
        TT("dve", F(T1), zi, zi, MUL)
        TT("dve", F(T0), F(T0), F(T1), ADD)
        P.op("dve", lambda e: e.reciprocal(out=T0[:], in_=T0[:]), reads=[T0], writes=[T0])
        TT("dve", (NR, NR[:, 1, :]), zr, F(T0), MUL)
        TT("dve", F(T1), zi, F(T0), MUL)
        TS("dve", (NI, NI[:, 1, :]), F(T1), -1.0, MUL)
        for k in range(2, 8):
            cmul((NR, NR[:, k, :]), (NI, NI[:, k, :]), (NR, NR[:, k - 1, :]), (NI, NI[:, k - 1, :]),
                 (NR, NR[:, 1, :]), (NI, NI[:, 1, :]))
        P.op("dve", lambda e: e.tensor_copy(out=QR[:, 0, :], in_=PR[:, 8, :]), reads=[PR], writes=[QR])
        P.op("dve", lambda e: e.tensor_copy(out=QI[:, 0, :], in_=PI[:, 8, :]), reads=[PI], writes=[QI])
        for l in range(1, 9):
            cmul((QR, QR[:, l, :]), (QI, QI[:, l, :]), (QR, QR[:, l - 1, :]), (QI, QI[:, l - 1, :]),
                 (QR, QR[:, l - 1, :]), (QI, QI[:, l - 1, :]))
        for l in range(9):
            TS("dve", F(T0), (QR, QR[:, l, :]), sgn[:, 2:3], MUL, extra=[sgn])
            STT("dve", (COLA, COLA[:, l, :]), (QI, QI[:, l, :]), sgn[:, 4:5], F(T0), MUL, ADD, extra=[sgn])
            TS("dve", F(T1), (QI, QI[:, l, :]), sgn[:, 2:3], MUL, extra=[sgn])
            STT("dve", (COLB, COLB[:, l, :]), (QR, QR[:, l, :]), sgn[:, 3:4], F(T1), MUL, ADD, extra=[sgn])
        TS("dve", F(T0), zr, -1.0, ADD)
        TT("dve", F(T1), F(LR), F(LR), MUL)
        TT("dve", F(T2), F(LI), F(LI), MUL)
        TT("dve", F(T1), F(T1), F(T2), ADD)
        P.op("dve", lambda e: e.reciprocal(out=T1[:], in_=T1[:]), reads=[T1], writes=[T1])
        TT("dve", F(T2), F(T0), F(LR), MUL)
        TT("dve", F(T3), zi, F(LI), MUL)
        TT("dve", F(T2), F(T2), F(T3), ADD)
        TT("dve", F(T2), F(T2), F(T1), MUL)
        TT("dve", F(T3), zi, F(LR), MUL)
        TT("dve", F(T4), F(T0), F(LI), MUL)
        TT("dve", F(T3), F(T3), F(T4), SUB)
        TT("dve", F(T3), F(T3), F(T1), MUL)
        TS("dve", F(T4), F(T3), sgn[:, 0:1], MUL, extra=[sgn])
        TS("dve", F(T5), F(T3), sgn[:, 1:2], MUL, extra=[sgn])
        def bc16(t):
            return (t, t[:].unsqueeze(2).broadcast_to([128, NCG, 16]))
        TT("dve", F(big1), F(bAr), bc16(T2), MUL)
        TT("pool", F(big2), F(bSr), bc16(T4), MUL)
        TT("dve", F(bbA), F(big1), F(big2), ADD)
        TT("dve", F(big1), F(bSr), bc16(T2), MUL)
        TT("pool", F(big2), F(bAr), bc16(T5), MUL)
        TT("dve", F(bbS), F(big1), F(big2), ADD)
        for c in range(2):
            cs = slice(c * 64, (c + 1) * 64)
            for k in range(8):
                ks = 7 - k if c == 0 else k
                P.op("dve", lambda e, cs=cs, c=c, k=k, ks=ks: e.tensor_copy(
                    out=BPR[:, c, :, k], in_=PR[:, ks, cs]), reads=[PR], writes=[BPR])
                P.op("dve", lambda e, cs=cs, c=c, k=k, ks=ks: e.tensor_scalar(
                    out=BPI[:, c, :, k], in0=PI[:, ks, cs], scalar1=sgn[:, 0:1], scalar2=None, op0=MUL),
                    reads=[PI, sgn], writes=[BPI])
                P.op("dve", lambda e, cs=cs, c=c, k=k, ks=ks: e.tensor_scalar(
                    out=CPR[:, c, :, k], in0=NR[:, ks, cs], scalar1=sgn[:, 1:2], scalar2=None, op0=MUL),
                    reads=[NR, sgn], writes=[CPR])
                P.op("dve", lambda e, cs=cs, c=c, k=k, ks=ks: e.tensor_scalar(
                    out=CPI[:, c, :, k], in0=NI[:, ks, cs], scalar1=-1.0, scalar2=None, op0=MUL),
                    reads=[NI], writes=[CPI])

        def prep(c, g, Ub, gi, k):
            cg = c * 64 + g
            Bs, Ct, BsT, Mg, g1, g2, Zs, R = Bs_[k], Ct_[k], BsT_[k], Mg_[k], g1_[k], g2_[k], Zs_[k], R_[k]
            e1, e2 = ("dve", "pool") if k % 2 == 0 else ("pool", "dve")
            P.op(e1, lambda e: e.tensor_tensor(out=g1[:], in0=BPR[:, c, g, :].unsqueeze(2).broadcast_to([128, 8, 16]),
                                               in1=bbA[:, cg, :].unsqueeze(1).broadcast_to([128, 8, 16]), op=MUL),
                 reads=[BPR, bbA], writes=[g1])
            P.op(e2, lambda e: e.tensor_tensor(out=g2[:], in0=BPI[:, c, g, :].unsqueeze(2).broadcast_to([128, 8, 16]),
                                               in1=bbS[:, cg, :].unsqueeze(1).broadcast_to([128, 8, 16]), op=MUL),
                 reads=[BPI, bbS], writes=[g2])
            P.op(e1, lambda e: e.tensor_tensor(out=Bs[:], in0=g1[:], in1=g2[:], op=ADD), reads=[g1, g2], writes=[Bs])
            P.op(e2, lambda e: e.tensor_tensor(out=g1[:], in0=CPR[:, c, g, :].unsqueeze(2).broadcast_to([128, 8, 16]),
                                               in1=ccA[:, cg, :].unsqueeze(1).broadcast_to([128, 8, 16]), op=MUL),
                 reads=[CPR, ccA], writes=[g1])
            P.op(e1, lambda e: e.tensor_tensor(out=g2[:], in0=CPI[:, c, g, :].unsqueeze(2).broadcast_to([128, 8, 16]),
                                               in1=ccS[:, cg, :].unsqueeze(1).broadcast_to([128, 8, 16]), op=MUL),
                 reads=[CPI, ccS], writes=[g2])
            P.op(e2, lambda e: e.tensor_tensor(out=Ct[:], in0=g1[:], in1=g2[:], op=ADD), reads=[g1, g2], writes=[Ct])
            P.op("dve", lambda e: e.tensor_tensor(
                out=R[:, :, 0:64], in0=idst[:].unsqueeze(1).broadcast_to([128, 9, 64]),
                in1=COLA[:, :, cg:cg + 1].broadcast_to([128, 9, 64]), op=MUL), reads=[idst, COLA], writes=[R])
            P.op("pool", lambda e: e.tensor_tensor(
                out=R[:, :, 64:128], in0=idst[:].unsqueeze(1).broadcast_to([128, 9, 64]),
                in1=COLB[:, :, cg:cg + 1].broadcast_to([128, 9, 64]), op=MUL), reads=[idst, COLB], writes=[R])
            Bs2 = Bs[:].rearrange("p a b -> p (a b)")
            Ct2 = Ct[:].rearrange("p a b -> p (a b)")
            tb = rot(banks[6:8])
            tbb = tb[:, :].bitcast(BF16)
            P.op("pe", lambda e: e.transpose(out=tbb[:, 0:128], in_=Bs2, identity=ident[:]), reads=[Bs, ident], writes=[tb])
            P.op("act", lambda e: e.copy(out=BsT[:], in_=tbb[:, 0:128]), reads=[tb], writes=[BsT])
            mb = banks[5]
            P.op("pe", lambda e: e.matmul(mb[:, 0:128], lhsT=Bs2, rhs=Ct2, start=True, stop=True), reads=[Bs, Ct], writes=[mb])
            if c == 0:
                P.op("dve", lambda e: e.tensor_tensor(out=Mg[:], in0=mb[:, 0:128], in1=mFB[:, 0, :], op=MUL),
                     reads=[mb, mFB], writes=[Mg])
            else:
                P.op("dve", lambda e: e.tensor_tensor(out=g1[:].rearrange("p a b -> p (a b)"), in0=mb[:, 0:128],
                                                      in1=mFB[:, 1, :], op=MUL), reads=[mb, mFB], writes=[g1])
                P.op("dve", lambda e: e.scalar_tensor_tensor(out=Mg[:], in0=identF[:], scalar=dTt[:, g:g + 1],
                                                             in1=g1[:].rearrange("p a b -> p (a b)"), op0=MUL, op1=ADD),
                     reads=[identF, dTt, g1], writes=[Mg])
            zb = rot(banks[0:2])
            ncol = 288 if c == 0 else 256
            ucols = slice(0, 288) if c == 0 else slice(32, 288)
            P.op("pe", lambda e: e.matmul(zb[:, 0:ncol], lhsT=BsT[:], rhs=Ub[:, gi, ucols], start=True, stop=True),
                 reads=[BsT, Ub], writes=[zb])
            P.op("act", lambda e: e.copy(out=Zs[:, 0:ncol], in_=zb[:, 0:ncol]), reads=[zb], writes=[Zs])
            if c == 1:
                P.op("act", lambda e: e.copy(out=Zs[:, 256:257], in_=Ei[:, g:g + 1]), reads=[Ei], writes=[Zs])
                P.dma("sp", lambda e: e.dma_start(out=Yp_[k][:], in_=Ydg[g][:]), reads=[Ydg[g]], writes=[Yp_[k]])

        def scan(c, ks):
            n = 288 if c == 0 else 257
            for l in range(9):
                sh = 1 << l
                if sh >= n:
                    break
                use_act = (l % 2 == 1)
                sbs = {}
                for k in ks:
                    Zs, S, R = Zs_[k], S_[k], R_[k]
                    src = Zs if l == 0 else S
                    sb = rot(banks[1:5])
                    sbs[k] = sb
                    rng = slice(0, n - sh) if c == 0 else slice(sh, n)
                    orng = slice(sh, n) if c == 0 else slice(0, n - sh)
                    if use_act:
                        P.op("pe", lambda e, src=src, sb=sb: e.matmul(
                            sb[:, 0:n], lhsT=ident[:], rhs=src[:, 0:n], start=True, stop=False),
                            reads=[ident, src], writes=[sb], flag=False)
                        P.op("pe", lambda e, l=l, src=src, sb=sb, R=R, rng=rng, orng=orng: e.matmul(
                            sb[:, orng], lhsT=R[:, l, :], rhs=src[:, rng], start=False, stop=True),
                            reads=[R, src], writes=[sb])
                    else:
                        P.op("pe", lambda e, l=l, sh=sh, src=src, sb=sb, R=R, rng=rng: e.matmul(
                            sb[:, 0:n - sh], lhsT=R[:, l, :], rhs=src[:, rng], start=True, stop=True),
                            reads=[R, src], writes=[sb])
                for k in ks:
                    Zs, S = Zs_[k], S_[k]
                    src = Zs if l == 0 else S
                    sb = sbs[k]
                    orng = slice(sh, n) if c == 0 else slice(0, n - sh)
                    if use_act:
                        P.op("act", lambda e, sb=sb, S=S: e.copy(out=S[:, 0:n], in_=sb[:, 0:n]), reads=[sb], writes=[S])
                    else:
                        P.op("dve", lambda e, sh=sh, src=src, sb=sb, S=S, orng=orng: e.tensor_tensor(
                            out=S[:, orng], in0=sb[:, 0:n - sh], in1=src[:, orng], op=ADD), reads=[sb, src], writes=[S])
                        if l == 0:
                            edge = slice(0, 1) if c == 0 else slice(n - 1, n)
                            P.op("act", lambda e, S=S, Zs=Zs, edge=edge: e.copy(out=S[:, edge], in_=Zs[:, edge]),
                                 reads=[Zs], writes=[S])

        def finish(c, g, Ub, gi, k):
            Ct, Mg, Zs, S, Pp, Yb = Ct_[k], Mg_[k], Zs_[k], S_[k], Pp_[k], Yb_[k]
            n = 288 if c == 0 else 257
            Ct2 = Ct[:].rearrange("p a b -> p (a b)")
            P.op("pool", lambda e: e.tensor_tensor(out=Pp[:, 0:n], in0=S[:, 0:n], in1=Zs[:, 0:n], op=SUB),
                 reads=[S, Zs], writes=[Pp])
            if c == 0:
                P.op("act", lambda e: e.copy(out=Eo[:, g:g + 1], in_=S[:, 287:288]), reads=[S], writes=[Eo])
            yb = rot(banks[5:6] + banks[0:1])
            pc = slice(32, 288) if c == 0 else slice(0, 256)
            P.op("pe", lambda e: e.matmul(yb[:, 0:256], lhsT=Mg[:], rhs=Ub[:, gi, 32:288], start=True, stop=False),
                 reads=[Mg, Ub], writes=[yb], flag=False)
            P.op("pe", lambda e: e.matmul(yb[:, 0:256], lhsT=Ct2, rhs=Pp[:, pc], start=False, stop=True),
                 reads=[Ct, Pp], writes=[yb])
            if c == 0:
                P.op("act", lambda e: e.copy(out=Yb[:], in_=yb[:, 0:256]), reads=[yb], writes=[Yb])
            else:
                P.op("dve", lambda e: e.tensor_tensor(out=Yb[:], in0=yb[:, 0:256], in1=Yp_[k][:], op=ADD),
                     reads=[yb, Yp_[k]], writes=[Yb])
            P.dma("sp", lambda e: e.dma_start(out=Ydg[g][:], in_=Yb[:]), reads=[Yb], writes=[Ydg[g]], semtile=Yb)

        UBS = 4
        for c in range(2):
            if c == 1:
                P.dma("sp", lambda e: e.dma_start(out=Ein.t.ap(), in_=Eo[:]), reads=[Eo], writes=[Ein], semtile=Eo)
                P.dma("pool", lambda e: e.collective_compute("AllGather", ALU.bypass, [[0, 1], [2, 3], [4, 5], [6, 7]],
                                                             ins=[Ein.t.ap()], outs=[Eall.t.ap()]),
                      reads=[Ein], writes=[Eall], semtile=Eall, inc=1)
                Et = g1_[0]
                P.dma("sp", lambda e: e.dma_start(out=Et[:].rearrange("p a b -> p (a b)").rearrange("p (r f) -> p r f", r=2),
                                                  in_=Eall.t.ap().rearrange("(r p) f -> p r f", p=128)),
                      reads=[Eall], writes=[Et])
                Et2 = Et[:].rearrange("p a b -> p (a b)")
                P.op("dve", lambda e: e.tensor_scalar(out=Ei[:], in0=Et2[:, 0:64], scalar1=sgn[:, 6:7], scalar2=None, op0=MUL),
                     reads=[Et, sgn], writes=[Ei])
                P.op("dve", lambda e: e.scalar_tensor_tensor(out=Ei[:], in0=Et2[:, 64:128], scalar=sgn[:, 7:8], in1=Ei[:],
                                                             op0=MUL, op1=ADD), reads=[Et, sgn, Ei], writes=[Ei])
            batches = []
            for g0 in range(0, 64, 8):
                for h2 in range(2):
                    batches.append((g0, h2))
            Ubs = {}
            def get_ub(g0):
                if g0 not in Ubs:
                    Ub = Ub_[(g0 // 8) % 2]
                    P.dma("sp", lambda e, g0=g0, Ub=Ub: e.dma_start(out=Ub[:], in_=Ud.t.ap()[:, g0:g0 + 8, :]),
                          reads=[Ud], writes=[Ub])
                    Ubs[g0] = Ub
                return Ubs[g0]
            def do_prep(bi):
                g0, h2 = batches[bi]
                Ub = get_ub(g0)
                for u in range(UBS):
                    gi = h2 * UBS + u
                    prep(c, g0 + gi, Ub, gi, (bi % 2) * UBS + u)
            do_prep(0)
            for bi in range(len(batches)):
                if bi + 1 < len(batches):
                    do_prep(bi + 1)
                g0, h2 = batches[bi]
                ks = [(bi % 2) * UBS + u for u in range(UBS)]
                scan(c, ks)
                for u in range(UBS):
                    gi = h2 * UBS + u
                    finish(c, g0 + gi, Ubs[g0], gi, (bi % 2) * UBS + u)

        o = 0
        wglu = carve("wglu", [128, 8, 2 * D], BF16)
        win1z = carve("win1z", [128, 8, D], BF16)
        wout1 = carve("wout1", [128, 8, D], BF16)
        yTs = [carve(f"yTs{ft}", [128, 8, 128], BF16) for ft in range(8)]
        fnw = carve("fnw", [128, D], F32)
        hT3 = [carve(f"hT3_{i}", [128, 8, 128], BF16) for i in range(2)]
        xn3 = carve("xn3", [128, D], BF16)
        gyT = [carve(f"gyT{i}", [128, 8, 128], BF16) for i in range(2)]
        sg = carve("sg", [128, D], BF16)
        sz = carve("sz", [128, D], BF16)
        tm = carve("tm", [128, D], BF16)
        mm = carve("mm", [128, D], BF16)
        mT = carve("mT", [128, 8, 128], BF16)
        ytmp3 = carve("ytmp3", [128, D], F32)
        outv = [ytmp3]
        print("layer1 phase3 arena words:", o)
        for kt in range(8):
            load_cast(win1z, win1z[:, kt, :], win1_d[kt * 128:(kt + 1) * 128, D:2 * D], 1024)
        for kt in range(8):
            load_cast(wglu, wglu[:, kt, 0:1024], wglu_d[kt * 128:(kt + 1) * 128, 0:1024], 1024)
            load_cast(wglu, wglu[:, kt, 1024:2048], wglu_d[kt * 128:(kt + 1) * 128, 1024:2048], 1024)
        for kt in range(8):
            load_cast(wout1, wout1[:, kt, :], wout1_d[kt * 128:(kt + 1) * 128, :], 1024)
        P.dma("sp", lambda e: e.dma_start(out=fnw[:], in_=fnw_d.partition_broadcast(128)), writes=[fnw])
        gys, hTs3 = {}, {}

        def pre3(n):
            if n % 8 == 0:
                hh = n // 8
                for g in range(64):
                    ft, gl = g // 8, g % 8
                    P.dma("sp", lambda e, g=g, ft=ft, gl=gl, hh=hh: e.dma_start(
                        out=yTs[ft][gl * 16:(gl + 1) * 16, :, :],
                        in_=Ydg[g][:, hh * 128:(hh + 1) * 128].rearrange("(t h) j -> h t j", h=16)),
                        reads=[Ydg[g]], writes=[yTs[ft]])
            gy = gyT[n % 2]
            for ft in range(8):
                P.op("act", lambda e, ft=ft, gy=gy, n=n: e.activation(
                    out=gy[:, ft, :].rearrange("p (j t) -> p j t", t=8),
                    in_=yTs[ft][:, :, (n % 8) * 16:(n % 8 + 1) * 16].rearrange("p t j -> p j t"), func=AF.Gelu),
                    reads=[yTs[ft]], writes=[gy])
            hT = hT3[n % 2]
            norm_hT(xs[n], 0, hT, xn3)
            gys[n], hTs3[n] = gy, hT

        pre3(0)
        for n in range(NT):
            xt = xs[n]
            gy, hT = gys[n], hTs3[n]
            for half in range(2):
                zb = banks[half]
                for kt in range(8):
                    P.op("pe", lambda e, kt=kt, half=half, zb=zb, hT=hT: e.matmul(
                        zb[:, :], lhsT=hT[:, kt, :], rhs=win1z[:, kt, half * 512:(half + 1) * 512],
                        start=(kt == 0), stop=(kt == 7)), reads=[win1z, hT], writes=[zb], flag=(kt == 7))
                P.op("act", lambda e, half=half, zb=zb: e.activation(out=sz[:, half * 512:(half + 1) * 512], in_=zb[:, :],
                                                                     func=AF.Silu), reads=[zb], writes=[sz])
            for q4 in range(4):
                gb = banks[2 + q4]
                for kt in range(8):
                    P.op("pe", lambda e, kt=kt, q4=q4, gb=gb, gy=gy: e.matmul(
                        gb[:, :], lhsT=gy[:, kt, :], rhs=wglu[:, kt, q4 * 512:(q4 + 1) * 512],
                        start=(kt == 0), stop=(kt == 7)), reads=[wglu, gy], writes=[gb], flag=(kt == 7))
            if n + 1 < NT and n % 8 != 7:
                pre3(n + 1)
            for half in range(2):
                P.op("act", lambda e, half=half: e.activation(out=sg[:, half * 512:(half + 1) * 512], in_=banks[4 + half][:, :],
                                                              func=AF.Sigmoid), reads=[banks[4 + half]], writes=[sg])
                P.op("dve", lambda e, half=half: e.tensor_tensor(out=tm[:, half * 512:(half + 1) * 512], in0=banks[2 + half][:, :],
                                                                 in1=sg[:, half * 512:(half + 1) * 512], op=ALU.mult),
                     reads=[banks[2 + half], sg], writes=[tm])
            P.op("dve", lambda e: e.tensor_tensor(out=mm[:], in0=tm[:], in1=sz[:], op=ALU.mult), reads=[tm, sz], writes=[mm])
            tb = banks[6]
            tbb = tb[:, :].bitcast(BF16)
            for kt in range(8):
                P.op("pe", lambda e, kt=kt, tbb=tbb: e.transpose(out=tbb[:, kt * 128:(kt + 1) * 128],
                                                                 in_=mm[:, kt * 128:(kt + 1) * 128], identity=ident[:]),
                     reads=[mm, ident], writes=[tb], flag=(kt == 7))
            P.op("act", lambda e, tbb=tbb: e.copy(out=mT[:].rearrange("p a b -> p (a b)"), in_=tbb), reads=[tb], writes=[mT])
            for half in range(2):
                yb = banks[half]
                for kt in range(8):
                    P.op("pe", lambda e, kt=kt, half=half, yb=yb: e.matmul(
                        yb[:, :], lhsT=mT[:, kt, :], rhs=wout1[:, kt, half * 512:(half + 1) * 512],
                        start=(kt == 0), stop=(kt == 7)), reads=[mT, wout1], writes=[yb], flag=(kt == 7))
                P.op("dve", lambda e, half=half, yb=yb: e.tensor_tensor(
                    out=ytmp3[:, half * 512:(half + 1) * 512], in0=yb[:, :], in1=gate_lat[:, half * 512:(half + 1) * 512],
                    op=ALU.mult), reads=[yb, gate_lat], writes=[ytmp3])
            P.op("dve", lambda e, xt=xt: e.tensor_tensor(out=xt[:], in0=xt[:], in1=ytmp3[:], op=ALU.add),
                 reads=[xt, ytmp3], writes=[xt])
            sst = rot(ss)
            ov = rot(outv)
            P.op("act", lambda e, xt=xt, sst=sst: e.activation(out=sg[:], in_=xt[:], func=AF.Square, accum_out=sst[:, 0:1]),
                 reads=[xt], writes=[sg, sst])
            P.op("act", lambda e, sst=sst: e.activation(out=sst[:, 1:2], in_=sst[:, 0:1], func=AF.Sqrt, scale=1.0 / D,
                                                        bias=small[:, 0:1]), reads=[sst, small], writes=[sst])
            P.op("dve", lambda e, sst=sst: e.reciprocal(out=sst[:, 2:3], in_=sst[:, 1:2]), reads=[sst], writes=[sst])
            P.op("dve", lambda e, xt=xt, sst=sst, ov=ov: e.scalar_tensor_tensor(
                out=ov[:], in0=xt[:], scalar=sst[:, 2:3], in1=fnw[:], op0=ALU.mult, op1=ALU.mult),
                reads=[xt, sst, fnw], writes=[ov])
            P.out_evs.append(P.dma("sp", lambda e, n=n, ov=ov: e.dma_start(out=out_d[n * 128:(n + 1) * 128, :], in_=ov[:]),
                                   reads=[ov]))
            if n + 1 < NT and n % 8 == 7:
                pre3(n + 1)

    if mode == "l0":
        for n in range(NT):
            P.out_evs.append(P.dma("sp", lambda e, n=n: e.dma_start(out=x1_d[n * 128:(n + 1) * 128, :], in_=xs[n][:]),
                                   reads=[xs[n]]))
        for n in range(2):
            P.out_evs.append(P.dma("sp", lambda e, n=n: e.dma_start(out=ctx1_d[n * 128:(n + 1) * 128, :], in_=cx[n][:]),
                                   reads=[cx[n]]))

    P.wait_all("sp", P.out_evs)
    P.emit()
    return nc


def _in_maps(inputs, mode):
    x = np.asarray(inputs["x"], np.float32)
    c = np.asarray(inputs["c"], np.float32)
    ctx = np.asarray(inputs["ctx"], np.float32)
    c_ctx = np.asarray(inputs["c_ctx"], np.float32)
    nwT = np.concatenate([_colT(inputs["norm_w"][l]) for l in range(2)], 1)
    badaT = np.concatenate([_colT(inputs["b_ada"][l]) for l in range(2)], 1)
    maps = []
    for core in range(NCORES):
        b, hf = core // 2, core % 2
        cs = _consts(hf)
        xh = np.zeros((256, D), np.float32)
        if hf == 0:
            xl = x[b, 0:TOK]
            xh[128:256] = x[b, TOK:TOK + 128]
            cl = ctx[b]
        else:
            xl = x[b, TOK:2 * TOK][::-1]
            xh[128:256] = x[b, TOK - 128:TOK][::-1]
            cl = ctx[b][::-1]
        cT = np.stack([_colT(c[b]), _colT(c_ctx)], 2).reshape(128, 16)
        m = dict(x=np.ascontiguousarray(xl), ctx=np.ascontiguousarray(cl),
                 cT=np.ascontiguousarray(cT), nwT=nwT, badaT=badaT,
                 w_ada=np.asarray(inputs["w_ada"], np.float32), b_ada=np.asarray(inputs["b_ada"], np.float32),
                 ident=cs["ident"])
        if mode in ("full", "l0"):
            m.update(xh=xh, attn_w_in=np.asarray(inputs["attn_w_in"][0], np.float32),
                     attn_w_out=np.asarray(inputs["attn_w_out"][0], np.float32),
                     attn_sink=np.asarray(inputs["attn_sink"], np.float32).reshape(1, 16),
                     cosT=cs["cosT"], sinT=cs["sinT"], masks=cs["masks"], psw=cs["psw"])
        if mode in ("full", "l1"):
            dirs = (hf, 1 - hf)
            def dup(a):
                t = np.concatenate([np.asarray(a[0][d_], np.float32).T for d_ in dirs], 1)
                return np.ascontiguousarray(np.concatenate([t, t], 0))
            bre, bim = inputs["ssm_b_re"][0], inputs["ssm_b_im"][0]
            cre, cim = inputs["ssm_c_re"][0], inputs["ssm_c_im"][0]
            def bl(a):
                return np.concatenate([np.asarray(a[d_], np.float32).transpose(1, 0, 2).reshape(64, 64 * 16) for d_ in dirs], 1)
            def cl_(a):
                return np.concatenate([np.asarray(a[d_], np.float32).transpose(2, 0, 1).reshape(64, 64 * 16) for d_ in dirs], 1)
            dvec = np.asarray(inputs["ssm_d"][0], np.float32).reshape(64, 16)
            dT = np.ascontiguousarray(np.tile(dvec.T, (8, 1)))
            sidx = np.arange(128) // 16
            mF = (sidx[None, :] >= sidx[:, None]).astype(np.float32)
            mB = (sidx[:, None] >= sidx[None, :]).astype(np.float32)
            sg = np.zeros((128, 8), np.float32)
            top = np.arange(128) < 64
            sg[:, 0] = np.where(top, -1.0, 1.0); sg[:, 1] = np.where(top, 1.0, -1.0)
            sg[:, 2] = top; sg[:, 3] = ~top; sg[:, 4] = np.where(top, 0.0, -1.0); sg[:, 5] = -1.0
            sg[:, 6] = 1.0 if hf == 1 else 0.0
            sg[:, 7] = 1.0 if hf == 0 else 0.0
            m.update(ssm_w_in=np.asarray(inputs["ssm_w_in"][0], np.float32),
                     ssm_w_glu=np.asarray(inputs["ssm_w_glu"][0], np.float32),
                     ssm_w_out=np.asarray(inputs["ssm_w_out"][0], np.float32),
                     fnw=np.asarray(inputs["final_norm_w"], np.float32).reshape(1, D),
                     lamR=dup(inputs["ssm_lam_re"]), lamI=dup(inputs["ssm_lam_im"]),
                     logdt=np.ascontiguousarray(np.concatenate([np.asarray(inputs["ssm_log_dt"][0][d_], np.float32) for d_ in dirs]).reshape(1, 128)),
                     bA=np.ascontiguousarray(np.concatenate([bl(bre), bl(bim)], 0)),
                     bS=np.ascontiguousarray(np.concatenate([bl(bim), bl(bre)], 0)),
                     cA=np.ascontiguousarray(np.concatenate([cl_(cre), cl_(cim)], 0)),
                     cS=np.ascontiguousarray(np.concatenate([cl_(cim), cl_(cre)], 0)),
                     dT=dT, mFB=np.ascontiguousarray(np.concatenate([mF, mB], 1)),
                     idst=np.ascontiguousarray(np.concatenate([np.eye(64, dtype=np.float32)] * 2, 0)), sgn=sg)
        maps.append(m)
    return maps


def gather(results, key, n=TOK):
    out = np.zeros((4, 2 * n, D), np.float32)
    for core in range(NCORES):
        b, hf = core // 2, core % 2
        r = results[core][key]
        out[b, hf * n:(hf + 1) * n] = r if hf == 0 else r[::-1]
    return out


def run_mode(inputs, mode):
    nc = build(mode)
    res = run_bass_kernel_spmd(nc, _in_maps(inputs, mode), core_ids=list(range(NCORES)))
    return res.results


def kernel(**inputs):
    r = run_mode(inputs, "full")
    return gather(r, "out")
```

```python
import os
import numpy as np
import concourse.bass as bass
import concourse.mybir as mybir
from concourse.bass_utils import run_bass_kernel_spmd

F32 = mybir.dt.float32
BF16 = mybir.dt.bfloat16
AF = mybir.ActivationFunctionType
ALU = mybir.AluOpType
AX = mybir.AxisListType

EPOCH = 3000
NCORES = 8
D = 1024
TOK = 2048
NT = 16
EPS = 1e-6


class Tile:
    def __init__(self, name, t):
        self.name = name
        self.t = t
        self.wev = None
        self.revs = {}
        self.dsem = None
        self.dcnt = 0
        self.aliases = []

    def __getitem__(self, k):
        return self.t[k]


class Prog:
    ENGS = ("pe", "act", "dve", "pool", "sp")

    def __init__(self, nc, same_engine_sync=True):
        self.nc = nc
        self.same_sync = same_engine_sync
        self.stream = {e: [] for e in self.ENGS}
        self.sems = {e: [] for e in self.ENGS}
        self.cnt = {e: 0 for e in self.ENGS}
        self.waited = {e: {} for e in self.ENGS}
        self.own = {e: set() for e in self.ENGS}
        self.nsem = 0
        for e in self.ENGS:
            self._new_epoch(e)
        self.tiles = []
        self.out_evs = []

    def _sem(self, name):
        self.nsem += 1
        return self.nc.alloc_semaphore(name=f"{name}_{self.nsem}")

    def _new_epoch(self, e):
        s = self._sem(f"e_{e}")
        self.sems[e].append(s)
        self.own[e].add(id(s))
        self.cnt[e] = 0

    def sbuf(self, name, shape, dtype):
        t = self.nc.alloc_sbuf_tensor("s_" + name, list(shape), dtype)
        tl = Tile(name, t)
        self.tiles.append(tl)
        return tl

    def psum(self, name, shape, dtype=F32):
        t = self.nc.alloc_psum_tensor("p_" + name, list(shape), dtype)
        tl = Tile(name, t)
        self.tiles.append(tl)
        return tl

    def dram(self, name, shape, dtype, **kw):
        t = self.nc.dram_tensor(name, list(shape), dtype, **kw)
        tl = Tile(name, t)
        self.tiles.append(tl)
        return tl

    def _need(self, eng, waits, ev):
        if ev is None:
            return
        sem, val = ev
        k = id(sem)
        if k in self.own[eng] and (eng == "pe" or not self.same_sync):
            return
        if self.waited[eng].get(k, 0) >= val:
            return
        if waits.get(k, (None, 0))[1] < val:
            waits[k] = (sem, val)

    def _deps(self, eng, reads, writes):
        waits = {}
        for t in reads:
            self._need(eng, waits, t.wev)
        for t0 in writes:
            for t in [t0] + t0.aliases:
                self._need(eng, waits, t.wev)
                for ev in t.revs.values():
                    self._need(eng, waits, ev)
        for t0 in reads:
            for t in t0.aliases:
                pass
        wl = list(waits.values())
        for sem, val in wl:
            self.waited[eng][id(sem)] = val
        return wl

    def _mark(self, ev, reads, writes):
        sem, val = ev
        for t in reads:
            old = t.revs.get(id(sem))
            if old is None or old[1] < val:
                t.revs[id(sem)] = ev
        for t in writes:
            t.wev = ev
            t.revs = {}

    def op(self, eng, fn, reads=(), writes=(), flag=True):
        wl = self._deps(eng, reads, writes)
        sem = self.sems[eng][-1]
        ev = (sem, self.cnt[eng] + 1)
        if flag:
            self.cnt[eng] += 1
        self.stream[eng].append((wl, fn, ev if flag else None, 1))
        self._mark(ev, reads, writes)
        if flag and self.cnt[eng] >= EPOCH:
            self._new_epoch(eng)
        return ev

    def dma(self, q, fn, reads=(), writes=(), semtile=None, inc=16):
        wl = self._deps(q, reads, writes)
        st = semtile or (writes[0] if writes else reads[0])
        if st.dsem is None:
            st.dsem = self._sem("d_" + st.name)
        st.dcnt += inc
        ev = (st.dsem, st.dcnt)
        self.stream[q].append((wl, fn, ev, inc))
        self._mark(ev, reads, writes)
        return ev

    def wait_all(self, eng, evs):
        waits = {}
        for ev in evs:
            self._need(eng, waits, ev)
        wl = list(waits.values())
        for sem, val in wl:
            self.waited[eng][id(sem)] = val
        self.stream[eng].append((wl, None, None, 0))

    def emit(self):
        nc = self.nc
        with nc.Block() as block:
            def replay(name):
                def f(e):
                    for wl, fn, ev, inc in self.stream[name]:
                        for sem, val in wl:
                            e.wait_ge(sem, val)
                        if fn is None:
                            continue
                        ins = fn(e)
                        if ev is not None:
                            ins.then_inc(ev[0], inc)
                return f
            block.tensor(replay("pe"))
            block.scalar(replay("act"))
            block.vector(replay("dve"))
            block.gpsimd(replay("pool"))
            block.sync(replay("sp"))


class Arena:
    def __init__(self, P, name, words):
        self.P = P
        self.name = name
        self.words = words
        self.base = P.nc.alloc_sbuf_tensor("s_" + name, [128, words], F32)
        self.carves = []

    def carve(self, name, off, shape, dtype):
        n = int(np.prod(shape[1:]))
        w = n if dtype == F32 else (n + 1) // 2
        assert off + w <= self.words, (name, off, w, self.words)
        ap = self.base[:, off:off + w]
        if dtype != F32:
            ap = ap.bitcast(dtype)
        if len(shape) == 3:
            ap = ap.rearrange("p (a b) -> p a b", a=shape[1])
        elif len(shape) == 4:
            ap = ap.rearrange("p (a b c) -> p a b c", a=shape[1], b=shape[2])
        tl = Tile(name, ap)
        for lo, hi, old in self.carves:
            if lo < off + w and off < hi:
                tl.aliases.append(old)
                old.aliases.append(tl)
        self.carves.append((off, off + w, tl))
        self.P.tiles.append(tl)
        return tl


def _rope_tables(hf):
    inv = (10000.0 ** (-np.arange(16, dtype=np.float32) / 16)).astype(np.float32)
    j = np.arange(2304)
    g = (j - 128) if hf == 0 else (4095 - (j - 128))
    g = np.clip(g, 0, 4095)
    row = (g // 64).astype(np.float32)
    col = (g % 64).astype(np.float32)
    cosT = np.zeros((2304, 64), np.float32)
    sinT = np.zeros((2304, 64), np.float32)
    for d in range(64):
        axis, half, f = d // 32, (d % 32) // 16, d % 16
        ang = ((row if axis == 0 else col) * inv[f]).astype(np.float32)
        cosT[:, d] = np.cos(ang)
        sinT[:, d] = np.sin(ang) * (-1.0 if half == 0 else 1.0)
    cosT = np.ascontiguousarray(cosT.reshape(18, 128, 64).transpose(1, 0, 2).reshape(128, 18 * 64))
    sinT = np.ascontiguousarray(sinT.reshape(18, 128, 64).transpose(1, 0, 2).reshape(128, 18 * 64))
    return cosT, sinT


def _consts(hf):
    cosT, sinT = _rope_tables(hf)
    jj = np.arange(128)[:, None]
    ii = np.arange(128)[None, :]
    mP = (jj >= ii).astype(np.float32)
    mN = (jj <= ii).astype(np.float32)
    masks = np.stack([mP, mN, mP * 0.0, mN], 1)
    ident = np.eye(128, dtype=np.float32)
    psw = np.zeros((128, 128), np.float32)
    for m in range(128):
        d = m % 64
        sw = d + 16 if (d % 32) < 16 else d - 16
        psw[(m // 64) * 64 + sw, m] = 1.0
    return dict(cosT=cosT, sinT=sinT, masks=np.ascontiguousarray(masks.reshape(128, 4 * 128)),
                ident=ident, psw=psw)


def _colT(v):
    return np.ascontiguousarray(np.asarray(v, np.float32).reshape(-1, 128).T)


def build(mode="full"):
    do0 = mode in ("full", "l0")
    do1 = mode in ("full", "l1")
    nc = bass.Bass("TRN2", target_bir_lowering=False, dynamic_dma_scratch_size=4096)
    P = Prog(nc)

    def din(name, shape):
        return nc.dram_tensor(name, list(shape), F32, kind="ExternalInput").ap()

    def dout(name, shape):
        return nc.dram_tensor(name, list(shape), F32, kind="ExternalOutput").ap()

    x_d = din("x", [TOK, D])
    ctx_d = din("ctx", [256, D])
    nwT_d = din("nwT", [128, 16])
    badaT_d = din("badaT", [128, 48])
    wadaS_d = din("wadaS", [2, 128, 3 * D])
    cT5_d = din("cT5", [128, 5])
    sel5_d = din("sel5", [128, 5])
    ident_d = din("ident", [128, 128])
    if do0:
        xh_d = din("xh", [256, D])
        win0_d = din("attn_w_in", [D, 2560])
        wout0_d = din("attn_w_out", [D, D])
        sink_d = din("attn_sink", [1, 16])
        cos_d = din("cosT", [128, 1152])
        sin_d = din("sinT", [128, 1152])
        masks_d = din("masks", [128, 512])
        psw_d = din("psw", [128, 128])
    if mode == "l0":
        x1_d = dout("x1", [TOK, D])
        ctx1_d = dout("ctx1", [256, D])

    xs = [P.sbuf(f"xs{n}", [128, D], F32) for n in range(NT)]
    cx = [P.sbuf(f"cx{n}", [128, D], F32) for n in range(2)]
    ident = P.sbuf("ident", [128, 128], BF16)
    small = P.sbuf("small", [128, 8], F32)
    nwT = P.sbuf("nwT", [128, 16], F32)
    badaT = P.sbuf("badaT", [128, 48], F32)
    modT = P.sbuf("modT", [128, 16, 2], F32)
    wmT = P.sbuf("wmT", [128, 8, 2], F32)
    gate_lat = P.sbuf("gate_lat", [128, D], F32)
    banks = [P.psum(f"B{i}", [128, 512], F32) for i in range(8)]
    ss = [P.sbuf(f"ss{i}", [128, 4], F32) for i in range(2)]

    A = Arena(P, "arena", 28200 + 2 * 1280)
    stages = [A.carve(f"stg{i}", 28200 + i * 1280, [128, 1280], F32) for i in range(2)]
    cast_rr = {"n": 0}

    def load_cast(dst_tile, dst_ap, src_ap, ncols, engines=("act", "dve", "pool"), view=None):
        st = stages[cast_rr["n"] % 2]
        eng = engines[cast_rr["n"] % len(engines)]
        cast_rr["n"] += 1
        P.dma("sp", lambda e: e.dma_start(out=st[:, 0:ncols], in_=src_ap), writes=[st])
        src = st[:, 0:ncols] if view is None else view(st[:, 0:ncols])
        if eng == "act":
            P.op("act", lambda e: e.copy(out=dst_ap, in_=src), reads=[st], writes=[dst_tile])
        else:
            P.op(eng, lambda e: e.tensor_copy(out=dst_ap, in_=src), reads=[st], writes=[dst_tile])

    P.dma("pool", lambda e: e.dma_start(out=ident[:], in_=ident_d), writes=[ident])
    P.dma("sp", lambda e: e.dma_start(out=nwT[:], in_=nwT_d), writes=[nwT])
    P.dma("sp", lambda e: e.dma_start(out=badaT[:], in_=badaT_d), writes=[badaT])
    for n in range(2):
        P.dma("pool", lambda e, n=n: e.dma_start(out=cx[n][:], in_=ctx_d[n * 128:(n + 1) * 128, :]),
              writes=[cx[n]])
    for n in range(NT):
        P.dma("pool", lambda e, n=n: e.dma_start(out=xs[n][:], in_=x_d[n * 128:(n + 1) * 128, :]),
              writes=[xs[n]])
    cnt = {"n": 0}

    def rot(lst):
        cnt["n"] += 1
        return lst[cnt["n"] % len(lst)]

    mod5 = P.sbuf("mod5", [128, 48, 5], F32)
    sel5 = P.sbuf("sel5", [128, 5], F32)
    cT5 = P.sbuf("cT5s", [128, 5], F32)
    sc5 = P.sbuf("sc5", [128, 5], BF16)
    gTl = P.sbuf("gTl", [128, 8], F32)
    ARin = P.dram("ARin", [128, 240], F32)
    ARout = P.dram("ARout", [128, 240], F32)
    P.dma("sp", lambda e: e.dma_start(out=sel5[:], in_=sel5_d), writes=[sel5])
    P.dma("sp", lambda e: e.dma_start(out=cT5[:], in_=cT5_d), writes=[cT5])
    P.op("act", lambda e: e.activation(out=sc5[:], in_=cT5[:], func=AF.Silu), reads=[cT5], writes=[sc5])

    def adaln_partial(wS):
        accb = banks[7]
        for l in range(2):
            for c0 in range(0, 3 * D, 1024):
                load_cast(wS, wS[:, l, c0:c0 + 1024], wadaS_d[l, :, c0:c0 + 1024], 1024)
        for l in range(2):
            for j in range(24):
                q = l * 24 + j
                P.op("pe", lambda e, l=l, j=j, q=q: e.matmul(accb[:, q * 5:q * 5 + 5], lhsT=wS[:, l, j * 128:(j + 1) * 128],
                                                             rhs=sc5[:], start=True, stop=True),
                     reads=[wS, sc5], writes=[accb], flag=(q == 47))
        P.op("act", lambda e: e.copy(out=mod5[:].rearrange("p a b -> p (a b)"), in_=accb[:, 0:240]), reads=[accb], writes=[mod5])
        P.dma("sp", lambda e: e.dma_start(out=ARin.t.ap(), in_=mod5[:].rearrange("p a b -> p (a b)")),
              reads=[mod5], writes=[ARin], semtile=mod5)
        P.dma("pool", lambda e: e.collective_compute("AllReduce", ALU.add, [list(range(NCORES))],
                                                     ins=[ARin.t.ap()], outs=[ARout.t.ap()]),
              reads=[ARin], writes=[ARout], semtile=ARout, inc=1)
        P.dma("sp", lambda e: e.dma_start(out=mod5[:].rearrange("p a b -> p (a b)"), in_=ARout.t.ap()),
              reads=[ARout], writes=[mod5])

    def adaln(layer, gate_tiles, rep):
        base = layer * 24
        for s_ in range(2):
            dst = modT[:, :, s_]
            if s_ == 1:
                P.op("dve", lambda e, dst=dst: e.tensor_tensor(out=dst, in0=mod5[:, base:base + 16, 4],
                                                               in1=badaT[:, layer * 24:layer * 24 + 16], op=ALU.add),
                     reads=[mod5, badaT], writes=[modT])
            else:
                P.op("dve", lambda e, dst=dst: e.scalar_tensor_tensor(
                    out=dst, in0=mod5[:, base:base + 16, 0], scalar=sel5[:, 0:1], in1=badaT[:, layer * 24:layer * 24 + 16],
                    op0=ALU.mult, op1=ALU.add), reads=[mod5, sel5, badaT], writes=[modT])
                for v in range(1, 4):
                    P.op("dve", lambda e, dst=dst, v=v: e.scalar_tensor_tensor(
                        out=dst, in0=mod5[:, base:base + 16, v], scalar=sel5[:, v:v + 1], in1=dst,
                        op0=ALU.mult, op1=ALU.add), reads=[mod5, sel5, modT], writes=[modT])
        P.op("dve", lambda e: e.tensor_scalar(out=wmT[:], in0=modT[:, 8:16, :], scalar1=1.0, scalar2=None,
                                              op0=ALU.add), reads=[modT], writes=[wmT])
        P.op("dve", lambda e: e.tensor_tensor(
            out=wmT[:], in0=wmT[:], in1=nwT[:, layer * 8:layer * 8 + 8].unsqueeze(2).broadcast_to([128, 8, 2]),
            op=ALU.mult), reads=[wmT, nwT], writes=[wmT])
        for (s_, gt) in gate_tiles:
            if s_ == 1:
                P.op("dve", lambda e: e.tensor_tensor(out=gTl[:], in0=mod5[:, base + 16:base + 24, 4],
                                                      in1=badaT[:, layer * 24 + 16:layer * 24 + 24], op=ALU.add),
                     reads=[mod5, badaT], writes=[gTl])
            else:
                P.op("dve", lambda e: e.scalar_tensor_tensor(
                    out=gTl[:], in0=mod5[:, base + 16:base + 24, 0], scalar=sel5[:, 0:1],
                    in1=badaT[:, layer * 24 + 16:layer * 24 + 24], op0=ALU.mult, op1=ALU.add),
                    reads=[mod5, sel5, badaT], writes=[gTl])
                for v in range(1, 4):
                    P.op("dve", lambda e, v=v: e.scalar_tensor_tensor(
                        out=gTl[:], in0=mod5[:, base + 16:base + 24, v], scalar=sel5[:, v:v + 1], in1=gTl[:],
                        op0=ALU.mult, op1=ALU.add), reads=[mod5, sel5, gTl], writes=[gTl])
            for hf_ in range(2):
                gb = banks[5 + hf_]
                for jj in range(4):
                    j = hf_ * 4 + jj
                    P.op("dve", lambda e, j=j: e.tensor_copy(out=rep[:], in_=gTl[:, j:j + 1].broadcast_to([128, 128])),
                         reads=[gTl], writes=[rep])
                    P.op("pe", lambda e, jj=jj, gb=gb: e.matmul(gb[:, jj * 128:(jj + 1) * 128], lhsT=rep[:],
                                                                rhs=ident[:], start=True, stop=True),
                         reads=[rep, ident], writes=[gb])
                P.op("act", lambda e, gb=gb, gt=gt, hf_=hf_: e.copy(out=gt[:, hf_ * 512:(hf_ + 1) * 512], in_=gb[:, :]),
                     reads=[gb], writes=[gt])

    def norm_hT(xt, s, hT, xn, c0=0):
        sst = rot(ss)
        tb = rot(banks[6:8])
        P.op("act", lambda e: e.activation(out=xn[:], in_=xt[:], func=AF.Square, accum_out=sst[:, 0:1]),
             reads=[xt], writes=[xn, sst])
        P.op("act", lambda e: e.activation(out=sst[:, 1:2], in_=sst[:, 0:1], func=AF.Sqrt, scale=1.0 / D,
                                           bias=small[:, 0:1]), reads=[sst, small], writes=[sst])
        P.op("dve", lambda e: e.reciprocal(out=sst[:, 2:3], in_=sst[:, 1:2]), reads=[sst], writes=[sst])
        P.op("act", lambda e: e.activation(out=xn[:], in_=xt[:], func=AF.Identity, scale=sst[:, 2:3]),
             reads=[xt, sst], writes=[xn])
        tbb = tb[:, :].bitcast(BF16)
        for kt in range(8):
            P.op("pe", lambda e, kt=kt: e.transpose(out=tbb[:, kt * 128:(kt + 1) * 128],
                                                    in_=xn[:, kt * 128:(kt + 1) * 128], identity=ident[:]),
                 reads=[xn, ident], writes=[tb], flag=(kt == 7))
        for kt in range(8):
            if kt % 2 == 0:
                P.op("dve", lambda e, kt=kt: e.tensor_scalar(
                    out=hT[:, kt, c0:c0 + 128], in0=tbb[:, kt * 128:(kt + 1) * 128], scalar1=wmT[:, kt, s:s + 1],
                    scalar2=modT[:, kt, s:s + 1], op0=ALU.mult, op1=ALU.add),
                    reads=[tb, wmT, modT], writes=[hT])
            else:
                P.op("act", lambda e, kt=kt: e.activation(
                    out=hT[:, kt, c0:c0 + 128], in_=tbb[:, kt * 128:(kt + 1) * 128], func=AF.Identity,
                    scale=wmT[:, kt, s:s + 1], bias=modT[:, kt, s:s + 1]),
                    reads=[tb, wmT, modT], writes=[hT])

    P.op("dve", lambda e: e.memset(small[:, 0:1], EPS), writes=[small])

    if do0:
        o = 0
        def carve(name, shape, dtype):
            nonlocal o
            n = int(np.prod(shape[1:]))
            w = n if dtype == F32 else (n + 1) // 2
            t = A.carve(name, o, shape, dtype)
            o += w
            return t
        win0 = carve("win0", [128, 8, 2560], BF16)
        wout0 = carve("wout0", [128, 8, D], BF16)
        cosT = carve("cosT", [128, 18, 64], BF16)
        sinT = carve("sinT", [128, 18, 64], BF16)
        masks = carve("masks", [128, 4, 128], BF16)
        psw = carve("psw", [128, 128], BF16)
        esink = carve("esink", [128, 16], F32)
        gate_ctx = carve("gate_ctx", [128, D], F32)
        KTs = [carve(f"KT{i}", [128, 2, 128], BF16) for i in range(4)]
        Vs = [carve(f"V{i}", [128, 4, 65], BF16) for i in range(4)]
        KTc = [carve(f"KTc{i}", [128, 2, 128], BF16) for i in range(2)]
        Vc = [carve(f"Vc{i}", [128, 4, 65], BF16) for i in range(2)]
        o_scr = o
        wchunk = carve("wchunk", [128, 8, 512], BF16)
        brow = carve("brow", [128, D], F32)
        xh = [carve(f"xh{i}", [128, D], F32) for i in range(2)]
        screp = carve("screp", [128, 2, 8, 128], BF16)
        o_early_end = o
        o = o_scr
        W = {}
        W["QT"] = [carve(f"QT{i}", [128, 2, 4, 128], BF16) for i in range(2)]
        W["SZ"] = [carve(f"SZ{i}", [128, D], BF16) for i in range(2)]
        W["PT"] = [carve(f"PT{i}", [128, 512], BF16) for i in range(10)]
        o_t1 = o
        W["t1"] = carve("t1", [128, D], BF16)
        W["g"] = carve("g", [128, D], BF16)
        o_w = o
        o = o_t1
        W["ytmp"] = carve("ytmp", [128, D], F32)
        o = max(o, o_w)
        W["gT"] = carve("gT", [128, 8, 128], BF16)
        o = max(o, o_early_end)
        hTs = [carve(f"hT{i}", [128, 8, 128], BF16) for i in range(2)]
        xn0 = carve("xn0", [128, D], BF16)
        rA = carve("rA", [128, 512], F32)
        rB = carve("rB", [128, 512], F32)
        qrot = carve("qrot", [128, D], BF16)
        krot = carve("krot", [128, 256], BF16)
        lsum = carve("lsum", [128, 16], F32)
        rl = carve("rl", [128, 16], F32)
        KTr = carve("KTr", [128, 2, 128], BF16)
        Vr = carve("Vr", [128, 4, 65], BF16)
        print("layer0 arena words used:", o)

        wS = A.carve("wS", o_scr, [128, 2, 3 * D], BF16)
        adaln_partial(wS)
        rep0 = A.carve("rep0", o_scr + 5120, [128, 128], BF16)
        adaln(0, [(0, gate_lat), (1, gate_ctx)], rep0)
        for kt in range(8):
            rows = slice(kt * 128, (kt + 1) * 128)
            for pr in range(2):
                cast_rr["n"] += 0
            st_q = None
            st = stages[cast_rr["n"] % 2]
            cast_rr["n"] += 1
            P.dma("sp", lambda e, st=st, rows=rows: e.dma_start(out=st[:, 0:1280], in_=win0_d[rows, 0:1280]), writes=[st])
            for pr in range(2):
                eng = ("act", "dve")[pr]
                src = st[:, pr * 512:(pr + 1) * 512].rearrange("p (s g d) -> p g s d", s=2, g=4)
                dst = win0[:, kt, pr * 512:(pr + 1) * 512].rearrange("p (g s d) -> p g s d", g=4, s=2)
                if eng == "act":
                    P.op("act", lambda e, src=src, dst=dst: e.copy(out=dst, in_=src), reads=[st], writes=[win0])
                else:
                    P.op("dve", lambda e, src=src, dst=dst: e.tensor_copy(out=dst, in_=src), reads=[st], writes=[win0])
            P.op("pool", lambda e, st=st, kt=kt: e.tensor_copy(out=win0[:, kt, 1024:1280], in_=st[:, 1024:1280]),
                 reads=[st], writes=[win0])
            load_cast(win0, win0[:, kt, 1280:2560], win0_d[rows, 1280:2560], 1280)
        for kt in range(8):
            load_cast(wout0, wout0[:, kt, :], wout0_d[kt * 128:(kt + 1) * 128, :], 1024)
        P.dma("pool", lambda e: e.dma_start(out=cosT[:].rearrange("p a b -> p (a b)"), in_=cos_d), writes=[cosT])
        P.dma("pool", lambda e: e.dma_start(out=sinT[:].rearrange("p a b -> p (a b)"), in_=sin_d), writes=[sinT])
        P.dma("pool", lambda e: e.dma_start(out=masks[:].rearrange("p a b -> p (a b)"), in_=masks_d),
              writes=[masks])
        P.dma("pool", lambda e: e.dma_start(out=psw[:], in_=psw_d), writes=[psw])
        P.dma("sp", lambda e: e.dma_start(out=esink[:], in_=sink_d.partition_broadcast(128)), writes=[esink])
        P.op("act", lambda e: e.activation(out=esink[:], in_=esink[:], func=AF.Exp), reads=[esink], writes=[esink])
        for i in range(2):
            P.dma("sp", lambda e, i=i: e.dma_start(out=xh[i][:], in_=xh_d[i * 128:(i + 1) * 128, :]), writes=[xh[i]])
        for i in range(4):
            P.op("pool", lambda e, i=i: e.memset(Vs[i][:, :, 64:65], 1.0), writes=[Vs[i]])
        for i in range(2):
            P.op("pool", lambda e, i=i: e.memset(Vc[i][:, :, 64:65], 1.0), writes=[Vc[i]])


        def rope_tok(src, ncols, ti, dst_tile, dst_ap, rope):
            nh = ncols // 64
            bank = src[0]
            sap = src[1]
            if not rope:
                P.op("act", lambda e: e.copy(out=dst_ap, in_=sap), reads=[bank], writes=[dst_tile])
                return
            P.op("act", lambda e: e.copy(out=rA[:, 0:ncols], in_=sap), reads=[bank], writes=[rA])
            rA3 = rA[:, 0:ncols].rearrange("p (h d) -> p h d", d=64)
            rB3 = rB[:, 0:ncols].rearrange("p (h d) -> p h d", d=64)
            for a_ in range(2):
                for half in range(2):
                    oo = a_ * 32 + half * 16
                    io = a_ * 32 + (1 - half) * 16
                    P.op("dve", lambda e, oo=oo, io=io: e.tensor_tensor(
                        out=rB3[:, :, oo:oo + 16], in0=rA3[:, :, io:io + 16],
                        in1=sinT[:, ti:ti + 1, oo:oo + 16].broadcast_to([128, nh, 16]),
                        op=ALU.mult), reads=[rA, sinT], writes=[rB])
            P.op("dve", lambda e: e.tensor_tensor(out=rA3, in0=rA3, in1=cosT[:, ti:ti + 1, :].broadcast_to([128, nh, 64]),
                                                  op=ALU.mult), reads=[rA, cosT], writes=[rA])
            P.op("pool", lambda e: e.tensor_tensor(out=dst_ap, in0=rA[:, 0:ncols], in1=rB[:, 0:ncols], op=ALU.add),
                 reads=[rA, rB], writes=[dst_tile])

        def proj(xt, s, ti, KT, V, QT=None, SZ=None, rope=True):
            hT = rot(hTs)
            norm_hT(xt, s, hT, xn0)
            kvb = banks[2]
            for kt in range(8):
                P.op("pe", lambda e, kt=kt: e.matmul(kvb[:, :], lhsT=hT[:, kt, :], rhs=win0[:, kt, 1024:1536],
                                                     start=(kt == 0), stop=(kt == 7)),
                     reads=[win0, hT], writes=[kvb], flag=(kt == 7))
            if os.environ.get("V_ENG") == "dve":
                P.op("dve", lambda e: e.tensor_copy(out=V[:, :, 0:64], in_=kvb[:, 256:512].rearrange("p (k d) -> p k d", k=4)),
                     reads=[kvb], writes=[V])
            else:
                P.op("act", lambda e: e.copy(out=V[:, :, 0:64], in_=kvb[:, 256:512].rearrange("p (k d) -> p k d", k=4)),
                     reads=[kvb], writes=[V])
            rope_tok((kvb, kvb[:, 0:256]), 256, ti, krot, krot[:], rope)
            tb = rot(banks[6:8])
            tbb = tb[:, :].bitcast(BF16)
            for kt2 in range(2):
                P.op("pe", lambda e, kt2=kt2, tbb=tbb: e.transpose(out=tbb[:, kt2 * 128:(kt2 + 1) * 128],
                                                                   in_=krot[:, kt2 * 128:(kt2 + 1) * 128], identity=ident[:]),
                     reads=[krot, ident], writes=[tb], flag=(kt2 == 1))
            P.op("act", lambda e, tbb=tbb: e.copy(out=KT[:].rearrange("p a b -> p (a b)"), in_=tbb[:, 0:256]),
                 reads=[tb], writes=[KT])
            if QT is None:
                return
            for half in range(2):
                qb = banks[half]
                for kt in range(8):
                    P.op("pe", lambda e, kt=kt, half=half, qb=qb: e.matmul(
                        qb[:, :], lhsT=hT[:, kt, :], rhs=win0[:, kt, half * 512:(half + 1) * 512],
                        start=(kt == 0), stop=(kt == 7)), reads=[win0, hT], writes=[qb], flag=(kt == 7))
            for half in range(2):
                zb = banks[4 + half]
                for kt in range(8):
                    P.op("pe", lambda e, kt=kt, half=half, zb=zb: e.matmul(
                        zb[:, :], lhsT=hT[:, kt, :], rhs=win0[:, kt, 1536 + half * 512:1536 + (half + 1) * 512],
                        start=(kt == 0), stop=(kt == 7)), reads=[win0, hT], writes=[zb], flag=(kt == 7))
            for half in range(2):
                qb = banks[half]
                rope_tok((qb, qb[:, :]), 512, ti, qrot, qrot[:, half * 512:(half + 1) * 512], rope)
            for half in range(2):
                zb = banks[4 + half]
                P.op("act", lambda e, half=half, zb=zb: e.activation(
                    out=SZ[:, half * 512:(half + 1) * 512], in_=zb[:, :], func=AF.Silu), reads=[zb], writes=[SZ])
            tb = rot(banks[6:8])
            tbb = tb[:, :].bitcast(BF16)
            for qi in range(8):
                P.op("pe", lambda e, qi=qi, tbb=tbb: e.transpose(out=tbb[:, qi * 128:(qi + 1) * 128],
                                                                 in_=qrot[:, qi * 128:(qi + 1) * 128], identity=ident[:]),
                     reads=[qrot, ident], writes=[tb], flag=(qi == 7))
            P.op("act", lambda e, tbb=tbb: e.copy(out=QT[:].rearrange("p a b c -> p (a b c)"), in_=tbb),
                 reads=[tb], writes=[QT])

        def attn(W, QT, SZ, blocks, xt, gate, mask_of):
            PT = W["PT"]
            nb = len(blocks)
            Ob = banks[3:6]
            pts = {}

            def qk(k):
                rows = slice(0, 64) if k % 2 == 0 else slice(64, 128)
                lst = []
                for bi, (KT, V, mi) in enumerate(blocks):
                    sb = rot(banks[0:3])
                    pt = PT[(k % 2) * 5 + bi]
                    P.op("pe", lambda e, KT=KT, sb=sb, rows=rows, k=k: e.matmul(
                        sb[:, :], lhsT=KT[rows, k // 2, :], rhs=QT[rows, k // 2, :, :], start=True, stop=True),
                        reads=[KT, QT], writes=[sb])
                    P.op("act", lambda e, sb=sb, pt=pt: e.activation(out=pt[:], in_=sb[:, :], func=AF.Exp, scale=0.125),
                         reads=[sb], writes=[pt])
                    if mi is not None:
                        eng = "pool" if bi == 0 else "dve"
                        P.op(eng, lambda e, pt=pt, mi=mi: e.tensor_tensor(
                            out=pt[:].rearrange("p (g q) -> p g q", g=4), in0=pt[:].rearrange("p (g q) -> p g q", g=4),
                            in1=masks[:, mi:mi + 1, :].broadcast_to([128, 4, 128]), op=ALU.mult),
                            reads=[pt, masks], writes=[pt])
                    lst.append(pt)
                pts[k] = lst

            def pv(k):
                for gq in range(4):
                    h = 4 * k + gq
                    ob = Ob[h // 7]
                    c0 = (h % 7) * 65
                    for bi, (KT, V, mi) in enumerate(blocks):
                        pt = pts[k][bi]
                        P.op("pe", lambda e, pt=pt, V=V, gq=gq, ob=ob, c0=c0, bi=bi, k=k: e.matmul(
                            ob[:, c0:c0 + 65], lhsT=pt[:, gq * 128:(gq + 1) * 128], rhs=V[:, k, :],
                            start=(bi == 0), stop=(bi == nb - 1)), reads=[pt, V], writes=[ob], flag=(bi == nb - 1))

            qk(0)
            for k in range(4):
                if k + 1 < 4:
                    qk(k + 1)
                pv(k)
            return None

        def attn_mid(W, SZ):
            Ob = banks[3:6]
            for j in range(3):
                nh = 7 if j < 2 else 2
                P.op("dve", lambda e, j=j, nh=nh: e.tensor_tensor(
                    out=lsum[:, 7 * j:7 * j + nh],
                    in0=Ob[j][:, 0:nh * 65].rearrange("p (h e) -> p h e", e=65)[:, :, 64],
                    in1=esink[:, 7 * j:7 * j + nh], op=ALU.add), reads=[Ob[j], esink], writes=[lsum])
            P.op("dve", lambda e: e.reciprocal(out=rl[:], in_=lsum[:]), reads=[lsum], writes=[rl])
            t1, g = W["t1"], W["g"]
            for j in range(3):
                nh = 7 if j < 2 else 2
                P.op("dve", lambda e, j=j, nh=nh: e.tensor_tensor(
                    out=t1[:, 7 * j * 64:(7 * j + nh) * 64].rearrange("p (h d) -> p h d", d=64),
                    in0=Ob[j][:, 0:nh * 65].rearrange("p (h e) -> p h e", e=65)[:, :, 0:64],
                    in1=rl[:, 7 * j:7 * j + nh].unsqueeze(2).broadcast_to([128, nh, 64]), op=ALU.mult),
                    reads=[Ob[j], rl], writes=[t1])
            P.op("dve", lambda e: e.tensor_tensor(out=g[:], in0=t1[:], in1=SZ[:], op=ALU.mult),
                 reads=[t1, SZ], writes=[g])

        def attn_tail(W, xt, gate):
            g, gT, ytmp = W["g"], W["gT"], W["ytmp"]
            tb = rot(banks[6:8])
            tbb = tb[:, :].bitcast(BF16)
            for kt in range(8):
                P.op("pe", lambda e, kt=kt: e.transpose(out=tbb[:, kt * 128:(kt + 1) * 128],
                                                        in_=g[:, kt * 128:(kt + 1) * 128], identity=ident[:]),
                     reads=[g, ident], writes=[tb], flag=(kt == 7))
            P.op("act", lambda e: e.copy(out=gT[:].rearrange("p a b -> p (a b)"), in_=tbb), reads=[tb], writes=[gT])
            for half in range(2):
                yb = banks[half]
                for kt in range(8):
                    P.op("pe", lambda e, kt=kt, half=half, yb=yb: e.matmul(
                        yb[:, :], lhsT=gT[:, kt, :], rhs=wout0[:, kt, half * 512:(half + 1) * 512],
                        start=(kt == 0), stop=(kt == 7)), reads=[gT, wout0], writes=[yb], flag=(kt == 7))
                P.op("dve", lambda e, half=half, yb=yb: e.tensor_tensor(
                    out=ytmp[:, half * 512:(half + 1) * 512], in0=yb[:, :], in1=gate[:, half * 512:(half + 1) * 512],
                    op=ALU.mult), reads=[yb, gate], writes=[ytmp])
            P.op("dve", lambda e: e.tensor_tensor(out=xt[:], in0=xt[:], in1=ytmp[:], op=ALU.add),
                 reads=[xt, ytmp], writes=[xt])

        slot = lambda b: (b + 1) % 4
        proj(xh[0], 0, 0, KTs[slot(-1)], Vs[slot(-1)])
        P.op("pool", lambda e: e.memset(Vr[:, :, 64:65], 1.0), writes=[Vr])
        proj(xh[1], 0, 17, KTr, Vr)
        for i in range(2):
            proj(cx[i], 1, 0, KTc[i], Vc[i], QT=W["QT"][i], SZ=W["SZ"][i], rope=False)
        for i in range(2):
            attn(W, W["QT"][i], W["SZ"][i], [(KTc[0], Vc[0], None), (KTc[1], Vc[1], None)], cx[i], gate_ctx, None)
            attn_mid(W, W["SZ"][i])
            attn_tail(W, cx[i], gate_ctx)
        def blk(b):
            if b == 16:
                return KTr, Vr
            return KTs[slot(b)], Vs[slot(b)]
        def do_proj(n):
            proj(xs[n], 0, 1 + n, *blk(n), QT=W["QT"][n % 2], SZ=W["SZ"][n % 2])
        do_proj(0)
        for n in range(NT):
            if n + 1 < NT:
                do_proj(n + 1)
            kp, vp = blk(n - 1)
            ks, vs_ = blk(n)
            kn, vn = blk(n + 1)
            blocks = [(kp, vp, 2 if n == 0 else 0), (ks, vs_, None), (kn, vn, 3 if n == NT - 1 else 1),
                      (KTc[0], Vc[0], None), (KTc[1], Vc[1], None)]
            attn(W, W["QT"][n % 2], W["SZ"][n % 2], blocks, xs[n], gate_lat, None)
            attn_mid(W, W["SZ"][n % 2])
            attn_tail(W, xs[n], gate_lat)

    if do1:
        win1_d = din("ssm_w_in", [D, 2 * D])
        wglu_d = din("ssm_w_glu", [D, 2 * D])
        wout1_d = din("ssm_w_out", [D, D])
        fnw_d = din("fnw", [1, D])
        lamR_d = din("lamR", [128, 128])
        lamI_d = din("lamI", [128, 128])
        logdt_d = din("logdt", [1, 128])
        bA_d = din("bA", [128, 2048])
        bS_d = din("bS", [128, 2048])
        cA_d = din("cA", [128, 2048])
        cS_d = din("cS", [128, 2048])
        dT_d = din("dT", [128, 64])
        mFB_d = din("mFB", [128, 256])
        idst_d = din("idst", [128, 64])
        sgn_d = din("sgn", [128, 8])
        out_d = dout("out", [TOK, D])
        Ud = P.dram("Ud", [128, 64, 288], BF16)
        Yd = P.dram("Yd", [128, 64, 256], BF16)
        Ydg = [Tile(f"Yd{g}", Yd.t.ap()[:, g, :]) for g in range(64)]
        Ein = P.dram("Ein", [128, 64], F32)
        Eall = P.dram("Eall", [256, 64], F32)
        A.carves = [(lo, hi, t) for (lo, hi, t) in A.carves]
        o = 0

        def carve(name, shape, dtype):
            nonlocal o
            n = int(np.prod(shape[1:]))
            w = n if dtype == F32 else (n + 1) // 2
            t = A.carve(name, o, shape, dtype)
            o += w
            return t

        def TT(eng, O, A_, B_, op):
            P.op(eng, lambda e: e.tensor_tensor(out=O[1], in0=A_[1], in1=B_[1], op=op), reads=[A_[0], B_[0]], writes=[O[0]])

        def TS(eng, O, A_, s1, op0, s2=None, op1=None, extra=()):
            if op1 is None:
                P.op(eng, lambda e: e.tensor_scalar(out=O[1], in0=A_[1], scalar1=s1, scalar2=None, op0=op0),
                     reads=[A_[0]] + list(extra), writes=[O[0]])
            else:
                P.op(eng, lambda e: e.tensor_scalar(out=O[1], in0=A_[1], scalar1=s1, scalar2=s2, op0=op0, op1=op1),
                     reads=[A_[0]] + list(extra), writes=[O[0]])

        def STT(eng, O, A_, sc, B_, op0, op1, extra=()):
            P.op(eng, lambda e: e.scalar_tensor_tensor(out=O[1], in0=A_[1], scalar=sc, in1=B_[1], op0=op0, op1=op1),
                 reads=[A_[0], B_[0]] + list(extra), writes=[O[0]])

        def ACTF(O, A_, func, scale=None, bias=None, extra=()):
            kw = {}
            if scale is not None:
                kw["scale"] = scale
            if bias is not None:
                kw["bias"] = bias
            P.op("act", lambda e: e.activation(out=O[1], in_=A_[1], func=func, **kw), reads=[A_[0]] + list(extra), writes=[O[0]])

        def F(t):
            return (t, t[:])

        win1u = carve("win1u", [128, 8, D], BF16)
        uTs = carve("uTs", [128, 8, 8, 288], BF16)
        wchunk1_big = carve("wS1", [128, 2, 3 * D], BF16)
        hT1 = [carve(f"hT1_{i}", [128, 8, 512], BF16) for i in range(2)]
        xn1 = carve("xn1", [128, D], BF16)
        o_p1 = o
        for kt in range(8):
            load_cast(win1u, win1u[:, kt, :], win1_d[kt * 128:(kt + 1) * 128, 0:D], 1024)
        if not do0:
            adaln_partial(wchunk1_big)
        rep1 = carve("rep1", [128, 128], BF16)
        adaln(1, [(0, gate_lat)], rep1)
        tiles1 = [(cx[0], 1), (cx[1], 1)] + [(xs[n], 0) for n in range(NT)]
        for t0 in range(0, len(tiles1), 4):
            grp = tiles1[t0:t0 + 4]
            nt = len(grp)
            hT = rot(hT1)
            for ti, (xt, sidx) in enumerate(grp):
                norm_hT(xt, sidx, hT, xn1, c0=ti * 128)
            j0 = t0 * 16
            for ft in range(8):
                acc = rot(banks[0:4])
                for kt in range(8):
                    P.op("pe", lambda e, kt=kt, ft=ft, acc=acc, hT=hT, nt=nt: e.matmul(
                        acc[:, 0:nt * 128], lhsT=win1u[:, kt, ft * 128:(ft + 1) * 128], rhs=hT[:, kt, 0:nt * 128],
                        start=(kt == 0), stop=(kt == 7)), reads=[win1u, hT], writes=[acc], flag=(kt == 7))
                if ft % 2 == 0:
                    P.op("act", lambda e, ft=ft, acc=acc, j0=j0, nt=nt: e.copy(
                        out=uTs[:, ft, :, j0:j0 + 16 * nt], in_=acc[:, 0:nt * 128].rearrange("p (j s) -> p s j", s=8)),
                        reads=[acc], writes=[uTs])
                else:
                    P.op("dve", lambda e, ft=ft, acc=acc, j0=j0, nt=nt: e.tensor_copy(
                        out=uTs[:, ft, :, j0:j0 + 16 * nt], in_=acc[:, 0:nt * 128].rearrange("p (j s) -> p s j", s=8)),
                        reads=[acc], writes=[uTs])
        for g in range(64):
            ft, gl = g // 8, g % 8
            P.dma("sp", lambda e, g=g, ft=ft, gl=gl: e.dma_start(
                out=Ud.t.ap()[:, g, :].rearrange("(s h) j -> h s j", h=16),
                in_=uTs[gl * 16:(gl + 1) * 16, ft, :, :]), reads=[uTs], writes=[Ud], semtile=uTs)

        o = 0
        NCG = 128
        def sm(name, k=1):
            return carve(name, [128, k, NCG] if k > 1 else [128, NCG], F32)
        COLA = sm("COLA", 9); COLB = sm("COLB", 9)
        BPR = carve("BPR", [128, 2, 64, 8], F32); BPI = carve("BPI", [128, 2, 64, 8], F32)
        CPR = carve("CPR", [128, 2, 64, 8], F32); CPI = carve("CPI", [128, 2, 64, 8], F32)
        bbA = carve("bbA", [128, NCG, 16], BF16); bbS = carve("bbS", [128, NCG, 16], BF16)
        ccA = carve("ccA", [128, NCG, 16], BF16); ccS = carve("ccS", [128, NCG, 16], BF16)
        dTt = carve("dTt", [128, 64], F32)
        mFB = carve("mFB", [128, 2, 128], F32)
        identF = carve("identF", [128, 128], F32)
        idst = carve("idst", [128, 64], F32)
        sgn = carve("sgn", [128, 8], F32)
        Eo = carve("Eo", [128, 64], F32)
        Ei = carve("Ei", [128, 64], F32)
        o_tab = o
        LR = sm("LR"); LI = sm("LI"); DT = sm("DT")
        PR = sm("PR", 9); PI = sm("PI", 9); NR = sm("NR", 8); NI = sm("NI", 8)
        QR = sm("QR", 9); QI = sm("QI", 9)
        tmp = [sm(f"tmp{i}") for i in range(8)]
        bAr = carve("bAr", [128, NCG, 16], F32); bSr = carve("bSr", [128, NCG, 16], F32)
        big1 = carve("big1", [128, NCG, 16], F32); big2 = carve("big2", [128, NCG, 16], F32)
        o = o_tab
        NB = 8
        Bs_ = [carve(f"Bs{i}", [128, 8, 16], BF16) for i in range(NB)]
        Ct_ = [carve(f"Ct{i}", [128, 8, 16], BF16) for i in range(NB)]
        BsT_ = [carve(f"BsT{i}", [128, 128], BF16) for i in range(NB)]
        Mg_ = [carve(f"Mg{i}", [128, 128], BF16) for i in range(NB)]
        g1_ = [carve(f"g1{i}", [128, 8, 16], F32) for i in range(NB)]
        g2_ = [carve(f"g2{i}", [128, 8, 16], F32) for i in range(NB)]
        Zs_ = [carve(f"Zs{i}", [128, 290], BF16) for i in range(NB)]
        S_ = [carve(f"S{i}", [128, 290], BF16) for i in range(NB)]
        Pp_ = [carve(f"Pp{i}", [128, 290], BF16) for i in range(NB)]
        R_ = [carve(f"R{i}", [128, 9, 128], BF16) for i in range(NB)]
        Yb_ = [carve(f"Yb{i}", [128, 256], BF16) for i in range(NB)]
        Yp_ = [carve(f"Yp{i}", [128, 256], BF16) for i in range(NB)]
        Ub_ = [carve(f"Ub{i}", [128, 8, 288], BF16) for i in range(2)]
        print("layer1 phase2 arena words:", o)
        o_p2 = o

        for t_, d_ in ((LR, lamR_d), (LI, lamI_d), (dTt, dT_d), (idst, idst_d), (sgn, sgn_d)):
            P.dma("pool", lambda e, t_=t_, d_=d_: e.dma_start(out=t_[:], in_=d_), writes=[t_])
        P.dma("pool", lambda e: e.dma_start(out=DT[:], in_=logdt_d.partition_broadcast(128)), writes=[DT])
        P.dma("pool", lambda e: e.dma_start(out=bAr[:].rearrange("p a b -> p (a b)"), in_=bA_d), writes=[bAr])
        P.dma("pool", lambda e: e.dma_start(out=bSr[:].rearrange("p a b -> p (a b)"), in_=bS_d), writes=[bSr])
        P.dma("pool", lambda e: e.dma_start(out=mFB[:].rearrange("p a b -> p (a b)"), in_=mFB_d), writes=[mFB])
        P.dma("pool", lambda e: e.dma_start(out=identF[:], in_=ident_d), writes=[identF])
        P.dma("pool", lambda e: e.dma_start(out=ccA[:].rearrange("p a b -> p (a b)"), in_=cA_d), writes=[ccA])
        P.dma("pool", lambda e: e.dma_start(out=ccS[:].rearrange("p a b -> p (a b)"), in_=cS_d), writes=[ccS])

        MUL, ADD, SUB = ALU.mult, ALU.add, ALU.subtract
        T0, T1, T2, T3, T4, T5, T6, T7 = tmp

        def cmul(outr, outi, ar_, ai_, br_, bi_):
            TT("dve", F(T6), ar_, br_, MUL)
            TT("pool", F(T7), ai_, bi_, MUL)
            TT("dve", outr, F(T6), F(T7), SUB)
            TT("dve", F(T6), ar_, bi_, MUL)
            TT("pool", F(T7), ai_, br_, MUL)
            TT("dve", outi, F(T6), F(T7), ADD)

        ACTF(F(DT), F(DT), AF.Exp)
        TT("dve", F(T0), F(LR), F(DT), MUL)
        TT("dve", F(T1), F(LI), F(DT), MUL)
        ACTF(F(T2), F(T0), AF.Exp, scale=0.125)
        ACTF(F(T3), F(T1), AF.Sin, scale=0.125)
        ACTF(F(T4), F(T1), AF.Sin, scale=0.0625)
        TT("dve", F(T4), F(T4), F(T4), MUL)
        TS("dve", F(T4), F(T4), -2.0, MUL, 1.0, ADD)
        zr, zi = (PR, PR[:, 1, :]), (PI, PI[:, 1, :])
        TT("dve", F(T0), F(T2), F(T4), MUL)
        TT("dve", F(T1), F(T2), F(T3), MUL)
        cur = (F(T0), F(T1))
        for it in range(3):
            dst = (zr, zi) if it == 2 else (F(T2), F(T3)) if it == 0 else (F(T4), F(T5))
            cmul(dst[0], dst[1], cur[0], cur[1], cur[0], cur[1])
            cur = dst
        P.op("dve", lambda e: e.memset(PR[:, 0, :], 1.0), writes=[PR])
        P.op("dve", lambda e: e.memset(PI[:, 0, :], 0.0), writes=[PI])
        P.op("dve", lambda e: e.memset(NR[:, 0, :], 1.0), writes=[NR])
        P.op("dve", lambda e: e.memset(NI[:, 0, :], 0.0), writes=[NI])
        for k in range(2, 9):
            cmul((PR, PR[:, k, :]), (PI, PI[:, k, :]), (PR, PR[:, k - 1, :]), (PI, PI[:, k - 1, :]), zr, zi)
        TT("dve", F(T0), zr, zr, MUL)
        TT("dve", F(T1), zi, zi, MUL)
        TT("dve", F(T0), F(T0), F(T1), ADD)
        P.op("dve", lambda e: e.reciprocal(out=T0[:], in_=T0[:]), reads=[T0], writes=[T0])
        TT("dve", (NR, NR[:, 1, :]), zr, F(T0), MUL)
        TT("dve", F(T1), zi, F(T0), MUL)
        TS("dve", (NI, NI[:, 1, :]), F(T1), -1.0, MUL)
        for k in range(2, 8):
            cmul((NR, NR[:, k, :]), (NI, NI[:, k, :]), (NR, NR[:, k - 1, :]), (NI, NI[:, k - 1, :]),
                 (NR, NR[:, 1, :]), (NI, NI[:, 1, :]))
        P.op("dve", lambda e: e.tensor_copy(out=QR[:, 0, :], in_=PR[:, 8, :]), reads=[PR], writes=[QR])
        P.op("dve", lambda e: e.tensor_copy(out=QI[:, 0, :], in_=PI[:, 8, :]), reads=[PI], writes=[QI])
        for l in range(1, 9):
            cmul((QR, QR[:, l, :]), (QI, QI[:, l, :]), (QR, QR[:, l - 1, :]), (QI, QI[:, l - 1, :]),
                 (QR, QR[:, l - 1, :]), (QI, QI[:, l - 1, :]))
        for l in range(9):
            TS("dve", F(T0), (QR, QR[:, l, :]), sgn[:, 2:3], MUL, extra=[sgn])
            STT("dve", (COLA, COLA[:, l, :]), (QI, QI[:, l, :]), sgn[:, 4:5], F(T0), MUL, ADD, extra=[sgn])
            TS("dve", F(T1), (QI, QI[:, l, :]), sgn[:, 2:3], MUL, extra=[sgn])
            STT("dve", (COLB, COLB[:, l, :]), (QR, QR[:, l, :]), sgn[:, 3:4], F(T1), MUL, ADD, extra=[sgn])
        TS("dve", F(T0), zr, -1.0, ADD)
        TT("dve", F(T1), F(LR), F(LR), MUL)
        TT("dve", F(T2), F(LI), F(LI), MUL)
        TT("dve", F(T1), F(T1), F(T2), ADD)
        P.op("dve", lambda e: e.reciprocal(out=T1[:], in_=T1[:]), reads=[T1], writes=[T1])
        TT("dve", F(T2), F(T0), F(LR), MUL)
        TT("dve", F(T3), zi, F(LI), MUL)
        TT("dve", F(T2), F(T2), F(T3), ADD)
        TT("dve", F(T2), F(T2), F(T1), MUL)
        TT("dve", F(T3), zi, F(LR), MUL)
        TT("dve", F(T4), F(T0), F(LI), MUL)
        TT("dve", F(T3), F(T3), F(T4), SUB)
        TT("dve", F(T3), F(T3), F(T1), MUL)
        TS("dve", F(T4), F(T3), sgn[:, 0:1], MUL, extra=[sgn])
        TS("dve", F(T5), F(T3), sgn[:, 1:2], MUL, extra=[sgn])
        def bc16(t):
            return (t, t[:].unsqueeze(2).broadcast_to([128, NCG, 16]))
        TT("dve", F(big1), F(bAr), bc16(T2), MUL)
        TT("pool", F(big2), F(bSr), bc16(T4), MUL)
        TT("dve", F(bbA), F(big1), F(big2), ADD)
        TT("dve", F(big1), F(bSr), bc16(T2), MUL)
        TT("pool", F(big2), F(bAr), bc16(T5), MUL)
        TT("dve", F(bbS), F(big1), F(big2), ADD)
        for c in range(2):
            cs = slice(c * 64, (c + 1) * 64)
            for k in range(8):
                ks = 7 - k if c == 0 else k
                P.op("dve", lambda e, cs=cs, c=c, k=k, ks=ks: e.tensor_copy(
                    out=BPR[:, c, :, k], in_=PR[:, ks, cs]), reads=[PR], writes=[BPR])
                P.op("dve", lambda e, cs=cs, c=c, k=k, ks=ks: e.tensor_scalar(
                    out=BPI[:, c, :, k], in0=PI[:, ks, cs], scalar1=sgn[:, 0:1], scalar2=None, op0=MUL),
                    reads=[PI, sgn], writes=[BPI])
                P.op("dve", lambda e, cs=cs, c=c, k=k, ks=ks: e.tensor_scalar(
                    out=CPR[:, c, :, k], in0=NR[:, ks, cs], scalar1=sgn[:, 1:2], scalar2=None, op0=MUL),
                    reads=[NR, sgn], writes=[CPR])
                P.op("dve", lambda e, cs=cs, c=c, k=k, ks=ks: e.tensor_scalar(
                    out=CPI[:, c, :, k], in0=NI[:, ks, cs], scalar1=-1.0, scalar2=None, op0=MUL),
                    reads=[NI], writes=[CPI])

        def prep(c, g, Ub, gi, k):
            cg = c * 64 + g
            Bs, Ct, BsT, Mg, g1, g2, Zs, R = Bs_[k], Ct_[k], BsT_[k], Mg_[k], g1_[k], g2_[k], Zs_[k], R_[k]
            e1, e2 = ("dve", "pool") if k % 2 == 0 else ("pool", "dve")
            P.op(e1, lambda e: e.tensor_tensor(out=g1[:], in0=BPR[:, c, g, :].unsqueeze(2).broadcast_to([128, 8, 16]),
                                               in1=bbA[:, cg, :].unsqueeze(1).broadcast_to([128, 8, 16]), op=MUL),
                 reads=[BPR, bbA], writes=[g1])
            P.op(e2, lambda e: e.tensor_tensor(out=g2[:], in0=BPI[:, c, g, :].unsqueeze(2).broadcast_to([128, 8, 16]),
                                               in1=bbS[:, cg, :].unsqueeze(1).broadcast_to([128, 8, 16]), op=MUL),
                 reads=[BPI, bbS], writes=[g2])
            P.op(e1, lambda e: e.tensor_tensor(out=Bs[:], in0=g1[:], in1=g2[:], op=ADD), reads=[g1, g2], writes=[Bs])
            P.op(e2, lambda e: e.tensor_tensor(out=g1[:], in0=CPR[:, c, g, :].unsqueeze(2).broadcast_to([128, 8, 16]),
                                               in1=ccA[:, cg, :].unsqueeze(1).broadcast_to([128, 8, 16]), op=MUL),
                 reads=[CPR, ccA], writes=[g1])
            P.op(e1, lambda e: e.tensor_tensor(out=g2[:], in0=CPI[:, c, g, :].unsqueeze(2).broadcast_to([128, 8, 16]),
                                               in1=ccS[:, cg, :].unsqueeze(1).broadcast_to([128, 8, 16]), op=MUL),
                 reads=[CPI, ccS], writes=[g2])
            P.op(e2, lambda e: e.tensor_tensor(out=Ct[:], in0=g1[:], in1=g2[:], op=ADD), reads=[g1, g2], writes=[Ct])
            P.op("dve", lambda e: e.tensor_tensor(
                out=R[:, :, 0:64], in0=idst[:].unsqueeze(1).broadcast_to([128, 9, 64]),
                in1=COLA[:, :, cg:cg + 1].broadcast_to([128, 9, 64]), op=MUL), reads=[idst, COLA], writes=[R])
            P.op("pool", lambda e: e.tensor_tensor(
                out=R[:, :, 64:128], in0=idst[:].unsqueeze(1).broadcast_to([128, 9, 64]),
                in1=COLB[:, :, cg:cg + 1].broadcast_to([128, 9, 64]), op=MUL), reads=[idst, COLB], writes=[R])
            Bs2 = Bs[:].rearrange("p a b -> p (a b)")
            Ct2 = Ct[:].rearrange("p a b -> p (a b)")
            tb = rot(banks[6:8])
            tbb = tb[:, :].bitcast(BF16)
            P.op("pe", lambda e: e.transpose(out=tbb[:, 0:128], in_=Bs2, identity=ident[:]), reads=[Bs, ident], writes=[tb])
            P.op("act", lambda e: e.copy(out=BsT[:], in_=tbb[:, 0:128]), reads=[tb], writes=[BsT])
            mb = banks[5]
            P.op("pe", lambda e: e.matmul(mb[:, 0:128], lhsT=Bs2, rhs=Ct2, start=True, stop=True), reads=[Bs, Ct], writes=[mb])
            if c == 0:
                P.op("dve", lambda e: e.tensor_tensor(out=Mg[:], in0=mb[:, 0:128], in1=mFB[:, 0, :], op=MUL),
                     reads=[mb, mFB], writes=[Mg])
            else:
                P.op("dve", lambda e: e.tensor_tensor(out=g1[:].rearrange("p a b -> p (a b)"), in0=mb[:, 0:128],
                                                      in1=mFB[:, 1, :], op=MUL), reads=[mb, mFB], writes=[g1])
                P.op("dve", lambda e: e.scalar_tensor_tensor(out=Mg[:], in0=identF[:], scalar=dTt[:, g:g + 1],
                                                             in1=g1[:].rearrange("p a b -> p (a b)"), op0=MUL, op1=ADD),
                     reads=[identF, dTt, g1], writes=[Mg])
            zb = rot(banks[0:2])
            ncol = 288 if c == 0 else 256
            ucols = slice(0, 288) if c == 0 else slice(32, 288)
            P.op("pe", lambda e: e.matmul(zb[:, 0:ncol], lhsT=BsT[:], rhs=Ub[:, gi, ucols], start=True, stop=True),
                 reads=[BsT, Ub], writes=[zb])
            P.op("act", lambda e: e.copy(out=Zs[:, 0:ncol], in_=zb[:, 0:ncol]), reads=[zb], writes=[Zs])
            if c == 1:
                P.op("act", lambda e: e.copy(out=Zs[:, 256:257], in_=Ei[:, g:g + 1]), reads=[Ei], writes=[Zs])
                P.dma("sp", lambda e: e.dma_start(out=Yp_[k][:], in_=Ydg[g][:]), reads=[Ydg[g]], writes=[Yp_[k]])

        def scan(c, ks):
            n = 288 if c == 0 else 257
            for l in range(9):
                sh = 1 << l
                if sh >= n:
                    break
                use_act = (l % 2 == 1)
                sbs = {}
                for k in ks:
                    Zs, S, R = Zs_[k], S_[k], R_[k]
                    src = Zs if l == 0 else S
                    sb = rot(banks[1:5])
                    sbs[k] = sb
                    rng = slice(0, n - sh) if c == 0 else slice(sh, n)
                    orng = slice(sh, n) if c == 0 else slice(0, n - sh)
                    if use_act:
                        P.op("pe", lambda e, src=src, sb=sb: e.matmul(
                            sb[:, 0:n], lhsT=ident[:], rhs=src[:, 0:n], start=True, stop=False),
                            reads=[ident, src], writes=[sb], flag=False)
                        P.op("pe", lambda e, l=l, src=src, sb=sb, R=R, rng=rng, orng=orng: e.matmul(
                            sb[:, orng], lhsT=R[:, l, :], rhs=src[:, rng], start=False, stop=True),
                            reads=[R, src], writes=[sb])
                    else:
                        P.op("pe", lambda e, l=l, sh=sh, src=src, sb=sb, R=R, rng=rng: e.matmul(
                            sb[:, 0:n - sh], lhsT=R[:, l, :], rhs=src[:, rng], start=True, stop=True),
                            reads=[R, src], writes=[sb])
                for k in ks:
                    Zs, S = Zs_[k], S_[k]
                    src = Zs if l == 0 else S
                    sb = sbs[k]
                    orng = slice(sh, n) if c == 0 else slice(0, n - sh)
                    if use_act:
                        P.op("act", lambda e, sb=sb, S=S: e.copy(out=S[:, 0:n], in_=sb[:, 0:n]), reads=[sb], writes=[S])
                    else:
                        P.op("dve", lambda e, sh=sh, src=src, sb=sb, S=S, orng=orng: e.tensor_tensor(
                            out=S[:, orng], in0=sb[:, 0:n - sh], in1=src[:, orng], op=ADD), reads=[sb, src], writes=[S])
                        if l == 0:
                            edge = slice(0, 1) if c == 0 else slice(n - 1, n)
                            P.op("act", lambda e, S=S, Zs=Zs, edge=edge: e.copy(out=S[:, edge], in_=Zs[:, edge]),
                                 reads=[Zs], writes=[S])

        def finish(c, g, Ub, gi, k):
            Ct, Mg, Zs, S, Pp, Yb = Ct_[k], Mg_[k], Zs_[k], S_[k], Pp_[k], Yb_[k]
            n = 288 if c == 0 else 257
            Ct2 = Ct[:].rearrange("p a b -> p (a b)")
            P.op("pool", lambda e: e.tensor_tensor(out=Pp[:, 0:n], in0=S[:, 0:n], in1=Zs[:, 0:n], op=SUB),
                 reads=[S, Zs], writes=[Pp])
            if c == 0:
                P.op("act", lambda e: e.copy(out=Eo[:, g:g + 1], in_=S[:, 287:288]), reads=[S], writes=[Eo])
            yb = rot(banks[5:6] + banks[0:1])
            pc = slice(32, 288) if c == 0 else slice(0, 256)
            P.op("pe", lambda e: e.matmul(yb[:, 0:256], lhsT=Mg[:], rhs=Ub[:, gi, 32:288], start=True, stop=False),
                 reads=[Mg, Ub], writes=[yb], flag=False)
            P.op("pe", lambda e: e.matmul(yb[:, 0:256], lhsT=Ct2, rhs=Pp[:, pc], start=False, stop=True),
                 reads=[Ct, Pp], writes=[yb])
            if c == 0:
                P.op("act", lambda e: e.copy(out=Yb[:], in_=yb[:, 0:256]), reads=[yb], writes=[Yb])
            else:
                P.op("dve", lambda e: e.tensor_tensor(out=Yb[:], in0=yb[:, 0:256], in1=Yp_[k][:], op=ADD),
                     reads=[yb, Yp_[k]], writes=[Yb])
            P.dma("sp", lambda e: e.dma_start(out=Ydg[g][:], in_=Yb[:]), reads=[Yb], writes=[Ydg[g]], semtile=Yb)

        UBS = 4
        for c in range(2):
            if c == 1:
                P.dma("sp", lambda e: e.dma_start(out=Ein.t.ap(), in_=Eo[:]), reads=[Eo], writes=[Ein], semtile=Eo)
                P.dma("pool", lambda e: e.collective_compute("AllGather", ALU.bypass, [[0, 1], [2, 3], [4, 5], [6, 7]],
                                                             ins=[Ein.t.ap()], outs=[Eall.t.ap()]),
                      reads=[Ein], writes=[Eall], semtile=Eall, inc=1)
                Et = g1_[0]
                P.dma("sp", lambda e: e.dma_start(out=Et[:].rearrange("p a b -> p (a b)").rearrange("p (r f) -> p r f", r=2),
                                                  in_=Eall.t.ap().rearrange("(r p) f -> p r f", p=128)),
                      reads=[Eall], writes=[Et])
                Et2 = Et[:].rearrange("p a b -> p (a b)")
                P.op("dve", lambda e: e.tensor_scalar(out=Ei[:], in0=Et2[:, 0:64], scalar1=sgn[:, 6:7], scalar2=None, op0=MUL),
                     reads=[Et, sgn], writes=[Ei])
                P.op("dve", lambda e: e.scalar_tensor_tensor(out=Ei[:], in0=Et2[:, 64:128], scalar=sgn[:, 7:8], in1=Ei[:],
                                                             op0=MUL, op1=ADD), reads=[Et, sgn, Ei], writes=[Ei])
            batches = []
            for g0 in range(0, 64, 8):
                for h2 in range(2):
                    batches.append((g0, h2))
            Ubs = {}
            def get_ub(g0):
                if g0 not in Ubs:
                    Ub = Ub_[(g0 // 8) % 2]
                    P.dma("sp", lambda e, g0=g0, Ub=Ub: e.dma_start(out=Ub[:], in_=Ud.t.ap()[:, g0:g0 + 8, :]),
                          reads=[Ud], writes=[Ub])
                    Ubs[g0] = Ub
                return Ubs[g0]
            def do_prep(bi):
                g0, h2 = batches[bi]
                Ub = get_ub(g0)
                for u in range(UBS):
                    gi = h2 * UBS + u
                    prep(c, g0 + gi, Ub, gi, (bi % 2) * UBS + u)
            do_prep(0)
            for bi in range(len(batches)):
                if bi + 1 < len(batches):
                    do_prep(bi + 1)
                g0, h2 = batches[bi]
                ks = [(bi % 2) * UBS + u for u in range(UBS)]
                scan(c, ks)
                for u in range(UBS):
                    gi = h2 * UBS + u
                    finish(c, g0 + gi, Ubs[g0], gi, (bi % 2) * UBS + u)

        o = 0
        wglu = carve("wglu", [128, 8, 2 * D], BF16)
        win1z = carve("win1z", [128, 8, D], BF16)
        wout1 = carve("wout1", [128, 8, D], BF16)
        yTs = [carve(f"yTs{ft}", [128, 8, 128], BF16) for ft in range(8)]
        fnw = carve("fnw", [128, D], F32)
        hT3 = [carve(f"hT3_{i}", [128, 8, 128], BF16) for i in range(2)]
        xn3 = carve("xn3", [128, D], BF16)
        gyT = [carve(f"gyT{i}", [128, 8, 128], BF16) for i in range(2)]
        sg = carve("sg", [128, D], BF16)
        sz = carve("sz", [128, D], BF16)
        tm = carve("tm", [128, D], BF16)
        mm = carve("mm", [128, D], BF16)
        mT = carve("mT", [128, 8, 128], BF16)
        ytmp3 = carve("ytmp3", [128, D], F32)
        outv = [ytmp3]
        print("layer1 phase3 arena words:", o)
        for kt in range(8):
            load_cast(win1z, win1z[:, kt, :], win1_d[kt * 128:(kt + 1) * 128, D:2 * D], 1024)
        for kt in range(8):
            load_cast(wglu, wglu[:, kt, 0:1024], wglu_d[kt * 128:(kt + 1) * 128, 0:1024], 1024)
            load_cast(wglu, wglu[:, kt, 1024:2048], wglu_d[kt * 128:(kt + 1) * 128, 1024:2048], 1024)
        for kt in range(8):
            load_cast(wout1, wout1[:, kt, :], wout1_d[kt * 128:(kt + 1) * 128, :], 1024)
        P.dma("sp", lambda e: e.dma_start(out=fnw[:], in_=fnw_d.partition_broadcast(128)), writes=[fnw])
        gys, hTs3 = {}, {}

        def pre3(n):
            if n % 8 == 0:
                hh = n // 8
                for g in range(64):
                    ft, gl = g // 8, g % 8
                    P.dma("pool", lambda e, g=g, ft=ft, gl=gl, hh=hh: e.dma_start(
                        out=yTs[ft][gl * 16:(gl + 1) * 16, :, :],
                        in_=Ydg[g][:, hh * 128:(hh + 1) * 128].rearrange("(t h) j -> h t j", h=16)),
                        reads=[Ydg[g]], writes=[yTs[ft]])
            gy = gyT[n % 2]
            for ft in range(8):
                P.op("act", lambda e, ft=ft, gy=gy, n=n: e.activation(
                    out=gy[:, ft, :].rearrange("p (j t) -> p j t", t=8),
                    in_=yTs[ft][:, :, (n % 8) * 16:(n % 8 + 1) * 16].rearrange("p t j -> p j t"), func=AF.Gelu),
                    reads=[yTs[ft]], writes=[gy])
            hT = hT3[n % 2]
            norm_hT(xs[n], 0, hT, xn3)
            gys[n], hTs3[n] = gy, hT

        pre3(0)
        for n in range(NT):
            xt = xs[n]
            gy, hT = gys[n], hTs3[n]
            for half in range(2):
                zb = banks[half]
                for kt in range(8):
                    P.op("pe", lambda e, kt=kt, half=half, zb=zb, hT=hT: e.matmul(
                        zb[:, :], lhsT=hT[:, kt, :], rhs=win1z[:, kt, half * 512:(half + 1) * 512],
                        start=(kt == 0), stop=(kt == 7)), reads=[win1z, hT], writes=[zb], flag=(kt == 7))
                P.op("act", lambda e, half=half, zb=zb: e.activation(out=sz[:, half * 512:(half + 1) * 512], in_=zb[:, :],
                                                                     func=AF.Silu), reads=[zb], writes=[sz])
            for q4 in range(4):
                gb = banks[2 + q4]
                for kt in range(8):
                    P.op("pe", lambda e, kt=kt, q4=q4, gb=gb, gy=gy: e.matmul(
                        gb[:, :], lhsT=gy[:, kt, :], rhs=wglu[:, kt, q4 * 512:(q4 + 1) * 512],
                        start=(kt == 0), stop=(kt == 7)), reads=[wglu, gy], writes=[gb], flag=(kt == 7))
            if n + 1 < NT and n % 8 != 7:
                pre3(n + 1)
            for half in range(2):
                P.op("act", lambda e, half=half: e.activation(out=sg[:, half * 512:(half + 1) * 512], in_=banks[4 + half][:, :],
                                                              func=AF.Sigmoid), reads=[banks[4 + half]], writes=[sg])
                P.op("dve", lambda e, half=half: e.tensor_tensor(out=tm[:, half * 512:(half + 1) * 512], in0=banks[2 + half][:, :],
                                                                 in1=sg[:, half * 512:(half + 1) * 512], op=ALU.mult),
                     reads=[banks[2 + half], sg], writes=[tm])
            P.op("dve", lambda e: e.tensor_tensor(out=mm[:], in0=tm[:], in1=sz[:], op=ALU.mult), reads=[tm, sz], writes=[mm])
            tb = banks[6]
            tbb = tb[:, :].bitcast(BF16)
            for kt in range(8):
                P.op("pe", lambda e, kt=kt, tbb=tbb: e.transpose(out=tbb[:, kt * 128:(kt + 1) * 128],
                                                                 in_=mm[:, kt * 128:(kt + 1) * 128], identity=ident[:]),
                     reads=[mm, ident], writes=[tb], flag=(kt == 7))
            P.op("act", lambda e, tbb=tbb: e.copy(out=mT[:].rearrange("p a b -> p (a b)"), in_=tbb), reads=[tb], writes=[mT])
            for half in range(2):
                yb = banks[half]
                for kt in range(8):
                    P.op("pe", lambda e, kt=kt, half=half, yb=yb: e.matmul(
                        yb[:, :], lhsT=mT[:, kt, :], rhs=wout1[:, kt, half * 512:(half + 1) * 512],
                        start=(kt == 0), stop=(kt == 7)), reads=[mT, wout1], writes=[yb], flag=(kt == 7))
                P.op("dve", lambda e, half=half, yb=yb: e.tensor_tensor(
                    out=ytmp3[:, half * 512:(half + 1) * 512], in0=yb[:, :], in1=gate_lat[:, half * 512:(half + 1) * 512],
                    op=ALU.mult), reads=[yb, gate_lat], writes=[ytmp3])
            P.op("dve", lambda e, xt=xt: e.tensor_tensor(out=xt[:], in0=xt[:], in1=ytmp3[:], op=ALU.add),
                 reads=[xt, ytmp3], writes=[xt])
            sst = rot(ss)
            ov = rot(outv)
            P.op("act", lambda e, xt=xt, sst=sst: e.activation(out=sg[:], in_=xt[:], func=AF.Square, accum_out=sst[:, 0:1]),
                 reads=[xt], writes=[sg, sst])
            P.op("act", lambda e, sst=sst: e.activation(out=sst[:, 1:2], in_=sst[:, 0:1], func=AF.Sqrt, scale=1.0 / D,
                                                        bias=small[:, 0:1]), reads=[sst, small], writes=[sst])
            P.op("dve", lambda e, sst=sst: e.reciprocal(out=sst[:, 2:3], in_=sst[:, 1:2]), reads=[sst], writes=[sst])
            P.op("dve", lambda e, xt=xt, sst=sst, ov=ov: e.scalar_tensor_tensor(
                out=ov[:], in0=xt[:], scalar=sst[:, 2:3], in1=fnw[:], op0=ALU.mult, op1=ALU.mult),
                reads=[xt, sst, fnw], writes=[ov])
            P.out_evs.append(P.dma("sp", lambda e, n=n, ov=ov: e.dma_start(out=out_d[n * 128:(n + 1) * 128, :], in_=ov[:]),
                                   reads=[ov]))
            if n + 1 < NT and n % 8 == 7:
                pre3(n + 1)

    if mode == "l0":
        for n in range(NT):
            P.out_evs.append(P.dma("sp", lambda e, n=n: e.dma_start(out=x1_d[n * 128:(n + 1) * 128, :], in_=xs[n][:]),
                                   reads=[xs[n]]))
        for n in range(2):
            P.out_evs.append(P.dma("sp", lambda e, n=n: e.dma_start(out=ctx1_d[n * 128:(n + 1) * 128, :], in_=cx[n][:]),
                                   reads=[cx[n]]))

    P.wait_all("sp", P.out_evs)
    P.emit()
    return nc


def _in_maps(inputs, mode):
    x = np.asarray(inputs["x"], np.float32)
    c = np.asarray(inputs["c"], np.float32)
    ctx = np.asarray(inputs["ctx"], np.float32)
    c_ctx = np.asarray(inputs["c_ctx"], np.float32)
    nwT = np.concatenate([_colT(inputs["norm_w"][l]) for l in range(2)], 1)
    badaT = np.concatenate([_colT(inputs["b_ada"][l]) for l in range(2)], 1)
    maps = []
    for core in range(NCORES):
        b, hf = core // 2, core % 2
        cs = _consts(hf)
        xh = np.zeros((256, D), np.float32)
        if hf == 0:
            xl = x[b, 0:TOK]
            xh[128:256] = x[b, TOK:TOK + 128]
            cl = ctx[b]
        else:
            xl = x[b, TOK:2 * TOK][::-1]
            xh[128:256] = x[b, TOK - 128:TOK][::-1]
            cl = ctx[b][::-1]
        r0 = core * 128
        cT5 = np.stack([c[0, r0:r0 + 128], c[1, r0:r0 + 128], c[2, r0:r0 + 128], c[3, r0:r0 + 128],
                        c_ctx[r0:r0 + 128]], 1).astype(np.float32)
        sel5 = np.zeros((128, 5), np.float32)
        sel5[:, b] = 1.0
        wadaS = np.ascontiguousarray(np.asarray(inputs["w_ada"], np.float32)[:, r0:r0 + 128, :])
        m = dict(x=np.ascontiguousarray(xl), ctx=np.ascontiguousarray(cl),
                 cT5=np.ascontiguousarray(cT5), sel5=sel5, wadaS=wadaS, nwT=nwT, badaT=badaT,
                 ident=cs["ident"])
        if mode in ("full", "l0"):
            m.update(xh=xh, attn_w_in=np.asarray(inputs["attn_w_in"][0], np.float32),
                     attn_w_out=np.asarray(inputs["attn_w_out"][0], np.float32),
                     attn_sink=np.asarray(inputs["attn_sink"], np.float32).reshape(1, 16),
                     cosT=cs["cosT"], sinT=cs["sinT"], masks=cs["masks"], psw=cs["psw"])
        if mode in ("full", "l1"):
            dirs = (hf, 1 - hf)
            def dup(a):
                t = np.concatenate([np.asarray(a[0][d_], np.float32).T for d_ in dirs], 1)
                return np.ascontiguousarray(np.concatenate([t, t], 0))
            bre, bim = inputs["ssm_b_re"][0], inputs["ssm_b_im"][0]
            cre, cim = inputs["ssm_c_re"][0], inputs["ssm_c_im"][0]
            def bl(a):
                return np.concatenate([np.asarray(a[d_], np.float32).transpose(1, 0, 2).reshape(64, 64 * 16) for d_ in dirs], 1)
            def cl_(a):
                return np.concatenate([np.asarray(a[d_], np.float32).transpose(2, 0, 1).reshape(64, 64 * 16) for d_ in dirs], 1)
            dvec = np.asarray(inputs["ssm_d"][0], np.float32).reshape(64, 16)
            dT = np.ascontiguousarray(np.tile(dvec.T, (8, 1)))
            sidx = np.arange(128) // 16
            mF = (sidx[None, :] >= sidx[:, None]).astype(np.float32)
            mB = (sidx[:, None] >= sidx[None, :]).astype(np.float32)
            sg = np.zeros((128, 8), np.float32)
            top = np.arange(128) < 64
            sg[:, 0] = np.where(top, -1.0, 1.0); sg[:, 1] = np.where(top, 1.0, -1.0)
            sg[:, 2] = top; sg[:, 3] = ~top; sg[:, 4] = np.where(top, 0.0, -1.0); sg[:, 5] = -1.0
            sg[:, 6] = 1.0 if hf == 1 else 0.0
            sg[:, 7] = 1.0 if hf == 0 else 0.0
            m.update(ssm_w_in=np.asarray(inputs["ssm_w_in"][0], np.float32),
                     ssm_w_glu=np.asarray(inputs["ssm_w_glu"][0], np.float32),
                     ssm_w_out=np.asarray(inputs["ssm_w_out"][0], np.float32),
                     fnw=np.asarray(inputs["final_norm_w"], np.float32).reshape(1, D),
                     lamR=dup(inputs["ssm_lam_re"]), lamI=dup(inputs["ssm_lam_im"]),
                     logdt=np.ascontiguousarray(np.concatenate([np.asarray(inputs["ssm_log_dt"][0][d_], np.float32) for d_ in dirs]).reshape(1, 128)),
                     bA=np.ascontiguousarray(np.concatenate([bl(bre), bl(bim)], 0)),
                     bS=np.ascontiguousarray(np.concatenate([bl(bim), bl(bre)], 0)),
                     cA=np.ascontiguousarray(np.concatenate([cl_(cre), cl_(cim)], 0)),
                     cS=np.ascontiguousarray(np.concatenate([cl_(cim), cl_(cre)], 0)),
                     dT=dT, mFB=np.ascontiguousarray(np.concatenate([mF, mB], 1)),
                     idst=np.ascontiguousarray(np.concatenate([np.eye(64, dtype=np.float32)] * 2, 0)), sgn=sg)
        maps.append(m)
    return maps


def gather(results, key, n=TOK):
    out = np.zeros((4, 2 * n, D), np.float32)
    for core in range(NCORES):
        b, hf = core // 2, core % 2
        r = results[core][key]
        out[b, hf * n:(hf + 1) * n] = r if hf == 0 else r[::-1]
    return out


def run_mode(inputs, mode):
    nc = build(mode)
    res = run_bass_kernel_spmd(nc, _in_maps(inputs, mode), core_ids=list(range(NCORES)))
    return res.results


def kernel(**inputs):
    r = run_mode(inputs, "full")
    return gather(r, "out")
```
